# Optimizing a Trainium2 kernel written in Bass

```python
import jax, jax.numpy as jnp
from jax import lax
import numpy as np

D_MODEL = 2048
BATCH = 1
SEQ = 16384
DEPTH = 2

HEAD_DIM = 128
ATTN_WIDTH = D_MODEL // 2
N_ATTN_HEADS = ATTN_WIDTH // HEAD_DIM
CONV_CH = D_MODEL // 4
CONV_GROUPS = CONV_CH // HEAD_DIM
CONV_K = 3
MEM_WIDTH = D_MODEL // 4
N_MEM_HEADS = MEM_WIDTH // HEAD_DIM
MEM_TOKENS = 256
MIX_WIDTH = ATTN_WIDTH + CONV_CH + MEM_WIDTH
IN_WIDTH = 3 * ATTN_WIDTH + 3 * CONV_CH + MEM_WIDTH
IN_SPLITS = [ATTN_WIDTH, 2 * ATTN_WIDTH, 3 * ATTN_WIDTH,
             3 * ATTN_WIDTH + CONV_CH, 3 * ATTN_WIDTH + 2 * CONV_CH,
             3 * ATTN_WIDTH + 3 * CONV_CH]
D_FF = 11 * D_MODEL // 4
FFN_RES = 0.5
MOBA_BLOCK = 256
MOBA_TOPK = 3
Q_CHUNK = 128
ROPE_THETA = 10000.0
RMS_EPS = 1e-6

kernel_name = "hymba_moba_shortconv_memory_macaron"


def rms_norm(x, g):
    xf = x.astype(jnp.float32)
    y = xf * lax.rsqrt(jnp.mean(xf * xf, axis=-1, keepdims=True) + RMS_EPS)
    return (y * g.astype(jnp.float32)).astype(x.dtype)


def swiglu(h, w_gate_up, w_down):
    g, u = jnp.split(h @ w_gate_up, 2, axis=-1)
    return (jax.nn.silu(g) * u) @ w_down


def rope_tables(positions):
    inv_freq = ROPE_THETA ** (-jnp.arange(0, HEAD_DIM, 2, dtype=jnp.float32) / HEAD_DIM)
    ang = positions.astype(jnp.float32)[..., None] * inv_freq
    return jnp.cos(ang)[:, :, None, :], jnp.sin(ang)[:, :, None, :]


def apply_rope(x, cos, sin):
    xf = x.astype(jnp.float32)
    x1, x2 = jnp.split(xf, 2, axis=-1)
    return jnp.concatenate([x1 * cos - x2 * sin, x2 * cos + x1 * sin], axis=-1).astype(x.dtype)


def moba_attention(q, k, v):
    b, s, h, d = q.shape
    n_blk = -(-s // MOBA_BLOCK)
    s_pad = n_blk * MOBA_BLOCK
    top_k = min(MOBA_TOPK, n_blk)
    scale = d ** -0.5
    qh = q.transpose(0, 2, 1, 3)
    pad = ((0, 0), (0, 0), (0, s_pad - s), (0, 0))
    kh = jnp.pad(k.transpose(0, 2, 1, 3), pad)
    vh = jnp.pad(v.transpose(0, 2, 1, 3), pad)
    k_blocks = kh.reshape(b, h, n_blk, MOBA_BLOCK, d)
    v_blocks = vh.reshape(b, h, n_blk, MOBA_BLOCK, d)
    k_mean = jnp.mean(k_blocks.astype(jnp.float32), axis=3)
    bi = jnp.arange(b)[:, None, None, None]
    hi = jnp.arange(h)[None, :, None, None]
    blk_ids = jnp.arange(n_blk)

    def chunk(ci):
        start = ci * Q_CHUNK
        qblk = start // MOBA_BLOCK
        qc = lax.dynamic_slice_in_dim(qh, start, Q_CHUNK, axis=2)
        gate = jnp.einsum('bhqd,bhnd->bhqn', qc.astype(jnp.float32), k_mean)
        gate = jnp.where(blk_ids < qblk, gate, -jnp.inf)
        _, idx = lax.top_k(gate, top_k)
        valid = idx < qblk
        k_sel = k_blocks[bi, hi, idx]
        v_sel = v_blocks[bi, hi, idx]
        s_sel = jnp.einsum('bhqd,bhqtkd->bhqtk', qc, k_sel,
                           preferred_element_type=jnp.float32) * scale
        s_sel = jnp.where(valid[..., None], s_sel, -jnp.inf)
        blk_start = qblk * MOBA_BLOCK
        k_own = lax.dynamic_slice_in_dim(kh, blk_start, MOBA_BLOCK, axis=2)
        v_own = lax.dynamic_slice_in_dim(vh, blk_start, MOBA_BLOCK, axis=2)
        s_own = jnp.einsum('bhqd,bhkd->bhqk', qc, k_own,
                           preferred_element_type=jnp.float32) * scale
        q_pos = start + jnp.arange(Q_CHUNK)
        k_pos = blk_start + jnp.arange(MOBA_BLOCK)
        s_own = jnp.where(k_pos[None, :] <= q_pos[:, None], s_own, -jnp.inf)
        n_sel = top_k * MOBA_BLOCK
        logits = jnp.concatenate([s_sel.reshape(b, h, Q_CHUNK, n_sel), s_own], axis=-1)
        p = jax.nn.softmax(logits, axis=-1).astype(v.dtype)
        p_sel = p[..., :n_sel].reshape(b, h, Q_CHUNK, top_k, MOBA_BLOCK)
        p_own = p[..., n_sel:]
        return (jnp.einsum('bhqtk,bhqtkd->bhqd', p_sel, v_sel)
                + jnp.einsum('bhqk,bhkd->bhqd', p_own, v_own))

    out = lax.map(chunk, jnp.arange(s // Q_CHUNK))
    return out.transpose(1, 0, 3, 2, 4).reshape(b, s, h * d)


def short_conv(u, w):
    s = u.shape[1]
    up = jnp.pad(u, ((0, 0), (CONV_K - 1, 0), (0, 0)))
    y = up[:, 0:s] * w[0]
    for j in range(1, CONV_K):
        y = y + up[:, j:j + s] * w[j]
    return y


def memory_attention(mq, mkv, q_g, k_g):
    b, s, _ = mq.shape
    m = mkv.shape[1]
    q = rms_norm(mq.reshape(b, s, N_MEM_HEADS, HEAD_DIM), q_g)
    mk, mv = jnp.split(mkv, 2, axis=-1)
    k = rms_norm(mk.reshape(b, m, N_MEM_HEADS, HEAD_DIM), k_g)
    v = mv.reshape(b, m, N_MEM_HEADS, HEAD_DIM)
    sc = jnp.einsum('bshd,bmhd->bhsm', q, k, preferred_element_type=jnp.float32) * HEAD_DIM ** -0.5
    p = jax.nn.softmax(sc, axis=-1).astype(v.dtype)
    return jnp.einsum('bhsm,bmhd->bshd', p, v).reshape(b, s, MEM_WIDTH)


def setup_inputs(seed: int = 0) -> dict:
    key = jax.random.key(seed)
    ks = jax.random.split(key, 20)

    def w(k, shape, fan_in):
        return jax.random.normal(k, shape, jnp.float32) * fan_in ** -0.5

    def gain(k, shape):
        return 1.0 + 0.01 * jax.random.normal(k, shape, jnp.float32)

    return {
        "x": jax.random.normal(ks[0], (BATCH, SEQ, D_MODEL), jnp.float32),
        "mem": jax.random.normal(ks[1], (BATCH, MEM_TOKENS, D_MODEL), jnp.float32),
        "positions": jnp.broadcast_to(jnp.arange(SEQ, dtype=jnp.int32), (BATCH, SEQ)),
        "ffn1_norm": gain(ks[2], (DEPTH, D_MODEL)),
        "ffn1_w_gate_up": w(ks[3], (DEPTH, D_MODEL, 2 * D_FF), D_MODEL),
        "ffn1_w_down": w(ks[4], (DEPTH, D_FF, D_MODEL), D_FF),
        "mix_norm": gain(ks[5], (DEPTH, D_MODEL)),
        "w_in": w(ks[6], (DEPTH, D_MODEL, IN_WIDTH), D_MODEL),
        "q_norm": gain(ks[7], (DEPTH, HEAD_DIM)),
        "k_norm": gain(ks[8], (DEPTH, HEAD_DIM)),
        "conv_w": w(ks[9], (DEPTH, CONV_K, CONV_CH), CONV_K),
        "mem_norm": gain(ks[10], (DEPTH, D_MODEL)),
        "w_mem_kv": w(ks[11], (DEPTH, D_MODEL, 2 * MEM_WIDTH), D_MODEL),
        "mq_norm": gain(ks[12], (DEPTH, HEAD_DIM)),
        "mk_norm": gain(ks[13], (DEPTH, HEAD_DIM)),
        "w_out": w(ks[14], (DEPTH, MIX_WIDTH, D_MODEL), MIX_WIDTH),
        "ffn2_norm": gain(ks[15], (DEPTH, D_MODEL)),
        "ffn2_w_gate_up": w(ks[16], (DEPTH, D_MODEL, 2 * D_FF), D_MODEL),
        "ffn2_w_down": w(ks[17], (DEPTH, D_FF, D_MODEL), D_FF),
    }


def reference(x, mem, positions, ffn1_norm, ffn1_w_gate_up, ffn1_w_down, mix_norm, w_in,
              q_norm, k_norm, conv_w, mem_norm, w_mem_kv, mq_norm, mk_norm, w_out,
              ffn2_norm, ffn2_w_gate_up, ffn2_w_down):
    b, s, _ = x.shape
    cos, sin = rope_tables(positions)
    for i in range(DEPTH):
        x = x + FFN_RES * swiglu(rms_norm(x, ffn1_norm[i]), ffn1_w_gate_up[i], ffn1_w_down[i])
        h = rms_norm(x, mix_norm[i])
        proj = h @ w_in[i]
        q, k, v, c_b, c_c, c_x, m_q = jnp.split(proj, IN_SPLITS, axis=-1)
        q = apply_rope(rms_norm(q.reshape(b, s, N_ATTN_HEADS, HEAD_DIM), q_norm[i]), cos, sin)
        k = apply_rope(rms_norm(k.reshape(b, s, N_ATTN_HEADS, HEAD_DIM), k_norm[i]), cos, sin)
        v = v.reshape(b, s, N_ATTN_HEADS, HEAD_DIM)
        y_attn = moba_attention(q, k, v)
        y_conv = c_b * short_conv(c_c * c_x, conv_w[i])
        mkv = rms_norm(mem, mem_norm[i]) @ w_mem_kv[i]
        y_mem = memory_attention(m_q, mkv, mq_norm[i], mk_norm[i])
        x = x + jnp.concatenate([y_attn, y_conv, y_mem], axis=-1) @ w_out[i]
        x = x + FFN_RES * swiglu(rms_norm(x, ffn2_norm[i]), ffn2_w_gate_up[i], ffn2_w_down[i])
    return x
```

```python
import contextlib
import numpy as np
import concourse.bass as bass
import concourse.mybir as mybir
from concourse.bass_utils import run_bass_kernel_spmd

F32 = mybir.dt.float32
BF16 = mybir.dt.bfloat16
I32 = mybir.dt.int32
AF = mybir.ActivationFunctionType
ALU = mybir.AluOpType
AX = mybir.AxisListType

NCORE = 8
HD = 128
BLK = 256
TOPK = 3
EPS = 1e-6
THETA = 10000.0
NEG = -30000.0
BIGNEG = -1.0e30


class Cfg:
    def __init__(self, D=2048, SEQ=16384, DEPTH=2, MEMT=256):
        self.D = D
        self.SEQ = SEQ
        self.DEPTH = DEPTH
        self.MEMT = MEMT
        self.DFF = 11 * D // 4
        self.KC = D // 128
        self.FC = self.DFF // 128
        self.AW = D // 2
        self.HA = self.AW // 128
        self.CC = D // 4
        self.CG = self.CC // 128
        self.MW = D // 4
        self.HM = self.MW // 128
        self.INW = 3 * self.AW + 3 * self.CC + self.MW
        self.TPC = SEQ // NCORE
        self.NT = self.TPC // 512
        self.SLOTS = 2 * self.NT
        self.E = NCORE * self.SLOTS
        self.E2 = 2 * self.E
        self.GF = 11
        self.NG = self.FC // self.GF
        self.VL = 4 * self.KC + 4 + 3 * self.CG
        self.NV = self.DEPTH * self.VL + 1


class Sched:
    ENG = ("pe", "act", "dve", "pool", "sp")
    NDMA = 12

    def __init__(self):
        self.ops = {e: [] for e in self.ENG}
        self.cnt = {e: 0 for e in self.ENG}
        self.lastw = {}
        self.readers = {}
        self.waited = {e: {} for e in self.ENG}
        self.dma_rr = {e: 0 for e in self.ENG}
        self.dma_val = {}
        self.cc_rr = 0
        self.final = []

    def _deps(self, eng, reads, writes, is_dma):
        deps = []
        for r in reads:
            t = self.lastw.get(r)
            if t is not None:
                if not (t[2] == eng and eng == "pe" and not is_dma and not t[3]):
                    deps.append(t)
        for w in writes:
            t = self.lastw.get(w)
            if t is not None and (is_dma or t[3] or t[2] != eng):
                deps.append(t)
            for t in self.readers.get(w, ()):
                if is_dma or t[3] or t[2] != eng:
                    deps.append(t)
        return deps

    def op(self, eng, fn, reads=(), writes=(), dma=False, cc=False):
        deps = self._deps(eng, reads, writes, dma)
        if cc:
            deps.extend((k, v, k[1], True) for k, v in self.dma_val.items())
        if dma:
            if cc:
                i = self.cc_rr
                self.cc_rr = (i + 1) % 4
                key = ("cc", eng, i)
            else:
                i = self.dma_rr[eng]
                self.dma_rr[eng] = (i + 1) % self.NDMA
                key = ("dma", eng, i)
            prev = self.dma_val.get(key, 0)
            if prev:
                deps.append((key, prev, eng, True))
            inc = 0 if cc else 16
            val = prev + (1 if cc else 16)
            self.dma_val[key] = val
            tok = (key, val, eng, True)
        else:
            self.cnt[eng] += 1
            key = ("eng", eng)
            tok = (key, self.cnt[eng], eng, False)
            inc = 1
        need = {}
        for (k, v, _, _) in deps:
            if v > need.get(k, 0):
                need[k] = v
        waits = []
        for k, v in need.items():
            if self.waited[eng].get(k, 0) < v:
                self.waited[eng][k] = v
                waits.append((k, v))
        for r in reads:
            self.readers.setdefault(r, []).append(tok)
        for w in writes:
            self.lastw[w] = tok
            self.readers[w] = []
        self.ops[eng].append((waits, fn, key, inc))
        return tok

    def finish(self, eng, keys):
        need = dict(self.dma_val)
        self.final = (eng, list(need.items()))

    def emit(self, nc, block_ctx, sems):
        engmap = {"pe": "tensor", "act": "scalar", "dve": "vector", "pool": "gpsimd", "sp": "sync"}
        for eng in self.ENG:
            ops = self.ops[eng]
            fin = self.final[1] if self.final and self.final[0] == eng else []
            if not ops and not fin:
                continue

            def body(e, ops=ops, fin=fin):
                for waits, fn, key, inc in ops:
                    for k, v in waits:
                        e.wait_ge(sems[k], v)
                    ins = fn(e)
                    if inc == 0:
                        ins.then_inc(sems[key])
                    else:
                        ins.then_inc(sems[key], inc)
                for k, v in fin:
                    e.wait_ge(sems[k], v)

            getattr(block_ctx, engmap[eng])(body)

    def sem_keys(self):
        keys = [("eng", e) for e in self.ENG]
        for e in self.ENG:
            for i in range(self.NDMA):
                keys.append(("dma", e, i))
        for i in range(4):
            keys.append(("cc", "pool", i))
        return keys


def build(cfg):
    c = cfg
    D, KC, FC, HA, CG, HM, NT, TPC, E, E2, SL = c.D, c.KC, c.FC, c.HA, c.CG, c.HM, c.NT, c.TPC, c.E, c.E2, c.SLOTS
    AW, CC, MW, DFF, MEMT, GF, NG = c.AW, c.CC, c.MW, c.DFF, c.MEMT, c.GF, c.NG
    SCALE = HD ** -0.5
    nc = bass.Bass("TRN2", target_bir_lowering=False)
    S = Sched()

    stage = getattr(c, "stage", None)
    in_names = []
    out_names = []
    decl = {}

    def din(name, shape, dt=F32):
        if name not in decl:
            decl[name] = nc.dram_tensor(name, list(shape), dt, kind="ExternalInput").ap()
            in_names.append(name)
        return decl[name]

    def dout(name, shape, dt=F32):
        if name not in decl:
            decl[name] = nc.dram_tensor(name, list(shape), dt, kind="ExternalOutput").ap()
            out_names.append(name)
        return decl[name]

    def dscr(name, shape, dt):
        if name not in decl:
            decl[name] = nc.dram_tensor(name, list(shape), dt, kind="Internal").ap()
        return decl[name]

    class Lazy:
        def __init__(self, fn):
            self.fn = fn

        def __getitem__(self, l):
            return self.fn(l)

    xT = Lazy(lambda _: din("xT", [D, TPC]))
    memT = din("memT", [D, MEMT])
    pos = din("pos", [1, TPC], I32)
    vecs = din("vecs", [128, c.NV])
    cst = din("cst", [128, 768])
    pmask = din("pmask", [TPC, E])
    selh = din("selh", [E2, 4 * NT])
    w_gu1 = Lazy(lambda l: din(f"ffn1_w_gate_up_{l}", [D, 2 * DFF]))
    w_d1 = Lazy(lambda l: din(f"ffn1_w_down_{l}", [DFF, D]))
    w_in = Lazy(lambda l: din(f"w_in_{l}", [D, c.INW]))
    w_mkv = Lazy(lambda l: din(f"w_mem_kv_{l}", [D, 2 * MW]))
    w_out = Lazy(lambda l: din(f"w_out_{l}", [D, D]))
    w_gu2 = Lazy(lambda l: din(f"ffn2_w_gate_up_{l}", [D, 2 * DFF]))
    w_d2 = Lazy(lambda l: din(f"ffn2_w_down_{l}", [DFF, D]))
    outT = Lazy(lambda _: dout("outT", [D, TPC]))

    def locbuf(nm, shape, dt):
        def fn(l):
            if stage is None:
                return dscr(f"{nm}_loc{l}", shape, dt)
            if stage == l:
                return dout(f"{nm}_loc{l}", shape, dt)
            return din(f"{nm}_loc{l}", shape, dt)
        return Lazy(fn)

    def allbuf(nm, shape, dt):
        def fn(l):
            shp = [NCORE * shape[0], shape[1]]
            if stage is None:
                return dscr(f"{nm}_all{l}", shp, dt)
            return din(f"{nm}_all{l}", shp, dt)
        return Lazy(fn)

    kT_loc = locbuf("kT", [HA * 128, TPC], BF16)
    kT_all = allbuf("kT", [HA * 128, TPC], BF16)
    v_loc = locbuf("v", [HA * TPC, 128], BF16)
    v_all = allbuf("v", [HA * TPC, 128], BF16)
    km_loc = locbuf("km", [HA * 128, SL], F32)
    km_all = allbuf("km", [HA * 128, SL], F32)
    ha_loc = locbuf("ha", [D, 2 * SL], BF16)
    ha_all = allbuf("ha", [D, 2 * SL], BF16)
    if stage is None:
        xs_in = xs_out = Lazy(lambda _: dscr("xs", [D, TPC], F32))
    else:
        xs_in = Lazy(lambda _: din("xs_in", [D, TPC]))
        xs_out = Lazy(lambda _: dout("xs_out", [D, TPC]))

    es = contextlib.ExitStack()
    with es:
        def sb(name, shape, dt):
            return es.enter_context(nc.sbuf_tensor(name, list(shape), dt))

        NWS = 4
        xt = sb("xt", [128, KC, 512], F32)
        hT = sb("hT", [128, KC, 512], BF16)
        aT = sb("aT", [128, GF, 512], BF16)
        wr = sb("wr", [128, NWS, max(KC, GF) * 256], BF16)
        mixT = sb("mixT", [128, KC, 512], BF16)
        vecs_sb = sb("vecs_sb", [128, c.NV], F32)
        cst_sb = sb("cst_sb", [128, 768], F32)
        RT = cst_sb[:, 0:128]
        ident_bf = sb("ident_bf", [128, 128], BF16)
        caus_bf = sb("caus_bf", [128, 2, 256], BF16)
        ones_d = sb("ones_d", [128, 128], BF16)
        ones_h = sb("ones_h", [128, 128], BF16)
        ones_1 = sb("ones_1", [128, 128], BF16)
        sqb = sb("sqb", [128, 2, 512], BF16)
        rstd2 = sb("rstd2", [128, 2, 512], F32)
        rstd = rstd2[:, 0, :]
        posi = sb("posi", [128, 512], I32)
        cosT = sb("cosT", [128, 512], F32)
        sinT = sb("sinT", [128, 512], F32)
        qn2 = sb("qn2", [128, 2, 512], F32)
        ql2 = sb("ql2", [128, 2, 512], BF16)
        ql = ql2[:, 0, :]
        qnh2 = sb("qnh2", [128, 2, 512], BF16)
        qnl2 = sb("qnl2", [128, 2, 512], BF16)
        qnh = qnh2[:, 0, :]
        qnl = qnl2[:, 0, :]
        rt_bf = sb("rt_bf", [128, 128], BF16)
        kmh = sb("kmh", [128, HA, E], BF16)
        kml = sb("kml", [128, HA, E], BF16)
        oneh = sb("oneh", [128, E * 128], BF16)
        t1 = sb("t1", [128, 512], F32)
        t2 = sb("t2", [128, 512], F32)
        qf2 = sb("qf2", [128, 2, 512], F32)
        qb2 = sb("qb2", [128, 2, 512], BF16)
        qf = qf2[:, 0, :]
        qb = qb2[:, 0, :]
        sg = sb("sg", [128, 2, 512], F32)
        vtmp = sb("vtmp", [128, 2, 256], BF16)
        kmt = sb("kmt", [128, HA, 2], F32)
        kmT = sb("kmT", [128, HA, E], F32)
        pm_sb = sb("pm_sb", [128, 4, E], F32)
        pmneg = sb("pmneg", [128, 4, E], F32)
        gm4 = sb("gm4", [128, 4, E], F32)
        top84 = sb("top84", [128, 4, 8], F32)
        selq4 = sb("selq4", [128, 4, E], F32)
        selqb4 = sb("selqb4", [128, 4, E], BF16)
        selbT2 = sb("selbT2", [128, 2, 512], BF16)
        selbT = selbT2[:, 0, :]
        NKR = 4
        ktr = sb("ktr", [128, NKR, 256], BF16)
        vtr = sb("vtr", [128, NKR, 256], BF16)
        NPR = 4
        pTr = sb("pTr", [128, NPR, 512], BF16)
        rl = sb("rl", [128, 512], F32)
        hh = sb("hh", [128, KC, E2], BF16)
        cch = sb("cch", [128, CC], F32)
        uh = sb("uh", [128, CC], BF16)
        selh_f = sb("selh_f", [128, 4 * NT], F32)
        selh_b = sb("selh_b", [128, 4 * NT], BF16)
        uext = sb("uext", [128, 2, 258], F32)
        cct = sb("cct", [128, 512], F32)
        ang = cct
        ang2 = rl
        yc = sb("yc", [128, 2, 256], F32)
        kmemT = sb("kmemT", [128, HM, MEMT], BF16)
        vmem = sb("vmem", [128, MEMT // 128, MW], BF16)
        ps = es.enter_context(nc.psum_tensor("ps", [128, 8 * 512], F32))

        def bank(i):
            return ps[:, i * 512:(i + 1) * 512]

        st = {"gb": 0, "ws": 0, "kr": 0, "pr": 0, "sq": 0, "ol": 0, "banks": [0, 1, 2]}

        def next_bank():
            lst = st["banks"]
            st["gb"] = (st["gb"] + 1) % len(lst)
            return lst[st["gb"]]

        def next_ws():
            b = st["ws"]
            st["ws"] = (b + 1) % NWS
            return b

        S.op("sp", lambda e: e.dma_start(out=vecs_sb[:], in_=vecs), writes=["vecs"], dma=True)
        S.op("sp", lambda e: e.dma_start(out=cst_sb[:], in_=cst), writes=["cst"], dma=True)
        S.op("sp", lambda e: e.dma_start(out=selh_f[0:E2, :], in_=selh), writes=["selh_f"], dma=True)
        S.op("dve", lambda e: e.tensor_copy(out=ident_bf[:], in_=cst_sb[:, 128:256]), reads=["cst"], writes=["ident"])
        S.op("dve", lambda e: e.tensor_tensor(out=rt_bf[:], in0=cst_sb[:, 0:128], in1=cst_sb[:, 0:128], op=ALU.mult), reads=["cst"], writes=["rt_bf"])
        S.op("dve", lambda e: e.tensor_copy(out=caus_bf[:, 0, :], in_=cst_sb[:, 256:512]), reads=["cst"], writes=["caus0"])
        S.op("dve", lambda e: e.tensor_copy(out=caus_bf[:, 1, :], in_=cst_sb[:, 512:768]), reads=["cst"], writes=["caus1"])
        S.op("dve", lambda e: e.tensor_copy(out=selh_b[0:E2, :], in_=selh_f[0:E2, :]), reads=["selh_f"], writes=["selh_b"])
        S.op("dve", lambda e: e.tensor_copy(out=oneh[0:E, :].rearrange("p (e m) -> p e m", m=128),
                                            in_=ident_bf[0:E, 0:E].unsqueeze(2).to_broadcast([E, E, 128])),
             reads=["ident"], writes=["oneh"])
        S.op("pool", lambda e: e.memset(ones_d[:], 1.0 / D), writes=["ones_d"])
        S.op("pool", lambda e: e.memset(ones_h[:], 1.0 / 128), writes=["ones_h"])
        S.op("pool", lambda e: e.memset(ones_1[:], 1.0), writes=["ones_1"])

        def vcol(l, off, n=1):
            b = l * c.VL + off
            return vecs_sb[:, b:b + n]

        INVF = vecs_sb[:, c.DEPTH * c.VL:c.DEPTH * c.VL + 1]

        def rmsnorm(src_chunks, src_keys, gbase_l, gbase_off, dst, dst_keys, N, ones_ap, ones_key):
            nk = len(src_chunks)
            pb = next_bank()
            for k in range(nk):
                s = st["sq"]
                st["sq"] ^= 1
                S.op("act", lambda e, k=k, s=s: e.activation(out=sqb[:, s, 0:N], in_=src_chunks[k], func=AF.Square),
                     reads=[src_keys[k]], writes=[("sqb", s)])
                S.op("pe", lambda e, k=k, s=s: e.matmul(bank(pb)[:, 0:N], lhsT=ones_ap, rhs=sqb[:, s, 0:N],
                                                       start=(k == 0), stop=(k == nk - 1)),
                     reads=[("sqb", s), ones_key], writes=[("ps", pb)])
            S.op("act", lambda e: e.activation(out=rstd[:, 0:N], in_=bank(pb)[:, 0:N], func=AF.Ln, bias=eps_col[:], scale=1.0),
                 reads=[("ps", pb), "eps_col"], writes=[("rstd", 0)])
            S.op("act", lambda e: e.activation(out=rstd[:, 0:N], in_=rstd[:, 0:N], func=AF.Exp, scale=-0.5), reads=[("rstd", 0)], writes=[("rstd", 0)])
            for k in range(nk):
                S.op("dve", lambda e, k=k: e.scalar_tensor_tensor(out=dst[k], in0=src_chunks[k],
                                                                   scalar=vcol(gbase_l, gbase_off + k),
                                                                   in1=rstd[:, 0:N], op0=ALU.mult, op1=ALU.mult),
                     reads=[src_keys[k], ("rstd", 0), "vecs"], writes=[dst_keys[k]])

        def load_w(wap, r0, nk, c0, w):
            s = next_ws()
            dst = wr[:, s, 0:nk * w].rearrange("p (k n) -> p k n", n=w)
            src = wap[r0:r0 + nk * 128, c0:c0 + w].rearrange("(k p) n -> p k n", p=128)
            S.op("pool", lambda e: e.dma_start(out=dst, in_=src), writes=[("wr", s)], dma=True)
            return s, dst

        def gemm_tile(s, wv, n0, rhs_chunks, rhs_keys, N, pb=None):
            if pb is None:
                pb = next_bank()
            nk = len(rhs_chunks)

            def fn(e):
                ins = None
                for k in range(nk):
                    ins = e.matmul(bank(pb)[:, 0:N], lhsT=wv[:, k, n0:n0 + 128], rhs=rhs_chunks[k],
                                   start=(k == 0), stop=(k == nk - 1))
                return ins

            S.op("pe", fn, reads=[("wr", s)] + list(rhs_keys), writes=[("ps", pb)])
            return pb

        def slabs(c0, width, maxw=256):
            out = []
            o = 0
            while o < width:
                w = min(maxw, width - o)
                out.append((c0 + o, w))
                o += w
            return out

        xt_ch = [xt[:, k, :] for k in range(KC)]
        xt_keys = [("xt", k) for k in range(KC)]
        hT_ch = [hT[:, k, :] for k in range(KC)]
        hT_keys = [("hT", k) for k in range(KC)]
        mix_ch = [mixT[:, k, :] for k in range(KC)]
        mix_keys = [("mix", k) for k in range(KC)]

        def ffn(l, w_gu, w_d, goff):
            rmsnorm(xt_ch, xt_keys, l, goff, hT_ch, hT_keys, 512, ones_d[:], "ones_d")
            for g in range(NG):
                f0 = g * GF
                for (c0, w) in slabs(f0 * 128, GF * 128):
                    sg_, wg = load_w(w_gu[l], 0, KC, c0, w)
                    su_, wu = load_w(w_gu[l], 0, KC, DFF + c0, w)
                    for n0 in range(0, w, 128):
                        i = (c0 + n0) // 128 - f0
                        pg = gemm_tile(sg_, wg, n0, hT_ch, hT_keys, 512)
                        pu = gemm_tile(su_, wu, n0, hT_ch, hT_keys, 512)
                        sgs = i % 2
                        S.op("act", lambda e, pg=pg, sgs=sgs: e.activation(out=sg[:, sgs, :], in_=bank(pg), func=AF.Silu),
                             reads=[("ps", pg)], writes=[("sg", sgs)])
                        S.op("dve", lambda e, pu=pu, sgs=sgs, i=i: e.tensor_tensor(out=aT[:, i, :], in0=sg[:, sgs, :],
                                                                                    in1=bank(pu), op=ALU.mult),
                             reads=[("ps", pu), ("sg", sgs)], writes=[("aT", i)])
                a_ch = [aT[:, i, :] for i in range(GF)]
                a_keys = [("aT", i) for i in range(GF)]
                for (c0, w) in slabs(0, D):
                    sd_, wd = load_w(w_d[l], f0 * 128, GF, c0, w)
                    for n0 in range(0, w, 128):
                        n = (c0 + n0) // 128
                        pd = gemm_tile(sd_, wd, n0, a_ch, a_keys, 512)
                        S.op("dve", lambda e, pd=pd, n=n: e.scalar_tensor_tensor(out=xt[:, n, :], in0=bank(pd), scalar=0.5,
                                                                                  in1=xt[:, n, :], op0=ALU.mult, op1=ALU.add),
                             reads=[("ps", pd), ("xt", n)], writes=[("xt", n)])

        def rope_tables(j):
            src = pos[0:1, j * 512:(j + 1) * 512].partition_broadcast(128)
            S.op("sp", lambda e: e.dma_start(out=posi[:], in_=src), writes=["posi"], dma=True)
            S.op("dve", lambda e: e.tensor_copy(out=ang[:], in_=posi[:]), reads=["posi"], writes=["cct"])
            S.op("dve", lambda e: e.tensor_scalar(out=ang[:], in0=ang[:], scalar1=INVF, scalar2=None, op0=ALU.mult),
                 reads=["cct", "vecs"], writes=["cct"])
            TWO_PI = float(2 * np.pi)

            def reduce_angle(dst, dkey, shift):
                S.op("dve", lambda e: e.tensor_scalar(out=dst[:], in0=ang[:], scalar1=shift, scalar2=None, op0=ALU.add),
                     reads=["cct"], writes=[dkey])
                S.op("dve", lambda e: e.tensor_scalar(out=t1[:], in0=dst[:], scalar1=1.0 / TWO_PI, scalar2=None, op0=ALU.mult),
                     reads=[dkey], writes=["t1"])
                S.op("dve", lambda e: e.tensor_copy(out=posi[:], in_=t1[:]), reads=["t1"], writes=["posi"])
                S.op("dve", lambda e: e.tensor_copy(out=t1[:], in_=posi[:]), reads=["posi"], writes=["t1"])
                S.op("dve", lambda e: e.scalar_tensor_tensor(out=dst[:], in0=t1[:], scalar=-TWO_PI, in1=dst[:], op0=ALU.mult, op1=ALU.add),
                     reads=["t1", dkey], writes=[dkey])
                S.op("dve", lambda e: e.tensor_scalar(out=t1[:], in0=dst[:], scalar1=float(np.pi), scalar2=None, op0=ALU.is_gt),
                     reads=[dkey], writes=["t1"])
                S.op("dve", lambda e: e.scalar_tensor_tensor(out=dst[:], in0=t1[:], scalar=-TWO_PI, in1=dst[:], op0=ALU.mult, op1=ALU.add),
                     reads=["t1", dkey], writes=[dkey])
                S.op("dve", lambda e: e.tensor_scalar(out=t1[:], in0=dst[:], scalar1=-float(np.pi), scalar2=None, op0=ALU.is_lt),
                     reads=[dkey], writes=["t1"])
                S.op("dve", lambda e: e.scalar_tensor_tensor(out=dst[:], in0=t1[:], scalar=TWO_PI, in1=dst[:], op0=ALU.mult, op1=ALU.add),
                     reads=["t1", dkey], writes=[dkey])

            reduce_angle(ang2, "rl", float(np.pi / 2))
            reduce_angle(t2, "t2", 0.0)
            S.op("act", lambda e: e.activation(out=sinT[:], in_=t2[:], func=AF.Sin), reads=["t2"], writes=["sinT"])
            S.op("act", lambda e: e.activation(out=cosT[:], in_=ang2[:], func=AF.Sin), reads=["rl"], writes=["cosT"])
            S.op("dve", lambda e: e.tensor_scalar(out=sinT[0:64, :], in0=sinT[0:64, :], scalar1=-1.0, scalar2=None, op0=ALU.mult),
                 reads=["sinT"], writes=["sinT"])

        eps_col = sb("eps_col", [128, 1], F32)
        S.op("pool", lambda e: e.memset(eps_col[:], EPS), writes=["eps_col"])

        def head_norm_a(pb, gcol, dst_f, dst_key, N, rope, d=0):
            s = st["sq"]
            st["sq"] ^= 1
            S.op("act", lambda e: e.activation(out=sqb[:, s, 0:N], in_=bank(pb)[:, 0:N], func=AF.Square),
                 reads=[("ps", pb)], writes=[("sqb", s)])
            p2 = next_bank()
            S.op("pe", lambda e: e.matmul(bank(p2)[:, 0:N], lhsT=ones_h[:], rhs=sqb[:, s, 0:N], start=True, stop=True),
                 reads=[("sqb", s), "ones_h"], writes=[("ps", p2)])
            rs = rstd2[:, d, :]
            S.op("act", lambda e: e.activation(out=rs[:, 0:N], in_=bank(p2)[:, 0:N], func=AF.Ln, bias=eps_col[:], scale=1.0),
                 reads=[("ps", p2), "eps_col"], writes=[("rstd", d)])
            S.op("act", lambda e: e.activation(out=rs[:, 0:N], in_=rs[:, 0:N], func=AF.Exp, scale=-0.5), reads=[("rstd", d)], writes=[("rstd", d)])
            qn_ = qn2[:, d, :]
            tgt = qn_ if rope else dst_f
            tkey = ("qn", d) if rope else dst_key
            S.op("dve", lambda e: e.scalar_tensor_tensor(out=tgt[:, 0:N], in0=bank(pb)[:, 0:N], scalar=gcol,
                                                         in1=rs[:, 0:N], op0=ALU.mult, op1=ALU.mult),
                 reads=[("ps", pb), ("rstd", d), "vecs"], writes=[tkey])
            if not rope:
                return None
            qh_ = qnh2[:, d, :]
            qo_ = qnl2[:, d, :]
            S.op("act", lambda e: e.activation(out=qh_[:, 0:N], in_=qn_[:, 0:N], func=AF.Copy), reads=[("qn", d)], writes=[("qnh", d)])
            S.op("dve", lambda e: e.tensor_tensor(out=qo_[:, 0:N], in0=qn_[:, 0:N], in1=qh_[:, 0:N], op=ALU.subtract),
                 reads=[("qn", d), ("qnh", d)], writes=[("qnl", d)])
            p3 = next_bank()

            def rfn(e):
                e.matmul(bank(p3)[:, 0:N], lhsT=rt_bf[:], rhs=qh_[:, 0:N], start=True, stop=False)
                return e.matmul(bank(p3)[:, 0:N], lhsT=rt_bf[:], rhs=qo_[:, 0:N], start=False, stop=True)

            S.op("pe", rfn, reads=[("qnh", d), ("qnl", d), "rt_bf"], writes=[("ps", p3)])
            return p3

        def head_norm_b(p3, dst_f, dst_key, N, d=0):
            qn_ = qn2[:, d, :]
            S.op("dve", lambda e: e.tensor_tensor(out=t1[:, 0:N], in0=qn_[:, 0:N], in1=cosT[:, 0:N], op=ALU.mult),
                 reads=[("qn", d), "cosT"], writes=["t1"])
            S.op("dve", lambda e: e.tensor_tensor(out=t2[:, 0:N], in0=bank(p3)[:, 0:N], in1=sinT[:, 0:N], op=ALU.mult),
                 reads=[("ps", p3), "sinT"], writes=["t2"])
            S.op("dve", lambda e: e.tensor_tensor(out=dst_f[:, 0:N], in0=t1[:, 0:N], in1=t2[:, 0:N], op=ALU.add),
                 reads=["t1", "t2"], writes=[dst_key])

        def head_norm(pb, gcol, dst_f, dst_key, N, rope, d=0):
            p3 = head_norm_a(pb, gcol, dst_f, dst_key, N, rope, d)
            if rope:
                head_norm_b(p3, dst_f, dst_key, N, d)

        def kv_phase(l, j):
            rmsnorm(xt_ch, xt_keys, l, KC, hT_ch, hT_keys, 512, ones_d[:], "ones_d")
            rope_tables(j)
            st["banks"] = [0, 1, 2, 3, 4, 5, 6, 7]
            st["gb"] = 0
            pbs = {}

            def k_gemm(h):
                s_, wv = load_w(w_in[l], 0, KC, AW + h * 128, 128)
                pbs[h] = gemm_tile(s_, wv, 0, hT_ch, hT_keys, 512)

            def k_tail(h, p3):
                d = h % 2
                kf_ = qf2[:, d, :]
                kb_ = qb2[:, d, :]
                kfk = "qf" if d == 0 else ("qf", 1)
                kbk = "qb" if d == 0 else ("qb", 1)
                head_norm_b(p3, kf_, kfk, 512, d)
                S.op("act", lambda e: e.activation(out=kb_, in_=kf_, func=AF.Copy), reads=[kfk], writes=[kbk])
                S.op("sp", lambda e: e.dma_start(out=kT_loc[l][h * 128:(h + 1) * 128, j * 512:(j + 1) * 512], in_=kb_),
                     reads=[kbk], writes=[("kT_loc", l)], dma=True)
                S.op("dve", lambda e: e.tensor_reduce(out=kmt[:, h, :], in_=kf_.rearrange("p (a b) -> p a b", a=2), axis=AX.X, op=ALU.add),
                     reads=[kfk], writes=[("kmt", h)])
                S.op("dve", lambda e: e.tensor_single_scalar(out=kmt[:, h, :], in_=kmt[:, h, :], scalar=1.0 / BLK, op=ALU.mult),
                     reads=[("kmt", h)], writes=[("kmt", h)])
                S.op("sp", lambda e: e.dma_start(out=km_loc[l][h * 128:(h + 1) * 128, 2 * j:2 * j + 2], in_=kmt[:, h, :]),
                     reads=[("kmt", h)], writes=[("km_loc", l)], dma=True)

            k_gemm(0)
            p3s = {}
            for h in range(HA):
                if h + 1 < HA:
                    k_gemm(h + 1)
                d = h % 2
                p3s[h] = head_norm_a(pbs[h], vcol(l, 4 * KC + 1), None, None, 512, True, d)
                if h >= 1:
                    k_tail(h - 1, p3s[h - 1])
            k_tail(HA - 1, p3s[HA - 1])
            for (c0, w) in slabs(2 * AW, AW):
                s_, wv = load_w(w_in[l], 0, KC, c0, w)
                for tc in range(4):
                    pb = next_bank()

                    def fn(e, tc=tc, wv=wv, pb=pb, w=w):
                        ins = None
                        for k in range(KC):
                            ins = e.matmul(bank(pb)[:, 0:w], lhsT=hT[:, k, tc * 128:(tc + 1) * 128], rhs=wv[:, k, 0:w],
                                           start=(k == 0), stop=(k == KC - 1))
                        return ins

                    S.op("pe", fn, reads=[("wr", s_)] + hT_keys, writes=[("ps", pb)])
                    vs = tc % 2
                    S.op("act", lambda e, pb=pb, vs=vs, w=w: e.activation(out=vtmp[:, vs, 0:w], in_=bank(pb)[:, 0:w], func=AF.Copy),
                         reads=[("ps", pb)], writes=[("vtmp", vs)])
                    for hh_ in range(w // 128):
                        h = (c0 - 2 * AW) // 128 + hh_
                        r0 = h * TPC + j * 512 + tc * 128
                        S.op("sp", lambda e, vs=vs, hh_=hh_, r0=r0: e.dma_start(out=v_loc[l][r0:r0 + 128, :],
                                                                                in_=vtmp[:, vs, hh_ * 128:(hh_ + 1) * 128]),
                             reads=[("vtmp", vs)], writes=[("v_loc", l)], dma=True)
            for half in range(2):
                col = half * 256 + 254
                dst = ha_loc[l].rearrange("(k p) c -> p k c", p=128)[:, :, (2 * j + half) * 2:(2 * j + half) * 2 + 2]
                S.op("sp", lambda e, col=col, dst=dst: e.dma_start(out=dst, in_=hT[:, :, col:col + 2]),
                     reads=hT_keys, writes=[("ha_loc", l)], dma=True)
            st["banks"] = [0, 1, 2]
            st["gb"] = 0

        def gather(l):
            rg = [list(range(NCORE))]
            for nm, loc, al in (("kT", kT_loc[l], kT_all[l]), ("v", v_loc[l], v_all[l]),
                                ("km", km_loc[l], km_all[l]), ("ha", ha_loc[l], ha_all[l])):
                S.op("pool", lambda e, loc=loc, al=al: e.collective_compute("AllGather", op=ALU.bypass, replica_groups=rg,
                                                                            ins=[loc.opt()], outs=[al.opt()]),
                     reads=[(nm + "_loc", l)], writes=[(nm + "_all", l)], dma=True, cc=True)

        pend = []

        def attn_pre(eng, fn, reads, writes):
            pend.append(("pre", (eng, fn, reads, writes)))

        def attn_core(kt_tiles, q_ap, N, c0, bias_fn, v_fn, keys_r, first, last, ob, lb):
            pend.append(("step", (kt_tiles, q_ap, N, c0, bias_fn, v_fn, keys_r, first, last, ob, lb)))

        def attn_flush():
            steps = []
            cur_pre = []
            posts_after = {}
            for kind, item in pend:
                if kind == "pre":
                    cur_pre.append(item)
                elif kind == "post":
                    posts_after.setdefault(len(steps) - 1, []).append(item)
                else:
                    steps.append((cur_pre, item))
                    cur_pre = []
            del pend[:]
            info = [None] * len(steps)

            def emit_qk(t):
                pre, (kt_tiles, q_ap, N, c0, bias_fn, v_fn, keys_r, first, last, ob, lb) = steps[t]
                for (eng, fn, reads, writes) in pre:
                    S.op(eng, fn, reads=reads, writes=writes, dma=True)
                pb = next_bank()
                lhsT_k, kkey = kt_tiles

                def fn1(e):
                    ins = e.matmul(bank(pb)[:, 0:N], lhsT=lhsT_k, rhs=q_ap, start=True, stop=(bias_fn is None))
                    if bias_fn is not None:
                        ins = bias_fn(e, bank(pb)[:, 0:N])
                    return ins

                S.op("pe", fn1, reads=[kkey] + keys_r, writes=[("ps", pb)])
                info[t] = pb

            def emit_rest(t):
                pre, (kt_tiles, q_ap, N, c0, bias_fn, v_fn, keys_r, first, last, ob, lb) = steps[t]
                pb = info[t]
                pr = st["pr"]
                st["pr"] = (pr + 1) % NPR
                S.op("act", lambda e: e.activation(out=pTr[:, pr, 0:N], in_=bank(pb)[:, 0:N], func=AF.Exp, scale=SCALE),
                     reads=[("ps", pb)], writes=[("pT", pr)])
                v_ap, vkey = v_fn

                def fn2(e):
                    e.matmul(bank(ob)[:, c0:c0 + N], lhsT=v_ap, rhs=pTr[:, pr, 0:N], start=first, stop=last)
                    return e.matmul(bank(lb)[:, c0:c0 + N], lhsT=ones_1[:], rhs=pTr[:, pr, 0:N], start=first, stop=last)

                S.op("pe", fn2, reads=[("pT", pr), vkey, "ones_1"], writes=[("ps", ob), ("ps", lb)])

            n = len(steps)
            if n == 0:
                return
            emit_qk(0)
            for t in range(n):
                if t + 1 < n:
                    emit_qk(t + 1)
                emit_rest(t)
                for p in posts_after.get(t, []):
                    p()

        def attn_finish(ob, lb, dst_k):
            attn_flush()
            S.op("act", lambda e: e.activation(out=rl[:], in_=bank(lb), func=AF.Ln), reads=[("ps", lb)], writes=["rl"])
            S.op("act", lambda e: e.activation(out=rl[:], in_=rl[:], func=AF.Exp, scale=-1.0), reads=["rl"], writes=["rl"])
            S.op("dve", lambda e: e.tensor_tensor(out=mixT[:, dst_k, :], in0=bank(ob), in1=rl[:], op=ALU.mult),
                 reads=[("ps", ob), "rl"], writes=[("mix", dst_k)])

        def next_ol():
            o = st["ol"]
            st["ol"] ^= 1
            return 4 + o, 6 + o

        def mem_prep(l):
            m_ch = [xt[:, k, 0:MEMT] for k in range(KC)]
            S.op("sp", lambda e: e.dma_start(out=xt[:, :, 0:MEMT], in_=memT.rearrange("(k p) t -> p k t", p=128)),
                 writes=xt_keys, dma=True)
            hm_ch = [hT[:, k, 0:MEMT] for k in range(KC)]
            hm_keys = hT_keys
            rmsnorm(m_ch, xt_keys, l, 2 * KC, hm_ch, hm_keys, MEMT, ones_d[:], "ones_d")
            for hm in range(HM):
                s_, wv = load_w(w_mkv[l], 0, KC, hm * 128, 128)
                pb = gemm_tile(s_, wv, 0, hm_ch, hm_keys, MEMT)
                head_norm(pb, vcol(l, 4 * KC + 3), qf, "qf", MEMT, False)
                S.op("act", lambda e, hm=hm: e.activation(out=kmemT[:, hm, :], in_=qf[:, 0:MEMT], func=AF.Copy),
                     reads=["qf"], writes=[("kmemT", hm)])
            for (c0, w) in slabs(MW, MW):
                s_, wv = load_w(w_mkv[l], 0, KC, c0, w)
                for mt in range(MEMT // 128):
                    pb = next_bank()

                    def fn(e, mt=mt, wv=wv, pb=pb, w=w):
                        ins = None
                        for k in range(KC):
                            ins = e.matmul(bank(pb)[:, 0:w], lhsT=hT[:, k, mt * 128:(mt + 1) * 128], rhs=wv[:, k, 0:w],
                                           start=(k == 0), stop=(k == KC - 1))
                        return ins

                    S.op("pe", fn, reads=[("wr", s_)] + hm_keys, writes=[("ps", pb)])
                    S.op("act", lambda e, mt=mt, pb=pb, w=w, c0=c0: e.activation(out=vmem[:, mt, c0 - MW:c0 - MW + w],
                                                                                in_=bank(pb)[:, 0:w], func=AF.Copy),
                         reads=[("ps", pb)], writes=[("vmem", mt)])

        def mixer(l, j):
            rmsnorm(xt_ch, xt_keys, l, KC, hT_ch, hT_keys, 512, ones_d[:], "ones_d")
            rope_tables(j)
            S.op("sp", lambda e: e.dma_start(out=pm_sb[:], in_=pmask[j * 512:(j + 1) * 512, :].rearrange("(a p) e -> p a e", p=128)),
                 writes=["pm"], dma=True)
            S.op("dve", lambda e: e.tensor_scalar(out=pmneg[:], in0=pm_sb[:], scalar1=-BIGNEG, scalar2=BIGNEG,
                                                   op0=ALU.mult, op1=ALU.add),
                 reads=["pm"], writes=["pmneg"])
            for h in range(HA):
                src = km_all[l].rearrange("(c r) s -> r c s", c=NCORE)[h * 128:(h + 1) * 128, :, :]
                S.op("sp", lambda e, h=h, src=src: e.dma_start(out=kmT[:, h, :].rearrange("p (c s) -> p c s", c=NCORE), in_=src),
                     reads=[("km_all", l)], writes=[("kmT", h)], dma=True)
                S.op("act", lambda e, h=h: e.activation(out=kmh[:, h, :], in_=kmT[:, h, :], func=AF.Copy), reads=[("kmT", h)], writes=[("kmh", h)])
                S.op("dve", lambda e, h=h: e.tensor_tensor(out=kml[:, h, :], in0=kmT[:, h, :], in1=kmh[:, h, :], op=ALU.subtract),
                     reads=[("kmT", h), ("kmh", h)], writes=[("kml", h)])
            def kq(name, d):
                return name if d == 0 else (name, d)

            def make_parts(h):
                d = h % 2
                gcol = vcol(l, 4 * KC + 0)
                qf_ = qf2[:, d, :]
                qb_ = qb2[:, d, :]
                ql_ = ql2[:, d, :]
                qn_ = qn2[:, d, :]
                rs = rstd2[:, d, :]
                sq_i = [0]

                def P1():
                    s_, wv = load_w(w_in[l], 0, KC, h * 128, 128)
                    gemm_tile(s_, wv, 0, hT_ch, hT_keys, 512, pb=3)
                    sq = st["sq"]
                    st["sq"] ^= 1
                    sq_i[0] = sq
                    S.op("act", lambda e: e.activation(out=sqb[:, sq, :], in_=bank(3), func=AF.Square),
                         reads=[("ps", 3)], writes=[("sqb", sq)])

                def P2():
                    sq = sq_i[0]
                    S.op("pe", lambda e: e.matmul(bank(2), lhsT=ones_h[:], rhs=sqb[:, sq, :], start=True, stop=True),
                         reads=[("sqb", sq), "ones_h"], writes=[("ps", 2)])
                    S.op("act", lambda e: e.activation(out=rs, in_=bank(2), func=AF.Ln, bias=eps_col[:], scale=1.0),
                         reads=[("ps", 2), "eps_col"], writes=[("rstd", d)])
                    S.op("act", lambda e: e.activation(out=rs, in_=rs, func=AF.Exp, scale=-0.5), reads=[("rstd", d)], writes=[("rstd", d)])
                    S.op("dve", lambda e: e.scalar_tensor_tensor(out=qn_, in0=bank(3), scalar=gcol, in1=rs, op0=ALU.mult, op1=ALU.mult),
                         reads=[("ps", 3), ("rstd", d), "vecs"], writes=[("qn", d)])
                    S.op("act", lambda e: e.activation(out=qnh[:], in_=qn_, func=AF.Copy), reads=[("qn", d)], writes=[("qnh", 0)])
                    S.op("dve", lambda e: e.tensor_tensor(out=qnl[:], in0=qn_, in1=qnh[:], op=ALU.subtract), reads=[("qn", d), ("qnh", 0)], writes=[("qnl", 0)])

                def P3():
                    def rfn(e):
                        e.matmul(bank(2), lhsT=rt_bf[:], rhs=qnh[:], start=True, stop=False)
                        return e.matmul(bank(2), lhsT=rt_bf[:], rhs=qnl[:], start=False, stop=True)

                    S.op("pe", rfn, reads=[("qnh", 0), ("qnl", 0), "rt_bf"], writes=[("ps", 2)])
                    S.op("dve", lambda e: e.tensor_tensor(out=t1[:], in0=qn_, in1=cosT[:], op=ALU.mult), reads=[("qn", d), "cosT"], writes=["t1"])
                    S.op("dve", lambda e: e.tensor_tensor(out=t2[:], in0=bank(2), in1=sinT[:], op=ALU.mult), reads=[("ps", 2), "sinT"], writes=["t2"])
                    S.op("dve", lambda e: e.tensor_tensor(out=qf_, in0=t1[:], in1=t2[:], op=ALU.add), reads=["t1", "t2"], writes=[kq("qf", d)])
                    S.op("act", lambda e: e.activation(out=qb_, in_=qf_, func=AF.Copy), reads=[kq("qf", d)], writes=[kq("qb", d)])
                    S.op("dve", lambda e: e.tensor_tensor(out=ql_, in0=qf_, in1=qb_, op=ALU.subtract), reads=[kq("qf", d), kq("qb", d)], writes=[("ql", d)])

                def P4():
                    def gfn(e):
                        ins = None
                        for qc in range(4):
                            o = bank(2)[:, qc * E:(qc + 1) * E]
                            e.matmul(o, lhsT=qb_[:, qc * 128:(qc + 1) * 128], rhs=kmh[:, h, :], start=True, stop=False)
                            e.matmul(o, lhsT=ql_[:, qc * 128:(qc + 1) * 128], rhs=kmh[:, h, :], start=False, stop=False)
                            ins = e.matmul(o, lhsT=qb_[:, qc * 128:(qc + 1) * 128], rhs=kml[:, h, :], start=False, stop=True)
                        return ins

                    S.op("pe", gfn, reads=[kq("qb", d), ("ql", d), ("kmh", h), ("kml", h)], writes=[("ps", 2)])
                    pg3 = bank(2)[:, 0:4 * E].rearrange("p (a e) -> p a e", a=4)
                    S.op("dve", lambda e: e.tensor_tensor(out=gm4[:], in0=pg3, in1=pm_sb[:], op=ALU.mult), reads=[("ps", 2), "pm"], writes=["gm"])
                    S.op("dve", lambda e: e.tensor_tensor(out=gm4[:], in0=gm4[:], in1=pmneg[:], op=ALU.add), reads=["gm", "pmneg"], writes=["gm"])
                    for qc in range(4):
                        S.op("dve", lambda e, qc=qc: e.max(out=top84[:, qc, :], in_=gm4[:, qc, :]), reads=["gm"], writes=[("top8", qc)])
                    for qc in range(4):
                        S.op("dve", lambda e, qc=qc: e.tensor_scalar(out=selq4[:, qc, :], in0=gm4[:, qc, :], scalar1=top84[:, qc, TOPK - 1:TOPK],
                                                                      scalar2=None, op0=ALU.is_ge),
                             reads=["gm", ("top8", qc)], writes=[("selq", qc)])
                    S.op("dve", lambda e: e.tensor_tensor(out=selq4[:], in0=selq4[:], in1=pm_sb[:], op=ALU.mult),
                         reads=[("selq", q_) for q_ in range(4)] + ["pm"], writes=["selq_all"])
                    S.op("dve", lambda e: e.tensor_scalar(out=selqb4[:], in0=selq4[:], scalar1=-NEG, scalar2=NEG, op0=ALU.mult, op1=ALU.add),
                         reads=["selq_all"], writes=["selqb"])

                def P5():
                    ptb = bank(3).bitcast(BF16)

                    def tfn(e):
                        ins = None
                        for qc in range(4):
                            ins = e.transpose(out=ptb[0:E, qc * 128:(qc + 1) * 128], in_=selqb4[:, qc, :], identity=ident_bf[:])
                        return ins

                    S.op("pe", tfn, reads=["selqb", "ident"], writes=[("ps", 3)])
                    S.op("act", lambda e: e.activation(out=selbT2[0:E, d, :], in_=ptb[0:E, 0:512], func=AF.Copy),
                         reads=[("ps", 3)], writes=[kq("selbT", d)])

                return [P1, P2, P3, P4, P5]

            def head_items(h, ob, lb):
                d = h % 2
                qb_ = qb2[:, d, :]
                sel_ = selbT2[0:E, d, :]
                items = []
                first = True
                for cc_ in range(NCORE):
                    for s2 in range(2 * (j + 1)):
                        e_idx = cc_ * SL + s2
                        kr = st["kr"]
                        st["kr"] = (kr + 1) % NKR
                        r0 = cc_ * HA * 128 + h * 128
                        items.append(("pre", ("sp", lambda e, kr=kr, r0=r0, s2=s2: e.dma_start(out=ktr[:, kr, :], in_=kT_all[l][r0:r0 + 128, s2 * 256:(s2 + 1) * 256]),
                                              [("kT_all", l)], [("ktr", kr)])))
                        v0 = cc_ * HA * TPC + h * TPC + s2 * 256
                        items.append(("pre", ("sp", lambda e, kr=kr, v0=v0: e.dma_start(out=vtr[:, kr, :].rearrange("p (t d) -> p t d", t=2),
                                                                                     in_=v_all[l][v0:v0 + 256, :].rearrange("(t p) d -> p t d", p=128)),
                                              [("v_all", l)], [("vtr", kr)])))
                        half_only = (s2 == 2 * j + 1)
                        c0_ = 256 if half_only else 0
                        n_ = 256 if half_only else 512
                        for t2_ in range(2):
                            oh = oneh[0:E, e_idx * 128:(e_idx + 1) * 128]

                            def bias_fn(e, out_ap, oh=oh, c0_=c0_, n_=n_):
                                return e.matmul(out_ap, lhsT=oh, rhs=sel_[:, c0_:c0_ + n_], start=False, stop=True)

                            items.append(("step", ((ktr[:, kr, t2_ * 128:(t2_ + 1) * 128], ("ktr", kr)), qb_[:, c0_:c0_ + n_], n_, c0_, bias_fn,
                                                   (vtr[:, kr, t2_ * 128:(t2_ + 1) * 128], ("vtr", kr)), [kq("qb", d), kq("selbT", d), "oneh"],
                                                   first, False, ob, lb)))
                            first = False
                for half in range(2):
                    kr = st["kr"]
                    st["kr"] = (kr + 1) % NKR
                    tk0 = j * 512 + half * 256
                    items.append(("pre", ("sp", lambda e, kr=kr, tk0=tk0: e.dma_start(out=ktr[:, kr, :], in_=kT_loc[l][h * 128:(h + 1) * 128, tk0:tk0 + 256]),
                                          [("kT_loc", l)], [("ktr", kr)])))
                    v0 = h * TPC + tk0
                    items.append(("pre", ("sp", lambda e, kr=kr, v0=v0: e.dma_start(out=vtr[:, kr, :].rearrange("p (t d) -> p t d", t=2),
                                                                                 in_=v_loc[l][v0:v0 + 256, :].rearrange("(t p) d -> p t d", p=128)),
                                          [("v_loc", l)], [("vtr", kr)])))
                    for t2_ in range(2):
                        def bias_fn(e, out_ap, t2_=t2_):
                            return e.matmul(out_ap, lhsT=ident_bf[:], rhs=caus_bf[:, t2_, :], start=False, stop=True)

                        items.append(("step", ((ktr[:, kr, t2_ * 128:(t2_ + 1) * 128], ("ktr", kr)), qb_[:, half * 256:(half + 1) * 256], 256,
                                               half * 256, bias_fn, (vtr[:, kr, t2_ * 128:(t2_ + 1) * 128], ("vtr", kr)),
                                               [kq("qb", d), "ident", "caus0", "caus1"], False, (half == 1 and t2_ == 1), ob, lb)))
                return items

            def fin_ops(ob, lb, dst_k):
                def f():
                    S.op("act", lambda e: e.activation(out=rl[:], in_=bank(lb), func=AF.Ln), reads=[("ps", lb)], writes=["rl"])
                    S.op("act", lambda e: e.activation(out=rl[:], in_=rl[:], func=AF.Exp, scale=-1.0), reads=["rl"], writes=["rl"])
                    S.op("dve", lambda e: e.tensor_tensor(out=mixT[:, dst_k, :], in0=bank(ob), in1=rl[:], op=ALU.mult),
                         reads=[("ps", ob), "rl"], writes=[("mix", dst_k)])
                return f

            st["banks"] = [0, 1]
            st["gb"] = 0
            for p in make_parts(0):
                p()
            offs = [44, 38, 28, 18, 8]
            for h in range(HA):
                ob, lb = next_ol()
                items = head_items(h, ob, lb)
                nsteps = sum(1 for k_, _ in items if k_ == "step")
                posts = {}
                if h + 1 < HA:
                    parts = make_parts(h + 1)
                    for k_, p in enumerate(parts):
                        pos = max(k_, nsteps - offs[k_])
                        posts.setdefault(pos, []).append(p)
                posts.setdefault(nsteps - 1, []).append(fin_ops(ob, lb, h))
                si = 0
                for kind, item in items:
                    pend.append((kind, item))
                    if kind == "step":
                        for p in posts.get(si, []):
                            pend.append(("post", p))
                        si += 1
            attn_flush()
            st["banks"] = [0, 1, 2, 3]
            st["gb"] = 0
            cbase = 3 * AW
            if j == 0:
                for k in range(KC):
                    src = ha_all[l].rearrange("(c r) s -> r c s", c=NCORE)[k * 128:(k + 1) * 128, :, :]
                    S.op("sp", lambda e, k=k, src=src: e.dma_start(out=hh[:, k, :].rearrange("p (c s) -> p c s", c=NCORE), in_=src),
                         reads=[("ha_all", l)], writes=[("hh", k)], dma=True)
            for g in range(CG):
                s_c, wc = load_w(w_in[l], 0, KC, cbase + CC + g * 128, 128)
                s_x, wx = load_w(w_in[l], 0, KC, cbase + 2 * CC + g * 128, 128)
                if j == 0:
                    p1 = next_bank()
                    p2 = next_bank()
                    for (pp, s__, ww) in ((p1, s_c, wc), (p2, s_x, wx)):
                        def fn(e, pp=pp, ww=ww):
                            ins = None
                            for k in range(KC):
                                ins = e.matmul(bank(pp)[0:E2, 0:128], lhsT=hh[:, k, :], rhs=ww[:, k, 0:128],
                                               start=(k == 0), stop=(k == KC - 1))
                            return ins
                        S.op("pe", fn, reads=[("wr", s__)] + [("hh", k) for k in range(KC)], writes=[("ps", pp)])
                    S.op("act", lambda e, g=g, p1=p1: e.activation(out=cch[0:E2, g * 128:(g + 1) * 128], in_=bank(p1)[0:E2, 0:128], func=AF.Copy),
                         reads=[("ps", p1)], writes=[("cch", g)])
                    S.op("dve", lambda e, g=g, p2=p2: e.tensor_tensor(out=uh[0:E2, g * 128:(g + 1) * 128], in0=cch[0:E2, g * 128:(g + 1) * 128],
                                                                      in1=bank(p2)[0:E2, 0:128], op=ALU.mult),
                         reads=[("ps", p2), ("cch", g)], writes=[("uh", g)])
                pc = gemm_tile(s_c, wc, 0, hT_ch, hT_keys, 512)
                px = gemm_tile(s_x, wx, 0, hT_ch, hT_keys, 512)
                S.op("act", lambda e, pc=pc: e.activation(out=cct[:], in_=bank(pc), func=AF.Copy), reads=[("ps", pc)], writes=["cct"])
                S.op("dve", lambda e, px=px: e.tensor_tensor(out=uext[:, :, 2:258], in0=cct[:].rearrange("p (a b) -> p a b", a=2),
                                                             in1=bank(px).rearrange("p (a b) -> p a b", a=2), op=ALU.mult),
                     reads=[("ps", px), "cct"], writes=["uext_m"])
                pp = next_bank()
                S.op("pe", lambda e, g=g, pp=pp: e.matmul(bank(pp)[:, 0:4], lhsT=uh[0:E2, g * 128:(g + 1) * 128],
                                                          rhs=selh_b[0:E2, j * 4:(j + 1) * 4], start=True, stop=True),
                     reads=[("uh", g), "selh_b"], writes=[("ps", pp)])
                S.op("act", lambda e, pp=pp: e.activation(out=uext[:, :, 0:2], in_=bank(pp)[:, 0:4].rearrange("p (a b) -> p a b", a=2), func=AF.Copy),
                     reads=[("ps", pp)], writes=["uext_h"])
                s_b, wb = load_w(w_in[l], 0, KC, cbase + g * 128, 128)
                pbb = gemm_tile(s_b, wb, 0, hT_ch, hT_keys, 512)
                w0 = vcol(l, 4 * KC + 4 + 0 * CG + g)
                w1 = vcol(l, 4 * KC + 4 + 1 * CG + g)
                w2 = vcol(l, 4 * KC + 4 + 2 * CG + g)
                S.op("dve", lambda e, w0=w0: e.tensor_scalar(out=yc[:], in0=uext[:, :, 0:256], scalar1=w0, scalar2=None, op0=ALU.mult),
                     reads=["uext_m", "uext_h", "vecs"], writes=["yc"])
                S.op("dve", lambda e, w1=w1: e.scalar_tensor_tensor(out=yc[:], in0=uext[:, :, 1:257], scalar=w1, in1=yc[:], op0=ALU.mult, op1=ALU.add),
                     reads=["uext_m", "uext_h", "yc", "vecs"], writes=["yc"])
                S.op("dve", lambda e, w2=w2: e.scalar_tensor_tensor(out=yc[:], in0=uext[:, :, 2:258], scalar=w2, in1=yc[:], op0=ALU.mult, op1=ALU.add),
                     reads=["uext_m", "uext_h", "yc", "vecs"], writes=["yc"])
                S.op("dve", lambda e, g=g, pbb=pbb: e.tensor_tensor(out=mixT[:, HA + g, :], in0=yc[:].rearrange("p a b -> p (a b)"),
                                                                   in1=bank(pbb), op=ALU.mult),
                     reads=[("ps", pbb), "yc"], writes=[("mix", HA + g)])
            for hm in range(HM):
                s_, wv = load_w(w_in[l], 0, KC, 3 * AW + 3 * CC + hm * 128, 128)
                pb = gemm_tile(s_, wv, 0, hT_ch, hT_keys, 512)
                head_norm(pb, vcol(l, 4 * KC + 2), qf, "qf", 512, False)
                S.op("act", lambda e: e.activation(out=qb[:], in_=qf[:], func=AF.Copy), reads=["qf"], writes=["qb"])
                ob, lb = next_ol()
                nmt = MEMT // 128
                for mt in range(nmt):
                    attn_core((kmemT[:, hm, mt * 128:(mt + 1) * 128], ("kmemT", hm)), qb[:], 512, 0, None,
                              (vmem[:, mt, hm * 128:(hm + 1) * 128], ("vmem", mt)), ["qb"], mt == 0, mt == nmt - 1, ob, lb)
                attn_finish(ob, lb, HA + CG + hm)
            for (c0, w) in slabs(0, D):
                s_, wv = load_w(w_out[l], 0, KC, c0, w)
                for n0 in range(0, w, 128):
                    n = (c0 + n0) // 128
                    pd = gemm_tile(s_, wv, n0, mix_ch, mix_keys, 512)
                    S.op("dve", lambda e, pd=pd, n=n: e.tensor_tensor(out=xt[:, n, :], in0=bank(pd), in1=xt[:, n, :], op=ALU.add),
                         reads=[("ps", pd), ("xt", n)], writes=[("xt", n)])

        def load_x(src, j):
            src_ap = src[0]
            S.op("sp", lambda e: e.dma_start(out=xt[:], in_=src_ap[:, j * 512:(j + 1) * 512].rearrange("(k p) t -> p k t", p=128)),
                 reads=[("xs", j)], writes=xt_keys, dma=True)

        def store_x(dst, j, key):
            dst_ap = dst[0]
            S.op("sp", lambda e: e.dma_start(out=dst_ap[:, j * 512:(j + 1) * 512].rearrange("(k p) t -> p k t", p=128), in_=xt[:]),
                 reads=xt_keys, writes=[(key, j)], dma=True)

        stop = getattr(c, "stop", 99)

        def program():
            if stage is None:
                for j in range(NT):
                    load_x(xT, j)
                    ffn(0, w_gu1, w_d1, 0)
                    kv_phase(0, j)
                    store_x(xs_out, j, "xs")
                for l in range(c.DEPTH):
                    gather(l)
                    mem_prep(l)
                    last = (l == c.DEPTH - 1)
                    for j in range(NT):
                        load_x(xs_in, j)
                        mixer(l, j)
                        ffn(l, w_gu2, w_d2, 3 * KC)
                        if not last:
                            ffn(l + 1, w_gu1, w_d1, 0)
                            kv_phase(l + 1, j)
                            store_x(xs_out, j, "xs")
                        else:
                            store_x(outT, j, "out")
                return [("out", j) for j in range(NT)]
            if stage == 0:
                for j in range(NT):
                    load_x(xT, j)
                    ffn(0, w_gu1, w_d1, 0)
                    kv_phase(0, j)
                    store_x(xs_out, j, "xso")
                return [("xso", j) for j in range(NT)] + [(n, 0) for n in ("kT_loc", "v_loc", "km_loc", "ha_loc")]
            l = stage - 1
            last = (l == c.DEPTH - 1)
            mem_prep(l)
            for j in range(NT):
                load_x(xs_in, j)
                mixer(l, j)
                ffn(l, w_gu2, w_d2, 3 * KC)
                if not last:
                    ffn(l + 1, w_gu1, w_d1, 0)
                    kv_phase(l + 1, j)
                    store_x(xs_out, j, "xso")
                else:
                    store_x(outT, j, "out")
            if last:
                return [("out", j) for j in range(NT)]
            return [("xso", j) for j in range(NT)] + [(n, l + 1) for n in ("kT_loc", "v_loc", "km_loc", "ha_loc")]

        fin_keys = program()
        S.finish("sp", fin_keys)

        sem_keys = S.sem_keys()
        sems = {}
        for i, k in enumerate(sem_keys):
            sems[k] = es.enter_context(nc.semaphore("s%d" % i))
        block = es.enter_context(nc.Block())
        S.emit(nc, block, sems)
    return nc, in_names, out_names


def core_blocks(c_, NT):
    return [8 * s + c_ for s in range(2 * NT)]


def block_home(b):
    return b % 8, b // 8


def host_layout(cfg, inputs):
    c = cfg
    x = np.asarray(inputs["x"], np.float32)[0]
    mem = np.asarray(inputs["mem"], np.float32)[0]
    positions = np.asarray(inputs["positions"]).astype(np.int32)[0]
    KC, CG = c.KC, c.CG
    vecs = np.zeros((128, c.NV), np.float32)

    def cols(v):
        return np.asarray(v, np.float32).reshape(-1, 128).T

    for l in range(c.DEPTH):
        b = l * c.VL
        vecs[:, b:b + KC] = cols(inputs["ffn1_norm"][l])
        vecs[:, b + KC:b + 2 * KC] = cols(inputs["mix_norm"][l])
        vecs[:, b + 2 * KC:b + 3 * KC] = cols(inputs["mem_norm"][l])
        vecs[:, b + 3 * KC:b + 4 * KC] = cols(inputs["ffn2_norm"][l])
        vecs[:, b + 4 * KC + 0] = np.asarray(inputs["q_norm"][l], np.float32)
        vecs[:, b + 4 * KC + 1] = np.asarray(inputs["k_norm"][l], np.float32)
        vecs[:, b + 4 * KC + 2] = np.asarray(inputs["mq_norm"][l], np.float32)
        vecs[:, b + 4 * KC + 3] = np.asarray(inputs["mk_norm"][l], np.float32)
        cw = np.asarray(inputs["conv_w"][l], np.float32)
        for jj in range(3):
            vecs[:, b + 4 * KC + 4 + jj * CG:b + 4 * KC + 4 + (jj + 1) * CG] = cols(cw[jj])
    invf = (np.float32(THETA) ** (-np.arange(0, HD, 2, dtype=np.float32) / np.float32(HD))).astype(np.float32)
    vecs[:, c.DEPTH * c.VL] = np.concatenate([invf, invf])
    cst = np.zeros((128, 768), np.float32)
    for i in range(64):
        cst[i + 64, i] = -1.0
        cst[i, i + 64] = 1.0
    cst[:, 128:256] = np.eye(128, dtype=np.float32)
    kk = np.arange(256)[:, None]
    qq = np.arange(256)[None, :]
    caus = np.where(kk <= qq, 0.0, NEG).astype(np.float32)
    cst[:, 256:512] = caus[0:128]
    cst[:, 512:768] = caus[128:256]
    memT = np.ascontiguousarray(mem.T)
    in_maps = []
    tok_idx = []
    for c_ in range(NCORE):
        blocks = core_blocks(c_, c.NT)
        idx = np.concatenate([np.arange(b * BLK, (b + 1) * BLK) for b in blocks])
        tok_idx.append(idx)
        qblk = np.repeat(np.asarray(blocks), BLK)
        eb = np.zeros(c.E, np.int64)
        for cc_ in range(NCORE):
            cb = core_blocks(cc_, c.NT)
            for s in range(c.SLOTS):
                eb[cc_ * c.SLOTS + s] = cb[s]
        pm = (eb[None, :] < qblk[:, None]).astype(np.float32)
        sh = np.zeros((c.E2, 4 * c.NT), np.float32)
        for j in range(c.NT):
            for half in range(2):
                b = blocks[2 * j + half]
                if b >= 1:
                    hc, hs = block_home(b - 1)
                    for i in range(2):
                        sh[hc * 2 * c.SLOTS + hs * 2 + i, j * 4 + half * 2 + i] = 1.0
        m = {
            "xT": np.ascontiguousarray(x[idx].T),
            "memT": memT,
            "pos": np.ascontiguousarray(positions[idx][None, :]),
            "vecs": vecs,
            "cst": cst,
            "pmask": pm,
            "selh": sh,
        }
        in_maps.append(m)
    return in_maps, tok_idx


_NC_CACHE = {}


def get_prog(cfg, stage):
    key = (cfg.D, cfg.SEQ, cfg.DEPTH, cfg.MEMT, stage)
    if key not in _NC_CACHE:
        c2 = Cfg(cfg.D, cfg.SEQ, cfg.DEPTH, cfg.MEMT)
        c2.stage = stage
        _NC_CACHE[key] = build(c2)
    return _NC_CACHE[key]


def run(cfg, inputs):
    in_maps, tok_idx = host_layout(cfg, inputs)
    pools = [dict(m) for m in in_maps]
    for l in range(cfg.DEPTH):
        for n in ("ffn1_w_gate_up", "ffn1_w_down", "w_in", "w_mem_kv", "w_out", "ffn2_w_gate_up", "ffn2_w_down"):
            arr = np.asarray(inputs[n], np.float32)[l]
            for p in pools:
                p[f"{n}_{l}"] = arr
    res = None
    for stage in range(cfg.DEPTH + 1):
        nc, in_names, out_names = get_prog(cfg, stage)
        maps = [{n: p[n] for n in in_names} for p in pools]
        res = run_bass_kernel_spmd(nc, maps, core_ids=list(range(NCORE)))
        if stage == cfg.DEPTH:
            break
        l = stage
        for nm in ("kT", "v", "km", "ha"):
            parts = [np.asarray(res.results[c_][f"{nm}_loc{l}"]) for c_ in range(NCORE)]
            al = np.concatenate(parts, axis=0)
            for c_ in range(NCORE):
                pools[c_][f"{nm}_loc{l}"] = parts[c_]
                pools[c_][f"{nm}_all{l}"] = al
        for c_ in range(NCORE):
            pools[c_]["xs_in"] = np.asarray(res.results[c_]["xs_out"])
    out = np.zeros((1, cfg.SEQ, cfg.D), np.float32)
    for c_ in range(NCORE):
        out[0, tok_idx[c_]] = np.asarray(res.results[c_]["outT"]).T
    return out


def kernel(**inputs):
    cfg = Cfg()
    return run(cfg, inputs)
```

```python
import contextlib
import numpy as np
import concourse.bass as bass
import concourse.mybir as mybir
from concourse.bass_utils import run_bass_kernel_spmd

F32 = mybir.dt.float32
BF16 = mybir.dt.bfloat16
I32 = mybir.dt.int32
AF = mybir.ActivationFunctionType
ALU = mybir.AluOpType
AX = mybir.AxisListType

NCORE = 8
HD = 128
BLK = 256
TOPK = 3
EPS = 1e-6
THETA = 10000.0
NEG = -30000.0
BIGNEG = -1.0e30


class Cfg:
    def __init__(self, D=2048, SEQ=16384, DEPTH=2, MEMT=256):
        self.D = D
        self.SEQ = SEQ
        self.DEPTH = DEPTH
        self.MEMT = MEMT
        self.DFF = 11 * D // 4
        self.KC = D // 128
        self.FC = self.DFF // 128
        self.AW = D // 2
        self.HA = self.AW // 128
        self.CC = D // 4
        self.CG = self.CC // 128
        self.MW = D // 4
        self.HM = self.MW // 128
        self.INW = 3 * self.AW + 3 * self.CC + self.MW
        self.TPC = SEQ // NCORE
        self.NT = self.TPC // 512
        self.SLOTS = 2 * self.NT
        self.E = NCORE * self.SLOTS
        self.E2 = 2 * self.E
        self.GF = 11
        self.NG = self.FC // self.GF
        self.VL = 4 * self.KC + 4 + 3 * self.CG
        self.NV = self.DEPTH * self.VL + 1


class Sched:
    ENG = ("pe", "act", "dve", "pool", "sp")
    NDMA = 12

    def __init__(self):
        self.ops = {e: [] for e in self.ENG}
        self.cnt = {e: 0 for e in self.ENG}
        self.lastw = {}
        self.readers = {}
        self.waited = {e: {} for e in self.ENG}
        self.dma_rr = {e: 0 for e in self.ENG}
        self.dma_val = {}
        self.cc_rr = 0
        self.final = []

    def _deps(self, eng, reads, writes, is_dma):
        deps = []
        for r in reads:
            t = self.lastw.get(r)
            if t is not None:
                if not (t[2] == eng and eng == "pe" and not is_dma and not t[3]):
                    deps.append(t)
        for w in writes:
            t = self.lastw.get(w)
            if t is not None and (is_dma or t[3] or t[2] != eng):
                deps.append(t)
            for t in self.readers.get(w, ()):
                if is_dma or t[3] or t[2] != eng:
                    deps.append(t)
        return deps

    def op(self, eng, fn, reads=(), writes=(), dma=False, cc=False):
        deps = self._deps(eng, reads, writes, dma)
        if cc:
            deps.extend((k, v, k[1], True) for k, v in self.dma_val.items())
        if dma:
            if cc:
                i = self.cc_rr
                self.cc_rr = (i + 1) % 4
                key = ("cc", eng, i)
            else:
                i = self.dma_rr[eng]
                self.dma_rr[eng] = (i + 1) % self.NDMA
                key = ("dma", eng, i)
            prev = self.dma_val.get(key, 0)
            if prev:
                deps.append((key, prev, eng, True))
            inc = 0 if cc else 16
            val = prev + (1 if cc else 16)
            self.dma_val[key] = val
            tok = (key, val, eng, True)
        else:
            self.cnt[eng] += 1
            key = ("eng", eng)
            tok = (key, self.cnt[eng], eng, False)
            inc = 1
        need = {}
        for (k, v, _, _) in deps:
            if v > need.get(k, 0):
                need[k] = v
        waits = []
        for k, v in need.items():
            if self.waited[eng].get(k, 0) < v:
                self.waited[eng][k] = v
                waits.append((k, v))
        for r in reads:
            self.readers.setdefault(r, []).append(tok)
        for w in writes:
            self.lastw[w] = tok
            self.readers[w] = []
        self.ops[eng].append((waits, fn, key, inc))
        return tok

    def finish(self, eng, keys):
        need = dict(self.dma_val)
        self.final = (eng, list(need.items()))

    def emit(self, nc, block_ctx, sems):
        engmap = {"pe": "tensor", "act": "scalar", "dve": "vector", "pool": "gpsimd", "sp": "sync"}
        for eng in self.ENG:
            ops = self.ops[eng]
            fin = self.final[1] if self.final and self.final[0] == eng else []
            if not ops and not fin:
                continue

            def body(e, ops=ops, fin=fin):
                for waits, fn, key, inc in ops:
                    for k, v in waits:
                        e.wait_ge(sems[k], v)
                    ins = fn(e)
                    if inc == 0:
                        ins.then_inc(sems[key])
                    else:
                        ins.then_inc(sems[key], inc)
                for k, v in fin:
                    e.wait_ge(sems[k], v)

            getattr(block_ctx, engmap[eng])(body)

    def sem_keys(self):
        keys = [("eng", e) for e in self.ENG]
        for e in self.ENG:
            for i in range(self.NDMA):
                keys.append(("dma", e, i))
        for i in range(4):
            keys.append(("cc", "pool", i))
        return keys


def build(cfg):
    c = cfg
    D, KC, FC, HA, CG, HM, NT, TPC, E, E2, SL = c.D, c.KC, c.FC, c.HA, c.CG, c.HM, c.NT, c.TPC, c.E, c.E2, c.SLOTS
    AW, CC, MW, DFF, MEMT, GF, NG = c.AW, c.CC, c.MW, c.DFF, c.MEMT, c.GF, c.NG
    SCALE = HD ** -0.5
    nc = bass.Bass("TRN2", target_bir_lowering=False)
    S = Sched()

    stage = getattr(c, "stage", None)
    in_names = []
    out_names = []
    decl = {}

    def din(name, shape, dt=F32):
        if name not in decl:
            decl[name] = nc.dram_tensor(name, list(shape), dt, kind="ExternalInput").ap()
            in_names.append(name)
        return decl[name]

    def dout(name, shape, dt=F32):
        if name not in decl:
            decl[name] = nc.dram_tensor(name, list(shape), dt, kind="ExternalOutput").ap()
            out_names.append(name)
        return decl[name]

    def dscr(name, shape, dt):
        if name not in decl:
            decl[name] = nc.dram_tensor(name, list(shape), dt, kind="Internal").ap()
        return decl[name]

    class Lazy:
        def __init__(self, fn):
            self.fn = fn

        def __getitem__(self, l):
            return self.fn(l)

    xT = Lazy(lambda _: din("xT", [D, TPC]))
    memT = din("memT", [D, MEMT])
    pos = din("pos", [1, TPC], I32)
    vecs = din("vecs", [128, c.NV])
    cst = din("cst", [128, 768])
    pmask = din("pmask", [TPC, E])
    selh = din("selh", [E2, 4 * NT])
    w_gu1 = Lazy(lambda l: din(f"ffn1_w_gate_up_{l}", [D, 2 * DFF]))
    w_d1 = Lazy(lambda l: din(f"ffn1_w_down_{l}", [DFF, D]))
    w_in = Lazy(lambda l: din(f"w_in_{l}", [D, c.INW]))
    w_mkv = Lazy(lambda l: din(f"w_mem_kv_{l}", [D, 2 * MW]))
    w_out = Lazy(lambda l: din(f"w_out_{l}", [D, D]))
    w_gu2 = Lazy(lambda l: din(f"ffn2_w_gate_up_{l}", [D, 2 * DFF]))
    w_d2 = Lazy(lambda l: din(f"ffn2_w_down_{l}", [DFF, D]))
    outT = Lazy(lambda _: dout("outT", [D, TPC]))

    def locbuf(nm, shape, dt):
        def fn(l):
            if stage is None:
                return dscr(f"{nm}_loc{l}", shape, dt)
            if stage == l:
                return dout(f"{nm}_loc{l}", shape, dt)
            return din(f"{nm}_loc{l}", shape, dt)
        return Lazy(fn)

    def allbuf(nm, shape, dt):
        def fn(l):
            shp = [NCORE * shape[0], shape[1]]
            if stage is None:
                return dscr(f"{nm}_all{l}", shp, dt)
            return din(f"{nm}_all{l}", shp, dt)
        return Lazy(fn)

    kT_loc = locbuf("kT", [HA * 128, TPC], BF16)
    kT_all = allbuf("kT", [HA * 128, TPC], BF16)
    v_loc = locbuf("v", [HA * TPC, 128], BF16)
    v_all = allbuf("v", [HA * TPC, 128], BF16)
    km_loc = locbuf("km", [HA * 128, SL], F32)
    km_all = allbuf("km", [HA * 128, SL], F32)
    ha_loc = locbuf("ha", [D, 2 * SL], BF16)
    ha_all = allbuf("ha", [D, 2 * SL], BF16)
    if stage is None:
        xs_in = xs_out = Lazy(lambda _: dscr("xs", [D, TPC], F32))
    else:
        xs_in = Lazy(lambda _: din("xs_in", [D, TPC]))
        xs_out = Lazy(lambda _: dout("xs_out", [D, TPC]))

    es = contextlib.ExitStack()
    with es:
        def sb(name, shape, dt):
            return es.enter_context(nc.sbuf_tensor(name, list(shape), dt))

        NWS = 4
        xt = sb("xt", [128, KC, 512], F32)
        hT = sb("hT", [128, KC, 512], BF16)
        aT = sb("aT", [128, GF, 512], BF16)
        wr = sb("wr", [128, NWS, max(KC, GF) * 256], BF16)
        mixT = sb("mixT", [128, KC, 512], BF16)
        vecs_sb = sb("vecs_sb", [128, c.NV], F32)
        cst_sb = sb("cst_sb", [128, 768], F32)
        RT = cst_sb[:, 0:128]
        ident_bf = sb("ident_bf", [128, 128], BF16)
        caus_bf = sb("caus_bf", [128, 2, 256], BF16)
        ones_d = sb("ones_d", [128, 128], BF16)
        ones_h = sb("ones_h", [128, 128], BF16)
        ones_1 = sb("ones_1", [128, 128], BF16)
        sqb = sb("sqb", [128, 2, 512], BF16)
        rstd2 = sb("rstd2", [128, 2, 512], F32)
        rstd = rstd2[:, 0, :]
        posi = sb("posi", [128, 512], I32)
        cosT = sb("cosT", [128, 512], F32)
        sinT = sb("sinT", [128, 512], F32)
        qn2 = sb("qn2", [128, 2, 512], F32)
        ql2 = sb("ql2", [128, 2, 512], BF16)
        ql = ql2[:, 0, :]
        qnh2 = sb("qnh2", [128, 2, 512], BF16)
        qnl2 = sb("qnl2", [128, 2, 512], BF16)
        qnh = qnh2[:, 0, :]
        qnl = qnl2[:, 0, :]
        rt_bf = sb("rt_bf", [128, 128], BF16)
        kmh = sb("kmh", [128, HA, E], BF16)
        kml = sb("kml", [128, HA, E], BF16)
        oneh = sb("oneh", [128, E * 128], BF16)
        t1 = sb("t1", [128, 512], F32)
        t2 = sb("t2", [128, 512], F32)
        qf2 = sb("qf2", [128, 2, 512], F32)
        qb2 = sb("qb2", [128, 2, 512], BF16)
        qf = qf2[:, 0, :]
        qb = qb2[:, 0, :]
        sg = sb("sg", [128, 2, 512], F32)
        vtmp = sb("vtmp", [128, 2, 256], BF16)
        kmt = sb("kmt", [128, HA, 2], F32)
        kmT = sb("kmT", [128, HA, E], F32)
        pm_sb = sb("pm_sb", [128, 4, E], F32)
        pmneg = sb("pmneg", [128, 4, E], F32)
        gm4 = sb("gm4", [128, 4, E], F32)
        top84 = sb("top84", [128, 4, 8], F32)
        selq4 = sb("selq4", [128, 4, E], F32)
        selqb4 = sb("selqb4", [128, 4, E], BF16)
        selbT2 = sb("selbT2", [128, 2, 512], BF16)
        selbT = selbT2[:, 0, :]
        NKR = 4
        ktr = sb("ktr", [128, NKR, 256], BF16)
        vtr = sb("vtr", [128, NKR, 256], BF16)
        NPR = 4
        pTr = sb("pTr", [128, NPR, 512], BF16)
        rl = sb("rl", [128, 512], F32)
        hh = sb("hh", [128, KC, E2], BF16)
        cch = sb("cch", [128, CC], F32)
        uh = sb("uh", [128, CC], BF16)
        selh_f = sb("selh_f", [128, 4 * NT], F32)
        selh_b = sb("selh_b", [128, 4 * NT], BF16)
        uext = sb("uext", [128, 2, 258], F32)
        cct = sb("cct", [128, 512], F32)
        ang = cct
        ang2 = rl
        yc = sb("yc", [128, 2, 256], F32)
        kmemT = sb("kmemT", [128, HM, MEMT], BF16)
        vmem = sb("vmem", [128, MEMT // 128, MW], BF16)
        ps = es.enter_context(nc.psum_tensor("ps", [128, 8 * 512], F32))

        def bank(i):
            return ps[:, i * 512:(i + 1) * 512]

        st = {"gb": 0, "ws": 0, "kr": 0, "pr": 0, "sq": 0, "ol": 0, "banks": [0, 1, 2]}

        def next_bank():
            lst = st["banks"]
            st["gb"] = (st["gb"] + 1) % len(lst)
            return lst[st["gb"]]

        def next_ws():
            b = st["ws"]
            st["ws"] = (b + 1) % NWS
            return b

        S.op("sp", lambda e: e.dma_start(out=vecs_sb[:], in_=vecs), writes=["vecs"], dma=True)
        S.op("sp", lambda e: e.dma_start(out=cst_sb[:], in_=cst), writes=["cst"], dma=True)
        S.op("sp", lambda e: e.dma_start(out=selh_f[0:E2, :], in_=selh), writes=["selh_f"], dma=True)
        S.op("dve", lambda e: e.tensor_copy(out=ident_bf[:], in_=cst_sb[:, 128:256]), reads=["cst"], writes=["ident"])
        S.op("dve", lambda e: e.tensor_tensor(out=rt_bf[:], in0=cst_sb[:, 0:128], in1=cst_sb[:, 0:128], op=ALU.mult), reads=["cst"], writes=["rt_bf"])
        S.op("dve", lambda e: e.tensor_copy(out=caus_bf[:, 0, :], in_=cst_sb[:, 256:512]), reads=["cst"], writes=["caus0"])
        S.op("dve", lambda e: e.tensor_copy(out=caus_bf[:, 1, :], in_=cst_sb[:, 512:768]), reads=["cst"], writes=["caus1"])
        S.op("dve", lambda e: e.tensor_copy(out=selh_b[0:E2, :], in_=selh_f[0:E2, :]), reads=["selh_f"], writes=["selh_b"])
        S.op("dve", lambda e: e.tensor_copy(out=oneh[0:E, :].rearrange("p (e m) -> p e m", m=128),
                                            in_=ident_bf[0:E, 0:E].unsqueeze(2).to_broadcast([E, E, 128])),
             reads=["ident"], writes=["oneh"])
        S.op("pool", lambda e: e.memset(ones_d[:], 1.0 / D), writes=["ones_d"])
        S.op("pool", lambda e: e.memset(ones_h[:], 1.0 / 128), writes=["ones_h"])
        S.op("pool", lambda e: e.memset(ones_1[:], 1.0), writes=["ones_1"])

        def vcol(l, off, n=1):
            b = l * c.VL + off
            return vecs_sb[:, b:b + n]

        INVF = vecs_sb[:, c.DEPTH * c.VL:c.DEPTH * c.VL + 1]

        def rmsnorm(src_chunks, src_keys, gbase_l, gbase_off, dst, dst_keys, N, ones_ap, ones_key):
            nk = len(src_chunks)
            pb = next_bank()
            for k in range(nk):
                s = st["sq"]
                st["sq"] ^= 1
                S.op("act", lambda e, k=k, s=s: e.activation(out=sqb[:, s, 0:N], in_=src_chunks[k], func=AF.Square),
                     reads=[src_keys[k]], writes=[("sqb", s)])
                S.op("pe", lambda e, k=k, s=s: e.matmul(bank(pb)[:, 0:N], lhsT=ones_ap, rhs=sqb[:, s, 0:N],
                                                       start=(k == 0), stop=(k == nk - 1)),
                     reads=[("sqb", s), ones_key], writes=[("ps", pb)])
            S.op("act", lambda e: e.activation(out=rstd[:, 0:N], in_=bank(pb)[:, 0:N], func=AF.Ln, bias=eps_col[:], scale=1.0),
                 reads=[("ps", pb), "eps_col"], writes=[("rstd", 0)])
            S.op("act", lambda e: e.activation(out=rstd[:, 0:N], in_=rstd[:, 0:N], func=AF.Exp, scale=-0.5), reads=[("rstd", 0)], writes=[("rstd", 0)])
            for k in range(nk):
                S.op("dve", lambda e, k=k: e.scalar_tensor_tensor(out=dst[k], in0=src_chunks[k],
                                                                   scalar=vcol(gbase_l, gbase_off + k),
                                                                   in1=rstd[:, 0:N], op0=ALU.mult, op1=ALU.mult),
                     reads=[src_keys[k], ("rstd", 0), "vecs"], writes=[dst_keys[k]])

        def load_w(wap, r0, nk, c0, w):
            s = next_ws()
            dst = wr[:, s, 0:nk * w].rearrange("p (k n) -> p k n", n=w)
            src = wap[r0:r0 + nk * 128, c0:c0 + w].rearrange("(k p) n -> p k n", p=128)
            S.op("pool", lambda e: e.dma_start(out=dst, in_=src), writes=[("wr", s)], dma=True)
            return s, dst

        def gemm_tile(s, wv, n0, rhs_chunks, rhs_keys, N, pb=None):
            if pb is None:
                pb = next_bank()
            nk = len(rhs_chunks)

            def fn(e):
                ins = None
                for k in range(nk):
                    ins = e.matmul(bank(pb)[:, 0:N], lhsT=wv[:, k, n0:n0 + 128], rhs=rhs_chunks[k],
                                   start=(k == 0), stop=(k == nk - 1))
                return ins

            S.op("pe", fn, reads=[("wr", s)] + list(rhs_keys), writes=[("ps", pb)])
            return pb

        def slabs(c0, width, maxw=256):
            out = []
            o = 0
            while o < width:
                w = min(maxw, width - o)
                out.append((c0 + o, w))
                o += w
            return out

        xt_ch = [xt[:, k, :] for k in range(KC)]
        xt_keys = [("xt", k) for k in range(KC)]
        hT_ch = [hT[:, k, :] for k in range(KC)]
        hT_keys = [("hT", k) for k in range(KC)]
        mix_ch = [mixT[:, k, :] for k in range(KC)]
        mix_keys = [("mix", k) for k in range(KC)]

        def ffn(l, w_gu, w_d, goff, on_final=None):
            rmsnorm(xt_ch, xt_keys, l, goff, hT_ch, hT_keys, 512, ones_d[:], "ones_d")
            for g in range(NG):
                f0 = g * GF
                for (c0, w) in slabs(f0 * 128, GF * 128):
                    sg_, wg = load_w(w_gu[l], 0, KC, c0, w)
                    su_, wu = load_w(w_gu[l], 0, KC, DFF + c0, w)
                    for n0 in range(0, w, 128):
                        i = (c0 + n0) // 128 - f0
                        pg = gemm_tile(sg_, wg, n0, hT_ch, hT_keys, 512)
                        pu = gemm_tile(su_, wu, n0, hT_ch, hT_keys, 512)
                        sgs = i % 2
                        S.op("act", lambda e, pg=pg, sgs=sgs: e.activation(out=sg[:, sgs, :], in_=bank(pg), func=AF.Silu),
                             reads=[("ps", pg)], writes=[("sg", sgs)])
                        S.op("dve", lambda e, pu=pu, sgs=sgs, i=i: e.tensor_tensor(out=aT[:, i, :], in0=sg[:, sgs, :],
                                                                                    in1=bank(pu), op=ALU.mult),
                             reads=[("ps", pu), ("sg", sgs)], writes=[("aT", i)])
                a_ch = [aT[:, i, :] for i in range(GF)]
                a_keys = [("aT", i) for i in range(GF)]
                for (c0, w) in slabs(0, D):
                    sd_, wd = load_w(w_d[l], f0 * 128, GF, c0, w)
                    for n0 in range(0, w, 128):
                        n = (c0 + n0) // 128
                        pd = gemm_tile(sd_, wd, n0, a_ch, a_keys, 512)
                        S.op("dve", lambda e, pd=pd, n=n: e.scalar_tensor_tensor(out=xt[:, n, :], in0=bank(pd), scalar=0.5,
                                                                                  in1=xt[:, n, :], op0=ALU.mult, op1=ALU.add),
                             reads=[("ps", pd), ("xt", n)], writes=[("xt", n)])
                        if on_final is not None and g == NG - 1:
                            on_final(n)

        def rope_tables(j):
            src = pos[0:1, j * 512:(j + 1) * 512].partition_broadcast(128)
            S.op("sp", lambda e: e.dma_start(out=posi[:], in_=src), writes=["posi"], dma=True)
            S.op("dve", lambda e: e.tensor_copy(out=ang[:], in_=posi[:]), reads=["posi"], writes=["cct"])
            S.op("dve", lambda e: e.tensor_scalar(out=ang[:], in0=ang[:], scalar1=INVF, scalar2=None, op0=ALU.mult),
                 reads=["cct", "vecs"], writes=["cct"])
            TWO_PI = float(2 * np.pi)

            def reduce_angle(dst, dkey, shift):
                S.op("dve", lambda e: e.tensor_scalar(out=dst[:], in0=ang[:], scalar1=shift, scalar2=None, op0=ALU.add),
                     reads=["cct"], writes=[dkey])
                S.op("dve", lambda e: e.tensor_scalar(out=t1[:], in0=dst[:], scalar1=1.0 / TWO_PI, scalar2=None, op0=ALU.mult),
                     reads=[dkey], writes=["t1"])
                S.op("dve", lambda e: e.tensor_copy(out=posi[:], in_=t1[:]), reads=["t1"], writes=["posi"])
                S.op("dve", lambda e: e.tensor_copy(out=t1[:], in_=posi[:]), reads=["posi"], writes=["t1"])
                S.op("dve", lambda e: e.scalar_tensor_tensor(out=dst[:], in0=t1[:], scalar=-TWO_PI, in1=dst[:], op0=ALU.mult, op1=ALU.add),
                     reads=["t1", dkey], writes=[dkey])
                S.op("dve", lambda e: e.tensor_scalar(out=t1[:], in0=dst[:], scalar1=float(np.pi), scalar2=None, op0=ALU.is_gt),
                     reads=[dkey], writes=["t1"])
                S.op("dve", lambda e: e.scalar_tensor_tensor(out=dst[:], in0=t1[:], scalar=-TWO_PI, in1=dst[:], op0=ALU.mult, op1=ALU.add),
                     reads=["t1", dkey], writes=[dkey])
                S.op("dve", lambda e: e.tensor_scalar(out=t1[:], in0=dst[:], scalar1=-float(np.pi), scalar2=None, op0=ALU.is_lt),
                     reads=[dkey], writes=["t1"])
                S.op("dve", lambda e: e.scalar_tensor_tensor(out=dst[:], in0=t1[:], scalar=TWO_PI, in1=dst[:], op0=ALU.mult, op1=ALU.add),
                     reads=["t1", dkey], writes=[dkey])

            reduce_angle(ang2, "rl", float(np.pi / 2))
            reduce_angle(t2, "t2", 0.0)
            S.op("act", lambda e: e.activation(out=sinT[:], in_=t2[:], func=AF.Sin), reads=["t2"], writes=["sinT"])
            S.op("act", lambda e: e.activation(out=cosT[:], in_=ang2[:], func=AF.Sin), reads=["rl"], writes=["cosT"])
            S.op("dve", lambda e: e.tensor_scalar(out=sinT[0:64, :], in0=sinT[0:64, :], scalar1=-1.0, scalar2=None, op0=ALU.mult),
                 reads=["sinT"], writes=["sinT"])

        eps_col = sb("eps_col", [128, 1], F32)
        S.op("pool", lambda e: e.memset(eps_col[:], EPS), writes=["eps_col"])

        def head_norm_a(pb, gcol, dst_f, dst_key, N, rope, d=0):
            s = st["sq"]
            st["sq"] ^= 1
            S.op("act", lambda e: e.activation(out=sqb[:, s, 0:N], in_=bank(pb)[:, 0:N], func=AF.Square),
                 reads=[("ps", pb)], writes=[("sqb", s)])
            p2 = next_bank()
            S.op("pe", lambda e: e.matmul(bank(p2)[:, 0:N], lhsT=ones_h[:], rhs=sqb[:, s, 0:N], start=True, stop=True),
                 reads=[("sqb", s), "ones_h"], writes=[("ps", p2)])
            rs = rstd2[:, d, :]
            S.op("act", lambda e: e.activation(out=rs[:, 0:N], in_=bank(p2)[:, 0:N], func=AF.Ln, bias=eps_col[:], scale=1.0),
                 reads=[("ps", p2), "eps_col"], writes=[("rstd", d)])
            S.op("act", lambda e: e.activation(out=rs[:, 0:N], in_=rs[:, 0:N], func=AF.Exp, scale=-0.5), reads=[("rstd", d)], writes=[("rstd", d)])
            qn_ = qn2[:, d, :]
            tgt = qn_ if rope else dst_f
            tkey = ("qn", d) if rope else dst_key
            S.op("dve", lambda e: e.scalar_tensor_tensor(out=tgt[:, 0:N], in0=bank(pb)[:, 0:N], scalar=gcol,
                                                         in1=rs[:, 0:N], op0=ALU.mult, op1=ALU.mult),
                 reads=[("ps", pb), ("rstd", d), "vecs"], writes=[tkey])
            if not rope:
                return None
            qh_ = qnh2[:, d, :]
            qo_ = qnl2[:, d, :]
            S.op("act", lambda e: e.activation(out=qh_[:, 0:N], in_=qn_[:, 0:N], func=AF.Copy), reads=[("qn", d)], writes=[("qnh", d)])
            S.op("dve", lambda e: e.tensor_tensor(out=qo_[:, 0:N], in0=qn_[:, 0:N], in1=qh_[:, 0:N], op=ALU.subtract),
                 reads=[("qn", d), ("qnh", d)], writes=[("qnl", d)])
            p3 = next_bank()

            def rfn(e):
                e.matmul(bank(p3)[:, 0:N], lhsT=rt_bf[:], rhs=qh_[:, 0:N], start=True, stop=False)
                return e.matmul(bank(p3)[:, 0:N], lhsT=rt_bf[:], rhs=qo_[:, 0:N], start=False, stop=True)

            S.op("pe", rfn, reads=[("qnh", d), ("qnl", d), "rt_bf"], writes=[("ps", p3)])
            return p3

        def head_norm_b(p3, dst_f, dst_key, N, d=0):
            qn_ = qn2[:, d, :]
            S.op("dve", lambda e: e.tensor_tensor(out=t1[:, 0:N], in0=qn_[:, 0:N], in1=cosT[:, 0:N], op=ALU.mult),
                 reads=[("qn", d), "cosT"], writes=["t1"])
            S.op("dve", lambda e: e.tensor_tensor(out=t2[:, 0:N], in0=bank(p3)[:, 0:N], in1=sinT[:, 0:N], op=ALU.mult),
                 reads=[("ps", p3), "sinT"], writes=["t2"])
            S.op("dve", lambda e: e.tensor_tensor(out=dst_f[:, 0:N], in0=t1[:, 0:N], in1=t2[:, 0:N], op=ALU.add),
                 reads=["t1", "t2"], writes=[dst_key])

        def head_norm(pb, gcol, dst_f, dst_key, N, rope, d=0):
            p3 = head_norm_a(pb, gcol, dst_f, dst_key, N, rope, d)
            if rope:
                head_norm_b(p3, dst_f, dst_key, N, d)

        def kv_phase(l, j):
            rmsnorm(xt_ch, xt_keys, l, KC, hT_ch, hT_keys, 512, ones_d[:], "ones_d")
            rope_tables(j)
            st["banks"] = [0, 1, 2, 3, 4, 5, 6, 7]
            st["gb"] = 0
            pbs = {}

            def k_gemm(h):
                s_, wv = load_w(w_in[l], 0, KC, AW + h * 128, 128)
                pbs[h] = gemm_tile(s_, wv, 0, hT_ch, hT_keys, 512)

            def k_tail(h, p3):
                d = h % 2
                kf_ = qf2[:, d, :]
                kb_ = qb2[:, d, :]
                kfk = "qf" if d == 0 else ("qf", 1)
                kbk = "qb" if d == 0 else ("qb", 1)
                head_norm_b(p3, kf_, kfk, 512, d)
                S.op("act", lambda e: e.activation(out=kb_, in_=kf_, func=AF.Copy), reads=[kfk], writes=[kbk])
                S.op("sp", lambda e: e.dma_start(out=kT_loc[l][h * 128:(h + 1) * 128, j * 512:(j + 1) * 512], in_=kb_),
                     reads=[kbk], writes=[("kT_loc", l)], dma=True)
                S.op("dve", lambda e: e.tensor_reduce(out=kmt[:, h, :], in_=kf_.rearrange("p (a b) -> p a b", a=2), axis=AX.X, op=ALU.add),
                     reads=[kfk], writes=[("kmt", h)])
                S.op("dve", lambda e: e.tensor_single_scalar(out=kmt[:, h, :], in_=kmt[:, h, :], scalar=1.0 / BLK, op=ALU.mult),
                     reads=[("kmt", h)], writes=[("kmt", h)])
                S.op("sp", lambda e: e.dma_start(out=km_loc[l][h * 128:(h + 1) * 128, 2 * j:2 * j + 2], in_=kmt[:, h, :]),
                     reads=[("kmt", h)], writes=[("km_loc", l)], dma=True)

            k_gemm(0)
            p3s = {}
            for h in range(HA):
                if h + 1 < HA:
                    k_gemm(h + 1)
                d = h % 2
                p3s[h] = head_norm_a(pbs[h], vcol(l, 4 * KC + 1), None, None, 512, True, d)
                if h >= 1:
                    k_tail(h - 1, p3s[h - 1])
            k_tail(HA - 1, p3s[HA - 1])
            for (c0, w) in slabs(2 * AW, AW):
                s_, wv = load_w(w_in[l], 0, KC, c0, w)
                for tc in range(4):
                    pb = next_bank()

                    def fn(e, tc=tc, wv=wv, pb=pb, w=w):
                        ins = None
                        for k in range(KC):
                            ins = e.matmul(bank(pb)[:, 0:w], lhsT=hT[:, k, tc * 128:(tc + 1) * 128], rhs=wv[:, k, 0:w],
                                           start=(k == 0), stop=(k == KC - 1))
                        return ins

                    S.op("pe", fn, reads=[("wr", s_)] + hT_keys, writes=[("ps", pb)])
                    vs = tc % 2
                    S.op("act", lambda e, pb=pb, vs=vs, w=w: e.activation(out=vtmp[:, vs, 0:w], in_=bank(pb)[:, 0:w], func=AF.Copy),
                         reads=[("ps", pb)], writes=[("vtmp", vs)])
                    for hh_ in range(w // 128):
                        h = (c0 - 2 * AW) // 128 + hh_
                        r0 = h * TPC + j * 512 + tc * 128
                        S.op("sp", lambda e, vs=vs, hh_=hh_, r0=r0: e.dma_start(out=v_loc[l][r0:r0 + 128, :],
                                                                                in_=vtmp[:, vs, hh_ * 128:(hh_ + 1) * 128]),
                             reads=[("vtmp", vs)], writes=[("v_loc", l)], dma=True)
            for half in range(2):
                col = half * 256 + 254
                dst = ha_loc[l].rearrange("(k p) c -> p k c", p=128)[:, :, (2 * j + half) * 2:(2 * j + half) * 2 + 2]
                S.op("sp", lambda e, col=col, dst=dst: e.dma_start(out=dst, in_=hT[:, :, col:col + 2]),
                     reads=hT_keys, writes=[("ha_loc", l)], dma=True)
            st["banks"] = [0, 1, 2]
            st["gb"] = 0

        def gather(l):
            rg = [list(range(NCORE))]
            for nm, loc, al in (("kT", kT_loc[l], kT_all[l]), ("v", v_loc[l], v_all[l]),
                                ("km", km_loc[l], km_all[l]), ("ha", ha_loc[l], ha_all[l])):
                S.op("pool", lambda e, loc=loc, al=al: e.collective_compute("AllGather", op=ALU.bypass, replica_groups=rg,
                                                                            ins=[loc.opt()], outs=[al.opt()]),
                     reads=[(nm + "_loc", l)], writes=[(nm + "_all", l)], dma=True, cc=True)

        pend = []

        def attn_pre(eng, fn, reads, writes):
            pend.append(("pre", (eng, fn, reads, writes)))

        def attn_core(kt_tiles, q_ap, N, c0, bias_fn, v_fn, keys_r, first, last, ob, lb):
            pend.append(("step", (kt_tiles, q_ap, N, c0, bias_fn, v_fn, keys_r, first, last, ob, lb)))

        def attn_flush():
            steps = []
            cur_pre = []
            posts_after = {}
            for kind, item in pend:
                if kind == "pre":
                    cur_pre.append(item)
                elif kind == "post":
                    posts_after.setdefault(len(steps) - 1, []).append(item)
                else:
                    steps.append((cur_pre, item))
                    cur_pre = []
            del pend[:]
            info = [None] * len(steps)

            def emit_qk(t):
                pre, (kt_tiles, q_ap, N, c0, bias_fn, v_fn, keys_r, first, last, ob, lb) = steps[t]
                for (eng, fn, reads, writes) in pre:
                    S.op(eng, fn, reads=reads, writes=writes, dma=True)
                pb = next_bank()
                lhsT_k, kkey = kt_tiles

                def fn1(e):
                    ins = e.matmul(bank(pb)[:, 0:N], lhsT=lhsT_k, rhs=q_ap, start=True, stop=(bias_fn is None))
                    if bias_fn is not None:
                        ins = bias_fn(e, bank(pb)[:, 0:N])
                    return ins

                S.op("pe", fn1, reads=[kkey] + keys_r, writes=[("ps", pb)])
                info[t] = pb

            def emit_rest(t):
                pre, (kt_tiles, q_ap, N, c0, bias_fn, v_fn, keys_r, first, last, ob, lb) = steps[t]
                pb = info[t]
                pr = st["pr"]
                st["pr"] = (pr + 1) % NPR
                S.op("act", lambda e: e.activation(out=pTr[:, pr, 0:N], in_=bank(pb)[:, 0:N], func=AF.Exp, scale=SCALE),
                     reads=[("ps", pb)], writes=[("pT", pr)])
                v_ap, vkey = v_fn

                def fn2(e):
                    e.matmul(bank(ob)[:, c0:c0 + N], lhsT=v_ap, rhs=pTr[:, pr, 0:N], start=first, stop=last)
                    return e.matmul(bank(lb)[:, c0:c0 + N], lhsT=ones_1[:], rhs=pTr[:, pr, 0:N], start=first, stop=last)

                S.op("pe", fn2, reads=[("pT", pr), vkey, "ones_1"], writes=[("ps", ob), ("ps", lb)])

            n = len(steps)
            if n == 0:
                return
            emit_qk(0)
            for t in range(n):
                if t + 1 < n:
                    emit_qk(t + 1)
                emit_rest(t)
                for p in posts_after.get(t, []):
                    p()

        def attn_finish(ob, lb, dst_k):
            attn_flush()
            S.op("act", lambda e: e.activation(out=rl[:], in_=bank(lb), func=AF.Ln), reads=[("ps", lb)], writes=["rl"])
            S.op("act", lambda e: e.activation(out=rl[:], in_=rl[:], func=AF.Exp, scale=-1.0), reads=["rl"], writes=["rl"])
            S.op("dve", lambda e: e.tensor_tensor(out=mixT[:, dst_k, :], in0=bank(ob), in1=rl[:], op=ALU.mult),
                 reads=[("ps", ob), "rl"], writes=[("mix", dst_k)])

        def next_ol():
            o = st["ol"]
            st["ol"] ^= 1
            return 4 + o, 6 + o

        def mem_prep(l):
            m_ch = [xt[:, k, 0:MEMT] for k in range(KC)]
            S.op("sp", lambda e: e.dma_start(out=xt[:, :, 0:MEMT], in_=memT.rearrange("(k p) t -> p k t", p=128)),
                 writes=xt_keys, dma=True)
            hm_ch = [hT[:, k, 0:MEMT] for k in range(KC)]
            hm_keys = hT_keys
            rmsnorm(m_ch, xt_keys, l, 2 * KC, hm_ch, hm_keys, MEMT, ones_d[:], "ones_d")
            for hm in range(HM):
                s_, wv = load_w(w_mkv[l], 0, KC, hm * 128, 128)
                pb = gemm_tile(s_, wv, 0, hm_ch, hm_keys, MEMT)
                head_norm(pb, vcol(l, 4 * KC + 3), qf, "qf", MEMT, False)
                S.op("act", lambda e, hm=hm: e.activation(out=kmemT[:, hm, :], in_=qf[:, 0:MEMT], func=AF.Copy),
                     reads=["qf"], writes=[("kmemT", hm)])
            for (c0, w) in slabs(MW, MW):
                s_, wv = load_w(w_mkv[l], 0, KC, c0, w)
                for mt in range(MEMT // 128):
                    pb = next_bank()

                    def fn(e, mt=mt, wv=wv, pb=pb, w=w):
                        ins = None
                        for k in range(KC):
                            ins = e.matmul(bank(pb)[:, 0:w], lhsT=hT[:, k, mt * 128:(mt + 1) * 128], rhs=wv[:, k, 0:w],
                                           start=(k == 0), stop=(k == KC - 1))
                        return ins

                    S.op("pe", fn, reads=[("wr", s_)] + hm_keys, writes=[("ps", pb)])
                    S.op("act", lambda e, mt=mt, pb=pb, w=w, c0=c0: e.activation(out=vmem[:, mt, c0 - MW:c0 - MW + w],
                                                                                in_=bank(pb)[:, 0:w], func=AF.Copy),
                         reads=[("ps", pb)], writes=[("vmem", mt)])

        def mixer(l, j):
            rmsnorm(xt_ch, xt_keys, l, KC, hT_ch, hT_keys, 512, ones_d[:], "ones_d")
            rope_tables(j)
            S.op("sp", lambda e: e.dma_start(out=pm_sb[:], in_=pmask[j * 512:(j + 1) * 512, :].rearrange("(a p) e -> p a e", p=128)),
                 writes=["pm"], dma=True)
            S.op("dve", lambda e: e.tensor_scalar(out=pmneg[:], in0=pm_sb[:], scalar1=-BIGNEG, scalar2=BIGNEG,
                                                   op0=ALU.mult, op1=ALU.add),
                 reads=["pm"], writes=["pmneg"])
            for h in range(HA):
                src = km_all[l].rearrange("(c r) s -> r c s", c=NCORE)[h * 128:(h + 1) * 128, :, :]
                S.op("sp", lambda e, h=h, src=src: e.dma_start(out=kmT[:, h, :].rearrange("p (c s) -> p c s", c=NCORE), in_=src),
                     reads=[("km_all", l)], writes=[("kmT", h)], dma=True)
                S.op("act", lambda e, h=h: e.activation(out=kmh[:, h, :], in_=kmT[:, h, :], func=AF.Copy), reads=[("kmT", h)], writes=[("kmh", h)])
                S.op("dve", lambda e, h=h: e.tensor_tensor(out=kml[:, h, :], in0=kmT[:, h, :], in1=kmh[:, h, :], op=ALU.subtract),
                     reads=[("kmT", h), ("kmh", h)], writes=[("kml", h)])
            def kq(name, d):
                return name if d == 0 else (name, d)

            def make_parts(h):
                d = h % 2
                gcol = vcol(l, 4 * KC + 0)
                qf_ = qf2[:, d, :]
                qb_ = qb2[:, d, :]
                ql_ = ql2[:, d, :]
                qn_ = qn2[:, d, :]
                rs = rstd2[:, d, :]
                sq_i = [0]

                def P1():
                    s_, wv = load_w(w_in[l], 0, KC, h * 128, 128)
                    gemm_tile(s_, wv, 0, hT_ch, hT_keys, 512, pb=3)
                    sq = st["sq"]
                    st["sq"] ^= 1
                    sq_i[0] = sq
                    S.op("act", lambda e: e.activation(out=sqb[:, sq, :], in_=bank(3), func=AF.Square),
                         reads=[("ps", 3)], writes=[("sqb", sq)])

                def P2():
                    sq = sq_i[0]
                    S.op("pe", lambda e: e.matmul(bank(2), lhsT=ones_h[:], rhs=sqb[:, sq, :], start=True, stop=True),
                         reads=[("sqb", sq), "ones_h"], writes=[("ps", 2)])
                    S.op("act", lambda e: e.activation(out=rs, in_=bank(2), func=AF.Ln, bias=eps_col[:], scale=1.0),
                         reads=[("ps", 2), "eps_col"], writes=[("rstd", d)])
                    S.op("act", lambda e: e.activation(out=rs, in_=rs, func=AF.Exp, scale=-0.5), reads=[("rstd", d)], writes=[("rstd", d)])
                    S.op("dve", lambda e: e.scalar_tensor_tensor(out=qn_, in0=bank(3), scalar=gcol, in1=rs, op0=ALU.mult, op1=ALU.mult),
                         reads=[("ps", 3), ("rstd", d), "vecs"], writes=[("qn", d)])
                    S.op("act", lambda e: e.activation(out=qnh[:], in_=qn_, func=AF.Copy), reads=[("qn", d)], writes=[("qnh", 0)])
                    S.op("dve", lambda e: e.tensor_tensor(out=qnl[:], in0=qn_, in1=qnh[:], op=ALU.subtract), reads=[("qn", d), ("qnh", 0)], writes=[("qnl", 0)])

                def P3():
                    def rfn(e):
                        e.matmul(bank(2), lhsT=rt_bf[:], rhs=qnh[:], start=True, stop=False)
                        return e.matmul(bank(2), lhsT=rt_bf[:], rhs=qnl[:], start=False, stop=True)

                    S.op("pe", rfn, reads=[("qnh", 0), ("qnl", 0), "rt_bf"], writes=[("ps", 2)])
                    S.op("dve", lambda e: e.tensor_tensor(out=t1[:], in0=qn_, in1=cosT[:], op=ALU.mult), reads=[("qn", d), "cosT"], writes=["t1"])
                    S.op("dve", lambda e: e.tensor_tensor(out=t2[:], in0=bank(2), in1=sinT[:], op=ALU.mult), reads=[("ps", 2), "sinT"], writes=["t2"])
                    S.op("dve", lambda e: e.tensor_tensor(out=qf_, in0=t1[:], in1=t2[:], op=ALU.add), reads=["t1", "t2"], writes=[kq("qf", d)])
                    S.op("act", lambda e: e.activation(out=qb_, in_=qf_, func=AF.Copy), reads=[kq("qf", d)], writes=[kq("qb", d)])
                    S.op("dve", lambda e: e.tensor_tensor(out=ql_, in0=qf_, in1=qb_, op=ALU.subtract), reads=[kq("qf", d), kq("qb", d)], writes=[("ql", d)])

                def P4():
                    def gfn(e):
                        ins = None
                        for qc in range(4):
                            o = bank(2)[:, qc * E:(qc + 1) * E]
                            e.matmul(o, lhsT=qb_[:, qc * 128:(qc + 1) * 128], rhs=kmh[:, h, :], start=True, stop=False)
                            e.matmul(o, lhsT=ql_[:, qc * 128:(qc + 1) * 128], rhs=kmh[:, h, :], start=False, stop=False)
                            ins = e.matmul(o, lhsT=qb_[:, qc * 128:(qc + 1) * 128], rhs=kml[:, h, :], start=False, stop=True)
                        return ins

                    S.op("pe", gfn, reads=[kq("qb", d), ("ql", d), ("kmh", h), ("kml", h)], writes=[("ps", 2)])
                    pg3 = bank(2)[:, 0:4 * E].rearrange("p (a e) -> p a e", a=4)
                    S.op("dve", lambda e: e.tensor_tensor(out=gm4[:], in0=pg3, in1=pm_sb[:], op=ALU.mult), reads=[("ps", 2), "pm"], writes=["gm"])
                    S.op("dve", lambda e: e.tensor_tensor(out=gm4[:], in0=gm4[:], in1=pmneg[:], op=ALU.add), reads=["gm", "pmneg"], writes=["gm"])
                    for qc in range(4):
                        S.op("dve", lambda e, qc=qc: e.max(out=top84[:, qc, :], in_=gm4[:, qc, :]), reads=["gm"], writes=[("top8", qc)])
                    for qc in range(4):
                        S.op("dve", lambda e, qc=qc: e.tensor_scalar(out=selq4[:, qc, :], in0=gm4[:, qc, :], scalar1=top84[:, qc, TOPK - 1:TOPK],
                                                                      scalar2=None, op0=ALU.is_ge),
                             reads=["gm", ("top8", qc)], writes=[("selq", qc)])
                    S.op("dve", lambda e: e.tensor_tensor(out=selq4[:], in0=selq4[:], in1=pm_sb[:], op=ALU.mult),
                         reads=[("selq", q_) for q_ in range(4)] + ["pm"], writes=["selq_all"])
                    S.op("dve", lambda e: e.tensor_scalar(out=selqb4[:], in0=selq4[:], scalar1=-NEG, scalar2=NEG, op0=ALU.mult, op1=ALU.add),
                         reads=["selq_all"], writes=["selqb"])

                def P5():
                    ptb = bank(3).bitcast(BF16)

                    def tfn(e):
                        ins = None
                        for qc in range(4):
                            ins = e.transpose(out=ptb[0:E, qc * 128:(qc + 1) * 128], in_=selqb4[:, qc, :], identity=ident_bf[:])
                        return ins

                    S.op("pe", tfn, reads=["selqb", "ident"], writes=[("ps", 3)])
                    S.op("act", lambda e: e.activation(out=selbT2[0:E, d, :], in_=ptb[0:E, 0:512], func=AF.Copy),
                         reads=[("ps", 3)], writes=[kq("selbT", d)])

                return [P1, P2, P3, P4, P5]

            def head_items(h, ob, lb):
                d = h % 2
                qb_ = qb2[:, d, :]
                sel_ = selbT2[0:E, d, :]
                items = []
                first = True
                for cc_ in range(NCORE):
                    for s2 in range(2 * (j + 1)):
                        e_idx = cc_ * SL + s2
                        kr = st["kr"]
                        st["kr"] = (kr + 1) % NKR
                        r0 = cc_ * HA * 128 + h * 128
                        items.append(("pre", ("sp", lambda e, kr=kr, r0=r0, s2=s2: e.dma_start(out=ktr[:, kr, :], in_=kT_all[l][r0:r0 + 128, s2 * 256:(s2 + 1) * 256]),
                                              [("kT_all", l)], [("ktr", kr)])))
                        v0 = cc_ * HA * TPC + h * TPC + s2 * 256
                        items.append(("pre", ("sp", lambda e, kr=kr, v0=v0: e.dma_start(out=vtr[:, kr, :].rearrange("p (t d) -> p t d", t=2),
                                                                                     in_=v_all[l][v0:v0 + 256, :].rearrange("(t p) d -> p t d", p=128)),
                                              [("v_all", l)], [("vtr", kr)])))
                        half_only = (s2 == 2 * j + 1)
                        c0_ = 256 if half_only else 0
                        n_ = 256 if half_only else 512
                        for t2_ in range(2):
                            oh = oneh[0:E, e_idx * 128:(e_idx + 1) * 128]

                            def bias_fn(e, out_ap, oh=oh, c0_=c0_, n_=n_):
                                return e.matmul(out_ap, lhsT=oh, rhs=sel_[:, c0_:c0_ + n_], start=False, stop=True)

                            items.append(("step", ((ktr[:, kr, t2_ * 128:(t2_ + 1) * 128], ("ktr", kr)), qb_[:, c0_:c0_ + n_], n_, c0_, bias_fn,
                                                   (vtr[:, kr, t2_ * 128:(t2_ + 1) * 128], ("vtr", kr)), [kq("qb", d), kq("selbT", d), "oneh"],
                                                   first, False, ob, lb)))
                            first = False
                for half in range(2):
                    kr = st["kr"]
                    st["kr"] = (kr + 1) % NKR
                    tk0 = j * 512 + half * 256
                    items.append(("pre", ("sp", lambda e, kr=kr, tk0=tk0: e.dma_start(out=ktr[:, kr, :], in_=kT_loc[l][h * 128:(h + 1) * 128, tk0:tk0 + 256]),
                                          [("kT_loc", l)], [("ktr", kr)])))
                    v0 = h * TPC + tk0
                    items.append(("pre", ("sp", lambda e, kr=kr, v0=v0: e.dma_start(out=vtr[:, kr, :].rearrange("p (t d) -> p t d", t=2),
                                                                                 in_=v_loc[l][v0:v0 + 256, :].rearrange("(t p) d -> p t d", p=128)),
                                          [("v_loc", l)], [("vtr", kr)])))
                    for t2_ in range(2):
                        def bias_fn(e, out_ap, t2_=t2_):
                            return e.matmul(out_ap, lhsT=ident_bf[:], rhs=caus_bf[:, t2_, :], start=False, stop=True)

                        items.append(("step", ((ktr[:, kr, t2_ * 128:(t2_ + 1) * 128], ("ktr", kr)), qb_[:, half * 256:(half + 1) * 256], 256,
                                               half * 256, bias_fn, (vtr[:, kr, t2_ * 128:(t2_ + 1) * 128], ("vtr", kr)),
                                               [kq("qb", d), "ident", "caus0", "caus1"], False, (half == 1 and t2_ == 1), ob, lb)))
                return items

            def fin_ops(ob, lb, dst_k):
                def f():
                    S.op("act", lambda e: e.activation(out=rl[:], in_=bank(lb), func=AF.Ln), reads=[("ps", lb)], writes=["rl"])
                    S.op("act", lambda e: e.activation(out=rl[:], in_=rl[:], func=AF.Exp, scale=-1.0), reads=["rl"], writes=["rl"])
                    S.op("dve", lambda e: e.tensor_tensor(out=mixT[:, dst_k, :], in0=bank(ob), in1=rl[:], op=ALU.mult),
                         reads=[("ps", ob), "rl"], writes=[("mix", dst_k)])
                return f

            st["banks"] = [0, 1]
            st["gb"] = 0
            for p in make_parts(0):
                p()
            offs = [44, 38, 28, 18, 8]
            for h in range(HA):
                ob, lb = next_ol()
                items = head_items(h, ob, lb)
                nsteps = sum(1 for k_, _ in items if k_ == "step")
                posts = {}
                if h + 1 < HA:
                    parts = make_parts(h + 1)
                    for k_, p in enumerate(parts):
                        pos = max(k_, nsteps - offs[k_])
                        posts.setdefault(pos, []).append(p)
                posts.setdefault(nsteps - 1, []).append(fin_ops(ob, lb, h))
                si = 0
                for kind, item in items:
                    pend.append((kind, item))
                    if kind == "step":
                        for p in posts.get(si, []):
                            pend.append(("post", p))
                        si += 1
            attn_flush()
            st["banks"] = [0, 1, 2, 3]
            st["gb"] = 0
            cbase = 3 * AW
            if j == 0:
                for k in range(KC):
                    src = ha_all[l].rearrange("(c r) s -> r c s", c=NCORE)[k * 128:(k + 1) * 128, :, :]
                    S.op("sp", lambda e, k=k, src=src: e.dma_start(out=hh[:, k, :].rearrange("p (c s) -> p c s", c=NCORE), in_=src),
                         reads=[("ha_all", l)], writes=[("hh", k)], dma=True)
            for g in range(CG):
                s_c, wc = load_w(w_in[l], 0, KC, cbase + CC + g * 128, 128)
                s_x, wx = load_w(w_in[l], 0, KC, cbase + 2 * CC + g * 128, 128)
                if j == 0:
                    p1 = next_bank()
                    p2 = next_bank()
                    for (pp, s__, ww) in ((p1, s_c, wc), (p2, s_x, wx)):
                        def fn(e, pp=pp, ww=ww):
                            ins = None
                            for k in range(KC):
                                ins = e.matmul(bank(pp)[0:E2, 0:128], lhsT=hh[:, k, :], rhs=ww[:, k, 0:128],
                                               start=(k == 0), stop=(k == KC - 1))
                            return ins
                        S.op("pe", fn, reads=[("wr", s__)] + [("hh", k) for k in range(KC)], writes=[("ps", pp)])
                    S.op("act", lambda e, g=g, p1=p1: e.activation(out=cch[0:E2, g * 128:(g + 1) * 128], in_=bank(p1)[0:E2, 0:128], func=AF.Copy),
                         reads=[("ps", p1)], writes=[("cch", g)])
                    S.op("dve", lambda e, g=g, p2=p2: e.tensor_tensor(out=uh[0:E2, g * 128:(g + 1) * 128], in0=cch[0:E2, g * 128:(g + 1) * 128],
                                                                      in1=bank(p2)[0:E2, 0:128], op=ALU.mult),
                         reads=[("ps", p2), ("cch", g)], writes=[("uh", g)])
                pc = gemm_tile(s_c, wc, 0, hT_ch, hT_keys, 512)
                px = gemm_tile(s_x, wx, 0, hT_ch, hT_keys, 512)
                S.op("act", lambda e, pc=pc: e.activation(out=cct[:], in_=bank(pc), func=AF.Copy), reads=[("ps", pc)], writes=["cct"])
                S.op("dve", lambda e, px=px: e.tensor_tensor(out=uext[:, :, 2:258], in0=cct[:].rearrange("p (a b) -> p a b", a=2),
                                                             in1=bank(px).rearrange("p (a b) -> p a b", a=2), op=ALU.mult),
                     reads=[("ps", px), "cct"], writes=["uext_m"])
                pp = next_bank()
                S.op("pe", lambda e, g=g, pp=pp: e.matmul(bank(pp)[:, 0:4], lhsT=uh[0:E2, g * 128:(g + 1) * 128],
                                                          rhs=selh_b[0:E2, j * 4:(j + 1) * 4], start=True, stop=True),
                     reads=[("uh", g), "selh_b"], writes=[("ps", pp)])
                S.op("act", lambda e, pp=pp: e.activation(out=uext[:, :, 0:2], in_=bank(pp)[:, 0:4].rearrange("p (a b) -> p a b", a=2), func=AF.Copy),
                     reads=[("ps", pp)], writes=["uext_h"])
                s_b, wb = load_w(w_in[l], 0, KC, cbase + g * 128, 128)
                pbb = gemm_tile(s_b, wb, 0, hT_ch, hT_keys, 512)
                w0 = vcol(l, 4 * KC + 4 + 0 * CG + g)
                w1 = vcol(l, 4 * KC + 4 + 1 * CG + g)
                w2 = vcol(l, 4 * KC + 4 + 2 * CG + g)
                S.op("dve", lambda e, w0=w0: e.tensor_scalar(out=yc[:], in0=uext[:, :, 0:256], scalar1=w0, scalar2=None, op0=ALU.mult),
                     reads=["uext_m", "uext_h", "vecs"], writes=["yc"])
                S.op("dve", lambda e, w1=w1: e.scalar_tensor_tensor(out=yc[:], in0=uext[:, :, 1:257], scalar=w1, in1=yc[:], op0=ALU.mult, op1=ALU.add),
                     reads=["uext_m", "uext_h", "yc", "vecs"], writes=["yc"])
                S.op("dve", lambda e, w2=w2: e.scalar_tensor_tensor(out=yc[:], in0=uext[:, :, 2:258], scalar=w2, in1=yc[:], op0=ALU.mult, op1=ALU.add),
                     reads=["uext_m", "uext_h", "yc", "vecs"], writes=["yc"])
                S.op("dve", lambda e, g=g, pbb=pbb: e.tensor_tensor(out=mixT[:, HA + g, :], in0=yc[:].rearrange("p a b -> p (a b)"),
                                                                   in1=bank(pbb), op=ALU.mult),
                     reads=[("ps", pbb), "yc"], writes=[("mix", HA + g)])
            for hm in range(HM):
                s_, wv = load_w(w_in[l], 0, KC, 3 * AW + 3 * CC + hm * 128, 128)
                pb = gemm_tile(s_, wv, 0, hT_ch, hT_keys, 512)
                head_norm(pb, vcol(l, 4 * KC + 2), qf, "qf", 512, False)
                S.op("act", lambda e: e.activation(out=qb[:], in_=qf[:], func=AF.Copy), reads=["qf"], writes=["qb"])
                ob, lb = next_ol()
                nmt = MEMT // 128
                for mt in range(nmt):
                    attn_core((kmemT[:, hm, mt * 128:(mt + 1) * 128], ("kmemT", hm)), qb[:], 512, 0, None,
                              (vmem[:, mt, hm * 128:(hm + 1) * 128], ("vmem", mt)), ["qb"], mt == 0, mt == nmt - 1, ob, lb)
                attn_finish(ob, lb, HA + CG + hm)
            for (c0, w) in slabs(0, D):
                s_, wv = load_w(w_out[l], 0, KC, c0, w)
                for n0 in range(0, w, 128):
                    n = (c0 + n0) // 128
                    pd = gemm_tile(s_, wv, n0, mix_ch, mix_keys, 512)
                    S.op("dve", lambda e, pd=pd, n=n: e.tensor_tensor(out=xt[:, n, :], in0=bank(pd), in1=xt[:, n, :], op=ALU.add),
                         reads=[("ps", pd), ("xt", n)], writes=[("xt", n)])

        XG = 4 if KC % 4 == 0 else 1
        XC = KC // XG

        def load_x(src, j):
            src_ap = src[0].rearrange("(k p) t -> p k t", p=128)
            for g_ in range(XG):
                S.op("sp", lambda e, g_=g_: e.dma_start(out=xt[:, g_ * XC:(g_ + 1) * XC, :], in_=src_ap[:, g_ * XC:(g_ + 1) * XC, j * 512:(j + 1) * 512]),
                     reads=[("xs", j, g_)], writes=xt_keys[g_ * XC:(g_ + 1) * XC], dma=True)

        def store_x_chunk(dst, j, key, g_):
            dst_ap = dst[0].rearrange("(k p) t -> p k t", p=128)
            S.op("sp", lambda e: e.dma_start(out=dst_ap[:, g_ * XC:(g_ + 1) * XC, j * 512:(j + 1) * 512], in_=xt[:, g_ * XC:(g_ + 1) * XC, :]),
                 reads=xt_keys[g_ * XC:(g_ + 1) * XC], writes=[(key, j, g_)], dma=True)

        def store_x(dst, j, key):
            for g_ in range(XG):
                store_x_chunk(dst, j, key, g_)

        def store_cb(dst, j, key):
            def cb(n):
                if (n + 1) % XC == 0:
                    store_x_chunk(dst, j, key, n // XC)
            return cb

        stop = getattr(c, "stop", 99)

        def program():
            if stage is None:
                for j in range(NT):
                    load_x(xT, j)
                    ffn(0, w_gu1, w_d1, 0)
                    kv_phase(0, j)
                    store_x(xs_out, j, "xs")
                for l in range(c.DEPTH):
                    gather(l)
                    mem_prep(l)
                    last = (l == c.DEPTH - 1)
                    for j in range(NT):
                        load_x(xs_in, j)
                        mixer(l, j)
                        ffn(l, w_gu2, w_d2, 3 * KC)
                        if not last:
                            ffn(l + 1, w_gu1, w_d1, 0)
                            kv_phase(l + 1, j)
                            store_x(xs_out, j, "xs")
                        else:
                            store_x(outT, j, "out")
                return [("out", j) for j in range(NT)]
            if stage == 0:
                for j in range(NT):
                    load_x(xT, j)
                    ffn(0, w_gu1, w_d1, 0, on_final=store_cb(xs_out, j, "xso"))
                    kv_phase(0, j)
                return [("xso", j) for j in range(NT)] + [(n, 0) for n in ("kT_loc", "v_loc", "km_loc", "ha_loc")]
            l = stage - 1
            last = (l == c.DEPTH - 1)
            mem_prep(l)
            for j in range(NT):
                load_x(xs_in, j)
                mixer(l, j)
                if not last:
                    ffn(l, w_gu2, w_d2, 3 * KC)
                    ffn(l + 1, w_gu1, w_d1, 0, on_final=store_cb(xs_out, j, "xso"))
                    kv_phase(l + 1, j)
                else:
                    ffn(l, w_gu2, w_d2, 3 * KC, on_final=store_cb(outT, j, "out"))
            if last:
                return [("out", j) for j in range(NT)]
            return [("xso", j) for j in range(NT)] + [(n, l + 1) for n in ("kT_loc", "v_loc", "km_loc", "ha_loc")]

        fin_keys = program()
        S.finish("sp", fin_keys)

        sem_keys = S.sem_keys()
        sems = {}
        for i, k in enumerate(sem_keys):
            sems[k] = es.enter_context(nc.semaphore("s%d" % i))
        block = es.enter_context(nc.Block())
        S.emit(nc, block, sems)
    return nc, in_names, out_names


def core_blocks(c_, NT):
    return [8 * s + c_ for s in range(2 * NT)]


def block_home(b):
    return b % 8, b // 8


def host_layout(cfg, inputs):
    c = cfg
    x = np.asarray(inputs["x"], np.float32)[0]
    mem = np.asarray(inputs["mem"], np.float32)[0]
    positions = np.asarray(inputs["positions"]).astype(np.int32)[0]
    KC, CG = c.KC, c.CG
    vecs = np.zeros((128, c.NV), np.float32)

    def cols(v):
        return np.asarray(v, np.float32).reshape(-1, 128).T

    for l in range(c.DEPTH):
        b = l * c.VL
        vecs[:, b:b + KC] = cols(inputs["ffn1_norm"][l])
        vecs[:, b + KC:b + 2 * KC] = cols(inputs["mix_norm"][l])
        vecs[:, b + 2 * KC:b + 3 * KC] = cols(inputs["mem_norm"][l])
        vecs[:, b + 3 * KC:b + 4 * KC] = cols(inputs["ffn2_norm"][l])
        vecs[:, b + 4 * KC + 0] = np.asarray(inputs["q_norm"][l], np.float32)
        vecs[:, b + 4 * KC + 1] = np.asarray(inputs["k_norm"][l], np.float32)
        vecs[:, b + 4 * KC + 2] = np.asarray(inputs["mq_norm"][l], np.float32)
        vecs[:, b + 4 * KC + 3] = np.asarray(inputs["mk_norm"][l], np.float32)
        cw = np.asarray(inputs["conv_w"][l], np.float32)
        for jj in range(3):
            vecs[:, b + 4 * KC + 4 + jj * CG:b + 4 * KC + 4 + (jj + 1) * CG] = cols(cw[jj])
    invf = (np.float32(THETA) ** (-np.arange(0, HD, 2, dtype=np.float32) / np.float32(HD))).astype(np.float32)
    vecs[:, c.DEPTH * c.VL] = np.concatenate([invf, invf])
    cst = np.zeros((128, 768), np.float32)
    for i in range(64):
        cst[i + 64, i] = -1.0
        cst[i, i + 64] = 1.0
    cst[:, 128:256] = np.eye(128, dtype=np.float32)
    kk = np.arange(256)[:, None]
    qq = np.arange(256)[None, :]
    caus = np.where(kk <= qq, 0.0, NEG).astype(np.float32)
    cst[:, 256:512] = caus[0:128]
    cst[:, 512:768] = caus[128:256]
    memT = np.ascontiguousarray(mem.T)
    in_maps = []
    tok_idx = []
    for c_ in range(NCORE):
        blocks = core_blocks(c_, c.NT)
        idx = np.concatenate([np.arange(b * BLK, (b + 1) * BLK) for b in blocks])
        tok_idx.append(idx)
        qblk = np.repeat(np.asarray(blocks), BLK)
        eb = np.zeros(c.E, np.int64)
        for cc_ in range(NCORE):
            cb = core_blocks(cc_, c.NT)
            for s in range(c.SLOTS):
                eb[cc_ * c.SLOTS + s] = cb[s]
        pm = (eb[None, :] < qblk[:, None]).astype(np.float32)
        sh = np.zeros((c.E2, 4 * c.NT), np.float32)
        for j in range(c.NT):
            for half in range(2):
                b = blocks[2 * j + half]
                if b >= 1:
                    hc, hs = block_home(b - 1)
                    for i in range(2):
                        sh[hc * 2 * c.SLOTS + hs * 2 + i, j * 4 + half * 2 + i] = 1.0
        m = {
            "xT": np.ascontiguousarray(x[idx].T),
            "memT": memT,
            "pos": np.ascontiguousarray(positions[idx][None, :]),
            "vecs": vecs,
            "cst": cst,
            "pmask": pm,
            "selh": sh,
        }
        in_maps.append(m)
    return in_maps, tok_idx


_NC_CACHE = {}


def get_prog(cfg, stage):
    key = (cfg.D, cfg.SEQ, cfg.DEPTH, cfg.MEMT, stage)
    if key not in _NC_CACHE:
        c2 = Cfg(cfg.D, cfg.SEQ, cfg.DEPTH, cfg.MEMT)
        c2.stage = stage
        _NC_CACHE[key] = build(c2)
    return _NC_CACHE[key]


def run(cfg, inputs):
    in_maps, tok_idx = host_layout(cfg, inputs)
    pools = [dict(m) for m in in_maps]
    for l in range(cfg.DEPTH):
        for n in ("ffn1_w_gate_up", "ffn1_w_down", "w_in", "w_mem_kv", "w_out", "ffn2_w_gate_up", "ffn2_w_down"):
            arr = np.asarray(inputs[n], np.float32)[l]
            for p in pools:
                p[f"{n}_{l}"] = arr
    res = None
    for stage in range(cfg.DEPTH + 1):
        nc, in_names, out_names = get_prog(cfg, stage)
        maps = [{n: p[n] for n in in_names} for p in pools]
        res = run_bass_kernel_spmd(nc, maps, core_ids=list(range(NCORE)))
        if stage == cfg.DEPTH:
            break
        l = stage
        for nm in ("kT", "v", "km", "ha"):
            parts = [np.asarray(res.results[c_][f"{nm}_loc{l}"]) for c_ in range(NCORE)]
            al = np.concatenate(parts, axis=0)
            for c_ in range(NCORE):
                pools[c_][f"{nm}_loc{l}"] = parts[c_]
                pools[c_][f"{nm}_all{l}"] = al
        for c_ in range(NCORE):
            pools[c_]["xs_in"] = np.asarray(res.results[c_]["xs_out"])
    out = np.zeros((1, cfg.SEQ, cfg.D), np.float32)
    for c_ in range(NCORE):
        out[0, tok_idx[c_]] = np.asarray(res.results[c_]["outT"]).T
    return out


def kernel(**inputs):
    cfg = Cfg()
    return run(cfg, inputs)
```

```python
import contextlib
import numpy as np
import concourse.bass as bass
import concourse.mybir as mybir
from concourse.bass_utils import run_bass_kernel_spmd

F32 = mybir.dt.float32
BF16 = mybir.dt.bfloat16
I32 = mybir.dt.int32
AF = mybir.ActivationFunctionType
ALU = mybir.AluOpType
AX = mybir.AxisListType

NCORE = 8
HD = 128
BLK = 256
TOPK = 3
EPS = 1e-6
THETA = 10000.0
NEG = -30000.0
BIGNEG = -1.0e30


class Cfg:
    def __init__(self, D=2048, SEQ=16384, DEPTH=2, MEMT=256):
        self.D = D
        self.SEQ = SEQ
        self.DEPTH = DEPTH
        self.MEMT = MEMT
        self.DFF = 11 * D // 4
        self.KC = D // 128
        self.FC = self.DFF // 128
        self.AW = D // 2
        self.HA = self.AW // 128
        self.CC = D // 4
        self.CG = self.CC // 128
        self.MW = D // 4
        self.HM = self.MW // 128
        self.INW = 3 * self.AW + 3 * self.CC + self.MW
        self.TPC = SEQ // NCORE
        self.NT = self.TPC // 512
        self.SLOTS = 2 * self.NT
        self.E = NCORE * self.SLOTS
        self.E2 = 2 * self.E
        self.GF = 11
        self.NG = self.FC // self.GF
        self.VL = 4 * self.KC + 4 + 3 * self.CG
        self.NV = self.DEPTH * self.VL + 1


class Sched:
    ENG = ("pe", "act", "dve", "pool", "sp")
    NDMA = 12

    def __init__(self):
        self.ops = {e: [] for e in self.ENG}
        self.cnt = {e: 0 for e in self.ENG}
        self.lastw = {}
        self.readers = {}
        self.waited = {e: {} for e in self.ENG}
        self.dma_rr = {e: 0 for e in self.ENG}
        self.dma_val = {}
        self.cc_rr = 0
        self.final = []

    def _deps(self, eng, reads, writes, is_dma):
        deps = []
        for r in reads:
            t = self.lastw.get(r)
            if t is not None:
                if not (t[2] == eng and eng == "pe" and not is_dma and not t[3]):
                    deps.append(t)
        for w in writes:
            t = self.lastw.get(w)
            if t is not None and (is_dma or t[3] or t[2] != eng):
                deps.append(t)
            for t in self.readers.get(w, ()):
                if is_dma or t[3] or t[2] != eng:
                    deps.append(t)
        return deps

    def op(self, eng, fn, reads=(), writes=(), dma=False, cc=False):
        deps = self._deps(eng, reads, writes, dma)
        if cc:
            deps.extend((k, v, k[1], True) for k, v in self.dma_val.items())
        if dma:
            if cc:
                i = self.cc_rr
                self.cc_rr = (i + 1) % 4
                key = ("cc", eng, i)
            else:
                i = self.dma_rr[eng]
                self.dma_rr[eng] = (i + 1) % self.NDMA
                key = ("dma", eng, i)
            prev = self.dma_val.get(key, 0)
            if prev:
                deps.append((key, prev, eng, True))
            inc = 0 if cc else 16
            val = prev + (1 if cc else 16)
            self.dma_val[key] = val
            tok = (key, val, eng, True)
        else:
            self.cnt[eng] += 1
            key = ("eng", eng)
            tok = (key, self.cnt[eng], eng, False)
            inc = 1
        need = {}
        for (k, v, _, _) in deps:
            if v > need.get(k, 0):
                need[k] = v
        waits = []
        for k, v in need.items():
            if self.waited[eng].get(k, 0) < v:
                self.waited[eng][k] = v
                waits.append((k, v))
        for r in reads:
            self.readers.setdefault(r, []).append(tok)
        for w in writes:
            self.lastw[w] = tok
            self.readers[w] = []
        self.ops[eng].append((waits, fn, key, inc))
        return tok

    def finish(self, eng, keys):
        need = dict(self.dma_val)
        self.final = (eng, list(need.items()))

    def emit(self, nc, block_ctx, sems):
        engmap = {"pe": "tensor", "act": "scalar", "dve": "vector", "pool": "gpsimd", "sp": "sync"}
        for eng in self.ENG:
            ops = self.ops[eng]
            fin = self.final[1] if self.final and self.final[0] == eng else []
            if not ops and not fin:
                continue

            def body(e, ops=ops, fin=fin):
                for waits, fn, key, inc in ops:
                    for k, v in waits:
                        e.wait_ge(sems[k], v)
                    ins = fn(e)
                    if inc == 0:
                        ins.then_inc(sems[key])
                    else:
                        ins.then_inc(sems[key], inc)
                for k, v in fin:
                    e.wait_ge(sems[k], v)

            getattr(block_ctx, engmap[eng])(body)

    def sem_keys(self):
        keys = [("eng", e) for e in self.ENG]
        for e in self.ENG:
            for i in range(self.NDMA):
                keys.append(("dma", e, i))
        for i in range(4):
            keys.append(("cc", "pool", i))
        return keys


def build(cfg):
    c = cfg
    D, KC, FC, HA, CG, HM, NT, TPC, E, E2, SL = c.D, c.KC, c.FC, c.HA, c.CG, c.HM, c.NT, c.TPC, c.E, c.E2, c.SLOTS
    AW, CC, MW, DFF, MEMT, GF, NG = c.AW, c.CC, c.MW, c.DFF, c.MEMT, c.GF, c.NG
    SCALE = HD ** -0.5
    nc = bass.Bass("TRN2", target_bir_lowering=False)
    S = Sched()

    stage = getattr(c, "stage", None)
    in_names = []
    out_names = []
    decl = {}

    def din(name, shape, dt=F32):
        if name not in decl:
            decl[name] = nc.dram_tensor(name, list(shape), dt, kind="ExternalInput").ap()
            in_names.append(name)
        return decl[name]

    def dout(name, shape, dt=F32):
        if name not in decl:
            decl[name] = nc.dram_tensor(name, list(shape), dt, kind="ExternalOutput").ap()
            out_names.append(name)
        return decl[name]

    def dscr(name, shape, dt):
        if name not in decl:
            decl[name] = nc.dram_tensor(name, list(shape), dt, kind="Internal").ap()
        return decl[name]

    class Lazy:
        def __init__(self, fn):
            self.fn = fn

        def __getitem__(self, l):
            return self.fn(l)

    xT = Lazy(lambda _: din("xT", [D, TPC]))
    memT = din("memT", [D, MEMT])
    pos = din("pos", [1, TPC], I32)
    vecs = din("vecs", [128, c.NV])
    cst = din("cst", [128, 768])
    pmask = din("pmask", [TPC, E])
    selh = din("selh", [E2, 4 * NT])
    w_gu1 = Lazy(lambda l: din(f"ffn1_w_gate_up_{l}", [D, 2 * DFF]))
    w_d1 = Lazy(lambda l: din(f"ffn1_w_down_{l}", [DFF, D]))
    w_in = Lazy(lambda l: din(f"w_in_{l}", [D, c.INW]))
    w_mkv = Lazy(lambda l: din(f"w_mem_kv_{l}", [D, 2 * MW]))
    w_out = Lazy(lambda l: din(f"w_out_{l}", [D, D]))
    w_gu2 = Lazy(lambda l: din(f"ffn2_w_gate_up_{l}", [D, 2 * DFF]))
    w_d2 = Lazy(lambda l: din(f"ffn2_w_down_{l}", [DFF, D]))
    outT = Lazy(lambda _: dout("outT", [D, TPC]))

    def locbuf(nm, shape, dt):
        def fn(l):
            if stage is None:
                return dscr(f"{nm}_loc{l}", shape, dt)
            if stage == l:
                return dout(f"{nm}_loc{l}", shape, dt)
            return din(f"{nm}_loc{l}", shape, dt)
        return Lazy(fn)

    def allbuf(nm, shape, dt):
        def fn(l):
            shp = [NCORE * shape[0], shape[1]]
            if stage is None:
                return dscr(f"{nm}_all{l}", shp, dt)
            return din(f"{nm}_all{l}", shp, dt)
        return Lazy(fn)

    kT_loc = locbuf("kT", [HA * 128, TPC], BF16)
    kT_all = allbuf("kT", [HA * 128, TPC], BF16)
    v_loc = locbuf("v", [HA * TPC, 128], BF16)
    v_all = allbuf("v", [HA * TPC, 128], BF16)
    km_loc = locbuf("km", [HA * 128, SL], F32)
    km_all = allbuf("km", [HA * 128, SL], F32)
    ha_loc = locbuf("ha", [D, 2 * SL], BF16)
    ha_all = allbuf("ha", [D, 2 * SL], BF16)
    if stage is None:
        xs_in = xs_out = Lazy(lambda _: dscr("xs", [D, TPC], F32))
    else:
        xs_in = Lazy(lambda _: din("xs_in", [D, TPC]))
        xs_out = Lazy(lambda _: dout("xs_out", [D, TPC]))

    es = contextlib.ExitStack()
    with es:
        def sb(name, shape, dt):
            return es.enter_context(nc.sbuf_tensor(name, list(shape), dt))

        NWS = 4
        xt = sb("xt", [128, KC, 512], F32)
        hT = sb("hT", [128, KC, 512], BF16)
        aT = sb("aT", [128, GF, 512], BF16)
        wr = sb("wr", [128, NWS, max(KC, GF) * 256], BF16)
        mixT = sb("mixT", [128, KC, 512], BF16)
        vecs_sb = sb("vecs_sb", [128, c.NV], F32)
        cst_sb = sb("cst_sb", [128, 768], F32)
        RT = cst_sb[:, 0:128]
        ident_bf = sb("ident_bf", [128, 128], BF16)
        caus_bf = sb("caus_bf", [128, 2, 256], BF16)
        ones_d = sb("ones_d", [128, 128], BF16)
        ones_h = sb("ones_h", [128, 128], BF16)
        ones_1 = sb("ones_1", [128, 128], BF16)
        sqb = sb("sqb", [128, 2, 512], BF16)
        rstd2 = sb("rstd2", [128, 2, 512], F32)
        rstd = rstd2[:, 0, :]
        posi = sb("posi", [128, 512], I32)
        cosT = sb("cosT", [128, 512], F32)
        sinT = sb("sinT", [128, 512], F32)
        qn2 = sb("qn2", [128, 2, 512], F32)
        ql2 = sb("ql2", [128, 2, 512], BF16)
        ql = ql2[:, 0, :]
        qnh2 = sb("qnh2", [128, 2, 512], BF16)
        qnl2 = sb("qnl2", [128, 2, 512], BF16)
        qnh = qnh2[:, 0, :]
        qnl = qnl2[:, 0, :]
        rt_bf = sb("rt_bf", [128, 128], BF16)
        kmh = sb("kmh", [128, HA, E], BF16)
        kml = sb("kml", [128, HA, E], BF16)
        oneh = sb("oneh", [128, E * 128], BF16)
        t1 = sb("t1", [128, 512], F32)
        t2 = sb("t2", [128, 512], F32)
        qf2 = sb("qf2", [128, 2, 512], F32)
        qb2 = sb("qb2", [128, 2, 512], BF16)
        qf = qf2[:, 0, :]
        qb = qb2[:, 0, :]
        sg = sb("sg", [128, 2, 512], F32)
        vtmp = sb("vtmp", [128, 2, 256], BF16)
        kmt = sb("kmt", [128, HA, 2], F32)
        kmT = sb("kmT", [128, HA, E], F32)
        pm_sb = sb("pm_sb", [128, 4, E], F32)
        pmneg = sb("pmneg", [128, 4, E], F32)
        gm4 = sb("gm4", [128, 4, E], F32)
        top84 = sb("top84", [128, 4, 8], F32)
        selq4 = sb("selq4", [128, 4, E], F32)
        selqb4 = sb("selqb4", [128, 4, E], BF16)
        selbT2 = sb("selbT2", [128, 2, 512], BF16)
        selbT = selbT2[:, 0, :]
        NKR = 4
        ktr = sb("ktr", [128, NKR, 256], BF16)
        vtr = sb("vtr", [128, NKR, 256], BF16)
        NPR = 4
        pTr = sb("pTr", [128, NPR, 512], BF16)
        rl = sb("rl", [128, 512], F32)
        hh = sb("hh", [128, KC, E2], BF16)
        cch = sb("cch", [128, CC], F32)
        uh = sb("uh", [128, CC], BF16)
        selh_f = sb("selh_f", [128, 4 * NT], F32)
        selh_b = sb("selh_b", [128, 4 * NT], BF16)
        uext = sb("uext", [128, 2, 258], F32)
        cct = sb("cct", [128, 512], F32)
        ang = cct
        ang2 = rl
        yc = sb("yc", [128, 2, 256], F32)
        kmemT = sb("kmemT", [128, HM, MEMT], BF16)
        vmem = sb("vmem", [128, MEMT // 128, MW], BF16)
        ps = es.enter_context(nc.psum_tensor("ps", [128, 8 * 512], F32))

        def bank(i):
            return ps[:, i * 512:(i + 1) * 512]

        st = {"gb": 0, "ws": 0, "kr": 0, "pr": 0, "sq": 0, "ol": 0, "banks": [0, 1, 2]}

        def next_bank():
            lst = st["banks"]
            st["gb"] = (st["gb"] + 1) % len(lst)
            return lst[st["gb"]]

        def next_ws():
            b = st["ws"]
            st["ws"] = (b + 1) % NWS
            return b

        S.op("sp", lambda e: e.dma_start(out=vecs_sb[:], in_=vecs), writes=["vecs"], dma=True)
        S.op("sp", lambda e: e.dma_start(out=cst_sb[:], in_=cst), writes=["cst"], dma=True)
        S.op("sp", lambda e: e.dma_start(out=selh_f[0:E2, :], in_=selh), writes=["selh_f"], dma=True)
        S.op("dve", lambda e: e.tensor_copy(out=ident_bf[:], in_=cst_sb[:, 128:256]), reads=["cst"], writes=["ident"])
        S.op("dve", lambda e: e.tensor_tensor(out=rt_bf[:], in0=cst_sb[:, 0:128], in1=cst_sb[:, 0:128], op=ALU.mult), reads=["cst"], writes=["rt_bf"])
        S.op("dve", lambda e: e.tensor_copy(out=caus_bf[:, 0, :], in_=cst_sb[:, 256:512]), reads=["cst"], writes=["caus0"])
        S.op("dve", lambda e: e.tensor_copy(out=caus_bf[:, 1, :], in_=cst_sb[:, 512:768]), reads=["cst"], writes=["caus1"])
        S.op("dve", lambda e: e.tensor_copy(out=selh_b[0:E2, :], in_=selh_f[0:E2, :]), reads=["selh_f"], writes=["selh_b"])
        S.op("pool", lambda e: e.memset(oneh[:], 0.0), writes=["oneh"])
        S.op("pool", lambda e: e.memset(selbT2[:], 0.0), writes=["selbT", ("selbT", 1)])
        S.op("dve", lambda e: e.tensor_copy(out=oneh[0:E, :].rearrange("p (e m) -> p e m", m=128),
                                            in_=ident_bf[0:E, 0:E].unsqueeze(2).to_broadcast([E, E, 128])),
             reads=["ident", "oneh"], writes=["oneh"])
        S.op("pool", lambda e: e.memset(ones_d[:], 1.0 / D), writes=["ones_d"])
        S.op("pool", lambda e: e.memset(ones_h[:], 1.0 / 128), writes=["ones_h"])
        S.op("pool", lambda e: e.memset(ones_1[:], 1.0), writes=["ones_1"])

        def vcol(l, off, n=1):
            b = l * c.VL + off
            return vecs_sb[:, b:b + n]

        INVF = vecs_sb[:, c.DEPTH * c.VL:c.DEPTH * c.VL + 1]

        def rmsnorm(src_chunks, src_keys, gbase_l, gbase_off, dst, dst_keys, N, ones_ap, ones_key):
            nk = len(src_chunks)
            pb = next_bank()
            for k in range(nk):
                s = st["sq"]
                st["sq"] ^= 1
                S.op("act", lambda e, k=k, s=s: e.activation(out=sqb[:, s, 0:N], in_=src_chunks[k], func=AF.Square),
                     reads=[src_keys[k]], writes=[("sqb", s)])
                S.op("pe", lambda e, k=k, s=s: e.matmul(bank(pb)[:, 0:N], lhsT=ones_ap, rhs=sqb[:, s, 0:N],
                                                       start=(k == 0), stop=(k == nk - 1)),
                     reads=[("sqb", s), ones_key], writes=[("ps", pb)])
            S.op("act", lambda e: e.activation(out=rstd[:, 0:N], in_=bank(pb)[:, 0:N], func=AF.Ln, bias=eps_col[:], scale=1.0),
                 reads=[("ps", pb), "eps_col"], writes=[("rstd", 0)])
            S.op("act", lambda e: e.activation(out=rstd[:, 0:N], in_=rstd[:, 0:N], func=AF.Exp, scale=-0.5), reads=[("rstd", 0)], writes=[("rstd", 0)])
            for k in range(nk):
                S.op("dve", lambda e, k=k: e.scalar_tensor_tensor(out=dst[k], in0=src_chunks[k],
                                                                   scalar=vcol(gbase_l, gbase_off + k),
                                                                   in1=rstd[:, 0:N], op0=ALU.mult, op1=ALU.mult),
                     reads=[src_keys[k], ("rstd", 0), "vecs"], writes=[dst_keys[k]])

        def load_w(wap, r0, nk, c0, w):
            s = next_ws()
            dst = wr[:, s, 0:nk * w].rearrange("p (k n) -> p k n", n=w)
            src = wap[r0:r0 + nk * 128, c0:c0 + w].rearrange("(k p) n -> p k n", p=128)
            S.op("pool", lambda e: e.dma_start(out=dst, in_=src), writes=[("wr", s)], dma=True)
            return s, dst

        def gemm_tile(s, wv, n0, rhs_chunks, rhs_keys, N, pb=None):
            if pb is None:
                pb = next_bank()
            nk = len(rhs_chunks)

            def fn(e):
                ins = None
                for k in range(nk):
                    ins = e.matmul(bank(pb)[:, 0:N], lhsT=wv[:, k, n0:n0 + 128], rhs=rhs_chunks[k],
                                   start=(k == 0), stop=(k == nk - 1))
                return ins

            S.op("pe", fn, reads=[("wr", s)] + list(rhs_keys), writes=[("ps", pb)])
            return pb

        def slabs(c0, width, maxw=256):
            out = []
            o = 0
            while o < width:
                w = min(maxw, width - o)
                out.append((c0 + o, w))
                o += w
            return out

        xt_ch = [xt[:, k, :] for k in range(KC)]
        xt_keys = [("xt", k) for k in range(KC)]
        hT_ch = [hT[:, k, :] for k in range(KC)]
        hT_keys = [("hT", k) for k in range(KC)]
        mix_ch = [mixT[:, k, :] for k in range(KC)]
        mix_keys = [("mix", k) for k in range(KC)]

        def ffn(l, w_gu, w_d, goff):
            rmsnorm(xt_ch, xt_keys, l, goff, hT_ch, hT_keys, 512, ones_d[:], "ones_d")
            for g in range(NG):
                f0 = g * GF
                for (c0, w) in slabs(f0 * 128, GF * 128):
                    sg_, wg = load_w(w_gu[l], 0, KC, c0, w)
                    su_, wu = load_w(w_gu[l], 0, KC, DFF + c0, w)
                    for n0 in range(0, w, 128):
                        i = (c0 + n0) // 128 - f0
                        pg = gemm_tile(sg_, wg, n0, hT_ch, hT_keys, 512)
                        pu = gemm_tile(su_, wu, n0, hT_ch, hT_keys, 512)
                        sgs = i % 2
                        S.op("act", lambda e, pg=pg, sgs=sgs: e.activation(out=sg[:, sgs, :], in_=bank(pg), func=AF.Silu),
                             reads=[("ps", pg)], writes=[("sg", sgs)])
                        S.op("dve", lambda e, pu=pu, sgs=sgs, i=i: e.tensor_tensor(out=aT[:, i, :], in0=sg[:, sgs, :],
                                                                                    in1=bank(pu), op=ALU.mult),
                             reads=[("ps", pu), ("sg", sgs)], writes=[("aT", i)])
                a_ch = [aT[:, i, :] for i in range(GF)]
                a_keys = [("aT", i) for i in range(GF)]
                for (c0, w) in slabs(0, D):
                    sd_, wd = load_w(w_d[l], f0 * 128, GF, c0, w)
                    for n0 in range(0, w, 128):
                        n = (c0 + n0) // 128
                        pd = gemm_tile(sd_, wd, n0, a_ch, a_keys, 512)
                        S.op("dve", lambda e, pd=pd, n=n: e.scalar_tensor_tensor(out=xt[:, n, :], in0=bank(pd), scalar=0.5,
                                                                                  in1=xt[:, n, :], op0=ALU.mult, op1=ALU.add),
                             reads=[("ps", pd), ("xt", n)], writes=[("xt", n)])

        def rope_tables(j):
            src = pos[0:1, j * 512:(j + 1) * 512].partition_broadcast(128)
            S.op("sp", lambda e: e.dma_start(out=posi[:], in_=src), writes=["posi"], dma=True)
            S.op("dve", lambda e: e.tensor_copy(out=ang[:], in_=posi[:]), reads=["posi"], writes=["cct"])
            S.op("dve", lambda e: e.tensor_scalar(out=ang[:], in0=ang[:], scalar1=INVF, scalar2=None, op0=ALU.mult),
                 reads=["cct", "vecs"], writes=["cct"])
            TWO_PI = float(2 * np.pi)

            def reduce_angle(dst, dkey, shift):
                S.op("dve", lambda e: e.tensor_scalar(out=dst[:], in0=ang[:], scalar1=shift, scalar2=None, op0=ALU.add),
                     reads=["cct"], writes=[dkey])
                S.op("dve", lambda e: e.tensor_scalar(out=t1[:], in0=dst[:], scalar1=1.0 / TWO_PI, scalar2=None, op0=ALU.mult),
                     reads=[dkey], writes=["t1"])
                S.op("dve", lambda e: e.tensor_copy(out=posi[:], in_=t1[:]), reads=["t1"], writes=["posi"])
                S.op("dve", lambda e: e.tensor_copy(out=t1[:], in_=posi[:]), reads=["posi"], writes=["t1"])
                S.op("dve", lambda e: e.scalar_tensor_tensor(out=dst[:], in0=t1[:], scalar=-TWO_PI, in1=dst[:], op0=ALU.mult, op1=ALU.add),
                     reads=["t1", dkey], writes=[dkey])
                S.op("dve", lambda e: e.tensor_scalar(out=t1[:], in0=dst[:], scalar1=float(np.pi), scalar2=None, op0=ALU.is_gt),
                     reads=[dkey], writes=["t1"])
                S.op("dve", lambda e: e.scalar_tensor_tensor(out=dst[:], in0=t1[:], scalar=-TWO_PI, in1=dst[:], op0=ALU.mult, op1=ALU.add),
                     reads=["t1", dkey], writes=[dkey])
                S.op("dve", lambda e: e.tensor_scalar(out=t1[:], in0=dst[:], scalar1=-float(np.pi), scalar2=None, op0=ALU.is_lt),
                     reads=[dkey], writes=["t1"])
                S.op("dve", lambda e: e.scalar_tensor_tensor(out=dst[:], in0=t1[:], scalar=TWO_PI, in1=dst[:], op0=ALU.mult, op1=ALU.add),
                     reads=["t1", dkey], writes=[dkey])

            reduce_angle(ang2, "rl", float(np.pi / 2))
            reduce_angle(t2, "t2", 0.0)
            S.op("act", lambda e: e.activation(out=sinT[:], in_=t2[:], func=AF.Sin), reads=["t2"], writes=["sinT"])
            S.op("act", lambda e: e.activation(out=cosT[:], in_=ang2[:], func=AF.Sin), reads=["rl"], writes=["cosT"])
            S.op("dve", lambda e: e.tensor_scalar(out=sinT[0:64, :], in0=sinT[0:64, :], scalar1=-1.0, scalar2=None, op0=ALU.mult),
                 reads=["sinT"], writes=["sinT"])

        eps_col = sb("eps_col", [128, 1], F32)
        S.op("pool", lambda e: e.memset(eps_col[:], EPS), writes=["eps_col"])

        def head_norm_a(pb, gcol, dst_f, dst_key, N, rope, d=0):
            s = st["sq"]
            st["sq"] ^= 1
            S.op("act", lambda e: e.activation(out=sqb[:, s, 0:N], in_=bank(pb)[:, 0:N], func=AF.Square),
                 reads=[("ps", pb)], writes=[("sqb", s)])
            p2 = next_bank()
            S.op("pe", lambda e: e.matmul(bank(p2)[:, 0:N], lhsT=ones_h[:], rhs=sqb[:, s, 0:N], start=True, stop=True),
                 reads=[("sqb", s), "ones_h"], writes=[("ps", p2)])
            rs = rstd2[:, d, :]
            S.op("act", lambda e: e.activation(out=rs[:, 0:N], in_=bank(p2)[:, 0:N], func=AF.Ln, bias=eps_col[:], scale=1.0),
                 reads=[("ps", p2), "eps_col"], writes=[("rstd", d)])
            S.op("act", lambda e: e.activation(out=rs[:, 0:N], in_=rs[:, 0:N], func=AF.Exp, scale=-0.5), reads=[("rstd", d)], writes=[("rstd", d)])
            qn_ = qn2[:, d, :]
            tgt = qn_ if rope else dst_f
            tkey = ("qn", d) if rope else dst_key
            S.op("dve", lambda e: e.scalar_tensor_tensor(out=tgt[:, 0:N], in0=bank(pb)[:, 0:N], scalar=gcol,
                                                         in1=rs[:, 0:N], op0=ALU.mult, op1=ALU.mult),
                 reads=[("ps", pb), ("rstd", d), "vecs"], writes=[tkey])
            if not rope:
                return None
            qh_ = qnh2[:, d, :]
            qo_ = qnl2[:, d, :]
            S.op("act", lambda e: e.activation(out=qh_[:, 0:N], in_=qn_[:, 0:N], func=AF.Copy), reads=[("qn", d)], writes=[("qnh", d)])
            S.op("dve", lambda e: e.tensor_tensor(out=qo_[:, 0:N], in0=qn_[:, 0:N], in1=qh_[:, 0:N], op=ALU.subtract),
                 reads=[("qn", d), ("qnh", d)], writes=[("qnl", d)])
            p3 = next_bank()

            def rfn(e):
                e.matmul(bank(p3)[:, 0:N], lhsT=rt_bf[:], rhs=qh_[:, 0:N], start=True, stop=False)
                return e.matmul(bank(p3)[:, 0:N], lhsT=rt_bf[:], rhs=qo_[:, 0:N], start=False, stop=True)

            S.op("pe", rfn, reads=[("qnh", d), ("qnl", d), "rt_bf"], writes=[("ps", p3)])
            return p3

        def head_norm_b(p3, dst_f, dst_key, N, d=0):
            qn_ = qn2[:, d, :]
            S.op("dve", lambda e: e.tensor_tensor(out=t1[:, 0:N], in0=qn_[:, 0:N], in1=cosT[:, 0:N], op=ALU.mult),
                 reads=[("qn", d), "cosT"], writes=["t1"])
            S.op("dve", lambda e: e.tensor_tensor(out=t2[:, 0:N], in0=bank(p3)[:, 0:N], in1=sinT[:, 0:N], op=ALU.mult),
                 reads=[("ps", p3), "sinT"], writes=["t2"])
            S.op("dve", lambda e: e.tensor_tensor(out=dst_f[:, 0:N], in0=t1[:, 0:N], in1=t2[:, 0:N], op=ALU.add),
                 reads=["t1", "t2"], writes=[dst_key])

        def head_norm(pb, gcol, dst_f, dst_key, N, rope, d=0):
            p3 = head_norm_a(pb, gcol, dst_f, dst_key, N, rope, d)
            if rope:
                head_norm_b(p3, dst_f, dst_key, N, d)

        def kv_phase(l, j):
            rmsnorm(xt_ch, xt_keys, l, KC, hT_ch, hT_keys, 512, ones_d[:], "ones_d")
            rope_tables(j)
            st["banks"] = [0, 1, 2, 3, 4, 5, 6, 7]
            st["gb"] = 0
            pbs = {}

            def k_gemm(h):
                s_, wv = load_w(w_in[l], 0, KC, AW + h * 128, 128)
                pbs[h] = gemm_tile(s_, wv, 0, hT_ch, hT_keys, 512)

            def k_tail(h, p3):
                d = h % 2
                kf_ = qf2[:, d, :]
                kb_ = qb2[:, d, :]
                kfk = "qf" if d == 0 else ("qf", 1)
                kbk = "qb" if d == 0 else ("qb", 1)
                head_norm_b(p3, kf_, kfk, 512, d)
                S.op("act", lambda e: e.activation(out=kb_, in_=kf_, func=AF.Copy), reads=[kfk], writes=[kbk])
                S.op("sp", lambda e: e.dma_start(out=kT_loc[l][h * 128:(h + 1) * 128, j * 512:(j + 1) * 512], in_=kb_),
                     reads=[kbk], writes=[("kT_loc", l)], dma=True)
                S.op("dve", lambda e: e.tensor_reduce(out=kmt[:, h, :], in_=kf_.rearrange("p (a b) -> p a b", a=2), axis=AX.X, op=ALU.add),
                     reads=[kfk], writes=[("kmt", h)])
                S.op("dve", lambda e: e.tensor_single_scalar(out=kmt[:, h, :], in_=kmt[:, h, :], scalar=1.0 / BLK, op=ALU.mult),
                     reads=[("kmt", h)], writes=[("kmt", h)])
                S.op("sp", lambda e: e.dma_start(out=km_loc[l][h * 128:(h + 1) * 128, 2 * j:2 * j + 2], in_=kmt[:, h, :]),
                     reads=[("kmt", h)], writes=[("km_loc", l)], dma=True)

            k_gemm(0)
            p3s = {}
            for h in range(HA):
                if h + 1 < HA:
                    k_gemm(h + 1)
                d = h % 2
                p3s[h] = head_norm_a(pbs[h], vcol(l, 4 * KC + 1), None, None, 512, True, d)
                if h >= 1:
                    k_tail(h - 1, p3s[h - 1])
            k_tail(HA - 1, p3s[HA - 1])
            for (c0, w) in slabs(2 * AW, AW):
                s_, wv = load_w(w_in[l], 0, KC, c0, w)
                for tc in range(4):
                    pb = next_bank()

                    def fn(e, tc=tc, wv=wv, pb=pb, w=w):
                        ins = None
                        for k in range(KC):
                            ins = e.matmul(bank(pb)[:, 0:w], lhsT=hT[:, k, tc * 128:(tc + 1) * 128], rhs=wv[:, k, 0:w],
                                           start=(k == 0), stop=(k == KC - 1))
                        return ins

                    S.op("pe", fn, reads=[("wr", s_)] + hT_keys, writes=[("ps", pb)])
                    vs = tc % 2
                    S.op("act", lambda e, pb=pb, vs=vs, w=w: e.activation(out=vtmp[:, vs, 0:w], in_=bank(pb)[:, 0:w], func=AF.Copy),
                         reads=[("ps", pb)], writes=[("vtmp", vs)])
                    for hh_ in range(w // 128):
                        h = (c0 - 2 * AW) // 128 + hh_
                        r0 = h * TPC + j * 512 + tc * 128
                        S.op("sp", lambda e, vs=vs, hh_=hh_, r0=r0: e.dma_start(out=v_loc[l][r0:r0 + 128, :],
                                                                                in_=vtmp[:, vs, hh_ * 128:(hh_ + 1) * 128]),
                             reads=[("vtmp", vs)], writes=[("v_loc", l)], dma=True)
            for half in range(2):
                col = half * 256 + 254
                dst = ha_loc[l].rearrange("(k p) c -> p k c", p=128)[:, :, (2 * j + half) * 2:(2 * j + half) * 2 + 2]
                S.op("sp", lambda e, col=col, dst=dst: e.dma_start(out=dst, in_=hT[:, :, col:col + 2]),
                     reads=hT_keys, writes=[("ha_loc", l)], dma=True)
            st["banks"] = [0, 1, 2]
            st["gb"] = 0

        def gather(l):
            rg = [list(range(NCORE))]
            for nm, loc, al in (("kT", kT_loc[l], kT_all[l]), ("v", v_loc[l], v_all[l]),
                                ("km", km_loc[l], km_all[l]), ("ha", ha_loc[l], ha_all[l])):
                S.op("pool", lambda e, loc=loc, al=al: e.collective_compute("AllGather", op=ALU.bypass, replica_groups=rg,
                                                                            ins=[loc.opt()], outs=[al.opt()]),
                     reads=[(nm + "_loc", l)], writes=[(nm + "_all", l)], dma=True, cc=True)

        pend = []

        def attn_pre(eng, fn, reads, writes):
            pend.append(("pre", (eng, fn, reads, writes)))

        def attn_core(kt_tiles, q_ap, N, c0, bias_fn, v_fn, keys_r, first, last, ob, lb):
            pend.append(("step", (kt_tiles, q_ap, N, c0, bias_fn, v_fn, keys_r, first, last, ob, lb)))

        def attn_flush():
            steps = []
            cur_pre = []
            posts_after = {}
            for kind, item in pend:
                if kind == "pre":
                    cur_pre.append(item)
                elif kind == "post":
                    posts_after.setdefault(len(steps) - 1, []).append(item)
                else:
                    steps.append((cur_pre, item))
                    cur_pre = []
            del pend[:]
            info = [None] * len(steps)

            def emit_qk(t):
                pre, (kt_tiles, q_ap, N, c0, bias_fn, v_fn, keys_r, first, last, ob, lb) = steps[t]
                for (eng, fn, reads, writes) in pre:
                    S.op(eng, fn, reads=reads, writes=writes, dma=True)
                pb = next_bank()
                lhsT_k, kkey = kt_tiles

                def fn1(e):
                    ins = e.matmul(bank(pb)[:, 0:N], lhsT=lhsT_k, rhs=q_ap, start=True, stop=(bias_fn is None))
                    if bias_fn is not None:
                        ins = bias_fn(e, bank(pb)[:, 0:N])
                    return ins

                S.op("pe", fn1, reads=[kkey] + keys_r, writes=[("ps", pb)])
                info[t] = pb

            def emit_rest(t):
                pre, (kt_tiles, q_ap, N, c0, bias_fn, v_fn, keys_r, first, last, ob, lb) = steps[t]
                pb = info[t]
                pr = st["pr"]
                st["pr"] = (pr + 1) % NPR
                S.op("act", lambda e: e.activation(out=pTr[:, pr, 0:N], in_=bank(pb)[:, 0:N], func=AF.Exp, scale=SCALE),
                     reads=[("ps", pb)], writes=[("pT", pr)])
                v_ap, vkey = v_fn

                def fn2(e):
                    e.matmul(bank(ob)[:, c0:c0 + N], lhsT=v_ap, rhs=pTr[:, pr, 0:N], start=first, stop=last)
                    return e.matmul(bank(lb)[:, c0:c0 + N], lhsT=ones_1[:], rhs=pTr[:, pr, 0:N], start=first, stop=last)

                S.op("pe", fn2, reads=[("pT", pr), vkey, "ones_1"], writes=[("ps", ob), ("ps", lb)])

            n = len(steps)
            if n == 0:
                return
            emit_qk(0)
            for t in range(n):
                if t + 1 < n:
                    emit_qk(t + 1)
                emit_rest(t)
                for p in posts_after.get(t, []):
                    p()

        def attn_finish(ob, lb, dst_k):
            attn_flush()
            S.op("act", lambda e: e.activation(out=rl[:], in_=bank(lb), func=AF.Ln), reads=[("ps", lb)], writes=["rl"])
            S.op("act", lambda e: e.activation(out=rl[:], in_=rl[:], func=AF.Exp, scale=-1.0), reads=["rl"], writes=["rl"])
            S.op("dve", lambda e: e.tensor_tensor(out=mixT[:, dst_k, :], in0=bank(ob), in1=rl[:], op=ALU.mult),
                 reads=[("ps", ob), "rl"], writes=[("mix", dst_k)])

        def next_ol():
            o = st["ol"]
            st["ol"] ^= 1
            return 4 + o, 6 + o

        def mem_prep(l):
            m_ch = [xt[:, k, 0:MEMT] for k in range(KC)]
            S.op("sp", lambda e: e.dma_start(out=xt[:, :, 0:MEMT], in_=memT.rearrange("(k p) t -> p k t", p=128)),
                 writes=xt_keys, dma=True)
            hm_ch = [hT[:, k, 0:MEMT] for k in range(KC)]
            hm_keys = hT_keys
            rmsnorm(m_ch, xt_keys, l, 2 * KC, hm_ch, hm_keys, MEMT, ones_d[:], "ones_d")
            for hm in range(HM):
                s_, wv = load_w(w_mkv[l], 0, KC, hm * 128, 128)
                pb = gemm_tile(s_, wv, 0, hm_ch, hm_keys, MEMT)
                head_norm(pb, vcol(l, 4 * KC + 3), qf, "qf", MEMT, False)
                S.op("act", lambda e, hm=hm: e.activation(out=kmemT[:, hm, :], in_=qf[:, 0:MEMT], func=AF.Copy),
                     reads=["qf"], writes=[("kmemT", hm)])
            for (c0, w) in slabs(MW, MW):
                s_, wv = load_w(w_mkv[l], 0, KC, c0, w)
                for mt in range(MEMT // 128):
                    pb = next_bank()

                    def fn(e, mt=mt, wv=wv, pb=pb, w=w):
                        ins = None
                        for k in range(KC):
                            ins = e.matmul(bank(pb)[:, 0:w], lhsT=hT[:, k, mt * 128:(mt + 1) * 128], rhs=wv[:, k, 0:w],
                                           start=(k == 0), stop=(k == KC - 1))
                        return ins

                    S.op("pe", fn, reads=[("wr", s_)] + hm_keys, writes=[("ps", pb)])
                    S.op("act", lambda e, mt=mt, pb=pb, w=w, c0=c0: e.activation(out=vmem[:, mt, c0 - MW:c0 - MW + w],
                                                                                in_=bank(pb)[:, 0:w], func=AF.Copy),
                         reads=[("ps", pb)], writes=[("vmem", mt)])

        def mixer(l, j):
            rmsnorm(xt_ch, xt_keys, l, KC, hT_ch, hT_keys, 512, ones_d[:], "ones_d")
            rope_tables(j)
            S.op("sp", lambda e: e.dma_start(out=pm_sb[:], in_=pmask[j * 512:(j + 1) * 512, :].rearrange("(a p) e -> p a e", p=128)),
                 writes=["pm"], dma=True)
            S.op("dve", lambda e: e.tensor_scalar(out=pmneg[:], in0=pm_sb[:], scalar1=-BIGNEG, scalar2=BIGNEG,
                                                   op0=ALU.mult, op1=ALU.add),
                 reads=["pm"], writes=["pmneg"])
            for h in range(HA):
                src = km_all[l].rearrange("(c r) s -> r c s", c=NCORE)[h * 128:(h + 1) * 128, :, :]
                S.op("sp", lambda e, h=h, src=src: e.dma_start(out=kmT[:, h, :].rearrange("p (c s) -> p c s", c=NCORE), in_=src),
                     reads=[("km_all", l)], writes=[("kmT", h)], dma=True)
                S.op("act", lambda e, h=h: e.activation(out=kmh[:, h, :], in_=kmT[:, h, :], func=AF.Copy), reads=[("kmT", h)], writes=[("kmh", h)])
                S.op("dve", lambda e, h=h: e.tensor_tensor(out=kml[:, h, :], in0=kmT[:, h, :], in1=kmh[:, h, :], op=ALU.subtract),
                     reads=[("kmT", h), ("kmh", h)], writes=[("kml", h)])
            def kq(name, d):
                return name if d == 0 else (name, d)

            def make_parts(h):
                d = h % 2
                gcol = vcol(l, 4 * KC + 0)
                qf_ = qf2[:, d, :]
                qb_ = qb2[:, d, :]
                ql_ = ql2[:, d, :]
                qn_ = qn2[:, d, :]
                rs = rstd2[:, d, :]
                sq_i = [0]

                def P1():
                    s_, wv = load_w(w_in[l], 0, KC, h * 128, 128)
                    gemm_tile(s_, wv, 0, hT_ch, hT_keys, 512, pb=3)
                    sq = st["sq"]
                    st["sq"] ^= 1
                    sq_i[0] = sq
                    S.op("act", lambda e: e.activation(out=sqb[:, sq, :], in_=bank(3), func=AF.Square),
                         reads=[("ps", 3)], writes=[("sqb", sq)])

                def P2():
                    sq = sq_i[0]
                    S.op("pe", lambda e: e.matmul(bank(2), lhsT=ones_h[:], rhs=sqb[:, sq, :], start=True, stop=True),
                         reads=[("sqb", sq), "ones_h"], writes=[("ps", 2)])
                    S.op("act", lambda e: e.activation(out=rs, in_=bank(2), func=AF.Ln, bias=eps_col[:], scale=1.0),
                         reads=[("ps", 2), "eps_col"], writes=[("rstd", d)])
                    S.op("act", lambda e: e.activation(out=rs, in_=rs, func=AF.Exp, scale=-0.5), reads=[("rstd", d)], writes=[("rstd", d)])
                    S.op("dve", lambda e: e.scalar_tensor_tensor(out=qn_, in0=bank(3), scalar=gcol, in1=rs, op0=ALU.mult, op1=ALU.mult),
                         reads=[("ps", 3), ("rstd", d), "vecs"], writes=[("qn", d)])
                    S.op("act", lambda e: e.activation(out=qnh[:], in_=qn_, func=AF.Copy), reads=[("qn", d)], writes=[("qnh", 0)])
                    S.op("dve", lambda e: e.tensor_tensor(out=qnl[:], in0=qn_, in1=qnh[:], op=ALU.subtract), reads=[("qn", d), ("qnh", 0)], writes=[("qnl", 0)])

                def P3():
                    def rfn(e):
                        e.matmul(bank(2), lhsT=rt_bf[:], rhs=qnh[:], start=True, stop=False)
                        return e.matmul(bank(2), lhsT=rt_bf[:], rhs=qnl[:], start=False, stop=True)

                    S.op("pe", rfn, reads=[("qnh", 0), ("qnl", 0), "rt_bf"], writes=[("ps", 2)])
                    S.op("dve", lambda e: e.tensor_tensor(out=t1[:], in0=qn_, in1=cosT[:], op=ALU.mult), reads=[("qn", d), "cosT"], writes=["t1"])
                    S.op("dve", lambda e: e.tensor_tensor(out=t2[:], in0=bank(2), in1=sinT[:], op=ALU.mult), reads=[("ps", 2), "sinT"], writes=["t2"])
                    S.op("dve", lambda e: e.tensor_tensor(out=qf_, in0=t1[:], in1=t2[:], op=ALU.add), reads=["t1", "t2"], writes=[kq("qf", d)])
                    S.op("act", lambda e: e.activation(out=qb_, in_=qf_, func=AF.Copy), reads=[kq("qf", d)], writes=[kq("qb", d)])
                    S.op("dve", lambda e: e.tensor_tensor(out=ql_, in0=qf_, in1=qb_, op=ALU.subtract), reads=[kq("qf", d), kq("qb", d)], writes=[("ql", d)])

                def P4():
                    def gfn(e):
                        ins = None
                        for qc in range(4):
                            o = bank(2)[:, qc * E:(qc + 1) * E]
                            e.matmul(o, lhsT=qb_[:, qc * 128:(qc + 1) * 128], rhs=kmh[:, h, :], start=True, stop=False)
                            e.matmul(o, lhsT=ql_[:, qc * 128:(qc + 1) * 128], rhs=kmh[:, h, :], start=False, stop=False)
                            ins = e.matmul(o, lhsT=qb_[:, qc * 128:(qc + 1) * 128], rhs=kml[:, h, :], start=False, stop=True)
                        return ins

                    S.op("pe", gfn, reads=[kq("qb", d), ("ql", d), ("kmh", h), ("kml", h)], writes=[("ps", 2)])
                    pg3 = bank(2)[:, 0:4 * E].rearrange("p (a e) -> p a e", a=4)
                    S.op("dve", lambda e: e.tensor_tensor(out=gm4[:], in0=pg3, in1=pm_sb[:], op=ALU.mult), reads=[("ps", 2), "pm"], writes=["gm"])
                    S.op("dve", lambda e: e.tensor_tensor(out=gm4[:], in0=gm4[:], in1=pmneg[:], op=ALU.add), reads=["gm", "pmneg"], writes=["gm"])
                    for qc in range(4):
                        S.op("dve", lambda e, qc=qc: e.max(out=top84[:, qc, :], in_=gm4[:, qc, :]), reads=["gm"], writes=[("top8", qc)])
                    for qc in range(4):
                        S.op("dve", lambda e, qc=qc: e.tensor_scalar(out=selq4[:, qc, :], in0=gm4[:, qc, :], scalar1=top84[:, qc, TOPK - 1:TOPK],
                                                                      scalar2=None, op0=ALU.is_ge),
                             reads=["gm", ("top8", qc)], writes=[("selq", qc)])
                    S.op("dve", lambda e: e.tensor_tensor(out=selq4[:], in0=selq4[:], in1=pm_sb[:], op=ALU.mult),
                         reads=[("selq", q_) for q_ in range(4)] + ["pm"], writes=["selq_all"])
                    S.op("dve", lambda e: e.tensor_scalar(out=selqb4[:], in0=selq4[:], scalar1=-NEG, scalar2=NEG, op0=ALU.mult, op1=ALU.add),
                         reads=["selq_all"], writes=["selqb"])

                def P5():
                    ptb = bank(3).bitcast(BF16)

                    def tfn(e):
                        ins = None
                        for qc in range(4):
                            ins = e.transpose(out=ptb[0:E, qc * 128:(qc + 1) * 128], in_=selqb4[:, qc, :], identity=ident_bf[:])
                        return ins

                    S.op("pe", tfn, reads=["selqb", "ident"], writes=[("ps", 3)])
                    S.op("act", lambda e: e.activation(out=selbT2[0:E, d, :], in_=ptb[0:E, 0:512], func=AF.Copy),
                         reads=[("ps", 3)], writes=[kq("selbT", d)])

                return [P1, P2, P3, P4, P5]

            def head_items(h, ob, lb):
                d = h % 2
                qb_ = qb2[:, d, :]
                sel_ = selbT2[:, d, :]
                items = []
                first = True
                for cc_ in range(NCORE):
                    for s2 in range(2 * (j + 1)):
                        e_idx = cc_ * SL + s2
                        kr = st["kr"]
                        st["kr"] = (kr + 1) % NKR
                        r0 = cc_ * HA * 128 + h * 128
                        items.append(("pre", ("sp", lambda e, kr=kr, r0=r0, s2=s2: e.dma_start(out=ktr[:, kr, :], in_=kT_all[l][r0:r0 + 128, s2 * 256:(s2 + 1) * 256]),
                                              [("kT_all", l)], [("ktr", kr)])))
                        v0 = cc_ * HA * TPC + h * TPC + s2 * 256
                        items.append(("pre", ("sp", lambda e, kr=kr, v0=v0: e.dma_start(out=vtr[:, kr, :].rearrange("p (t d) -> p t d", t=2),
                                                                                     in_=v_all[l][v0:v0 + 256, :].rearrange("(t p) d -> p t d", p=128)),
                                              [("v_all", l)], [("vtr", kr)])))
                        half_only = (s2 == 2 * j + 1)
                        c0_ = 256 if half_only else 0
                        n_ = 256 if half_only else 512
                        for t2_ in range(2):
                            oh = oneh[:, e_idx * 128:(e_idx + 1) * 128]

                            def bias_fn(e, out_ap, oh=oh, c0_=c0_, n_=n_):
                                return e.matmul(out_ap, lhsT=oh, rhs=sel_[:, c0_:c0_ + n_], start=False, stop=True)

                            items.append(("step", ((ktr[:, kr, t2_ * 128:(t2_ + 1) * 128], ("ktr", kr)), qb_[:, c0_:c0_ + n_], n_, c0_, bias_fn,
                                                   (vtr[:, kr, t2_ * 128:(t2_ + 1) * 128], ("vtr", kr)), [kq("qb", d), kq("selbT", d), "oneh"],
                                                   first, False, ob, lb)))
                            first = False
                for half in range(2):
                    kr = st["kr"]
                    st["kr"] = (kr + 1) % NKR
                    tk0 = j * 512 + half * 256
                    items.append(("pre", ("sp", lambda e, kr=kr, tk0=tk0: e.dma_start(out=ktr[:, kr, :], in_=kT_loc[l][h * 128:(h + 1) * 128, tk0:tk0 + 256]),
                                          [("kT_loc", l)], [("ktr", kr)])))
                    v0 = h * TPC + tk0
                    items.append(("pre", ("sp", lambda e, kr=kr, v0=v0: e.dma_start(out=vtr[:, kr, :].rearrange("p (t d) -> p t d", t=2),
                                                                                 in_=v_loc[l][v0:v0 + 256, :].rearrange("(t p) d -> p t d", p=128)),
                                          [("v_loc", l)], [("vtr", kr)])))
                    for t2_ in range(2):
                        def bias_fn(e, out_ap, t2_=t2_):
                            return e.matmul(out_ap, lhsT=ident_bf[:], rhs=caus_bf[:, t2_, :], start=False, stop=True)

                        items.append(("step", ((ktr[:, kr, t2_ * 128:(t2_ + 1) * 128], ("ktr", kr)), qb_[:, half * 256:(half + 1) * 256], 256,
                                               half * 256, bias_fn, (vtr[:, kr, t2_ * 128:(t2_ + 1) * 128], ("vtr", kr)),
                                               [kq("qb", d), "ident", "caus0", "caus1"], False, (half == 1 and t2_ == 1), ob, lb)))
                return items

            def fin_ops(ob, lb, dst_k):
                def f():
                    S.op("act", lambda e: e.activation(out=rl[:], in_=bank(lb), func=AF.Ln), reads=[("ps", lb)], writes=["rl"])
                    S.op("act", lambda e: e.activation(out=rl[:], in_=rl[:], func=AF.Exp, scale=-1.0), reads=["rl"], writes=["rl"])
                    S.op("dve", lambda e: e.tensor_tensor(out=mixT[:, dst_k, :], in0=bank(ob), in1=rl[:], op=ALU.mult),
                         reads=[("ps", ob), "rl"], writes=[("mix", dst_k)])
                return f

            st["banks"] = [0, 1]
            st["gb"] = 0
            for p in make_parts(0):
                p()
            offs = [44, 38, 28, 18, 8]
            for h in range(HA):
                ob, lb = next_ol()
                items = head_items(h, ob, lb)
                nsteps = sum(1 for k_, _ in items if k_ == "step")
                posts = {}
                if h + 1 < HA:
                    parts = make_parts(h + 1)
                    for k_, p in enumerate(parts):
                        pos = max(k_, nsteps - offs[k_])
                        posts.setdefault(pos, []).append(p)
                posts.setdefault(nsteps - 1, []).append(fin_ops(ob, lb, h))
                si = 0
                for kind, item in items:
                    pend.append((kind, item))
                    if kind == "step":
                        for p in posts.get(si, []):
                            pend.append(("post", p))
                        si += 1
            attn_flush()
            st["banks"] = [0, 1, 2, 3]
            st["gb"] = 0
            cbase = 3 * AW
            if j == 0:
                for k in range(KC):
                    src = ha_all[l].rearrange("(c r) s -> r c s", c=NCORE)[k * 128:(k + 1) * 128, :, :]
                    S.op("sp", lambda e, k=k, src=src: e.dma_start(out=hh[:, k, :].rearrange("p (c s) -> p c s", c=NCORE), in_=src),
                         reads=[("ha_all", l)], writes=[("hh", k)], dma=True)
            for g in range(CG):
                s_c, wc = load_w(w_in[l], 0, KC, cbase + CC + g * 128, 128)
                s_x, wx = load_w(w_in[l], 0, KC, cbase + 2 * CC + g * 128, 128)
                if j == 0:
                    p1 = next_bank()
                    p2 = next_bank()
                    for (pp, s__, ww) in ((p1, s_c, wc), (p2, s_x, wx)):
                        def fn(e, pp=pp, ww=ww):
                            ins = None
                            for k in range(KC):
                                ins = e.matmul(bank(pp)[0:E2, 0:128], lhsT=hh[:, k, :], rhs=ww[:, k, 0:128],
                                               start=(k == 0), stop=(k == KC - 1))
                            return ins
                        S.op("pe", fn, reads=[("wr", s__)] + [("hh", k) for k in range(KC)], writes=[("ps", pp)])
                    S.op("act", lambda e, g=g, p1=p1: e.activation(out=cch[0:E2, g * 128:(g + 1) * 128], in_=bank(p1)[0:E2, 0:128], func=AF.Copy),
                         reads=[("ps", p1)], writes=[("cch", g)])
                    S.op("dve", lambda e, g=g, p2=p2: e.tensor_tensor(out=uh[0:E2, g * 128:(g + 1) * 128], in0=cch[0:E2, g * 128:(g + 1) * 128],
                                                                      in1=bank(p2)[0:E2, 0:128], op=ALU.mult),
                         reads=[("ps", p2), ("cch", g)], writes=[("uh", g)])
                pc = gemm_tile(s_c, wc, 0, hT_ch, hT_keys, 512)
                px = gemm_tile(s_x, wx, 0, hT_ch, hT_keys, 512)
                S.op("act", lambda e, pc=pc: e.activation(out=cct[:], in_=bank(pc), func=AF.Copy), reads=[("ps", pc)], writes=["cct"])
                S.op("dve", lambda e, px=px: e.tensor_tensor(out=uext[:, :, 2:258], in0=cct[:].rearrange("p (a b) -> p a b", a=2),
                                                             in1=bank(px).rearrange("p (a b) -> p a b", a=2), op=ALU.mult),
                     reads=[("ps", px), "cct"], writes=["uext_m"])
                pp = next_bank()
                S.op("pe", lambda e, g=g, pp=pp: e.matmul(bank(pp)[:, 0:4], lhsT=uh[0:E2, g * 128:(g + 1) * 128],
                                                          rhs=selh_b[0:E2, j * 4:(j + 1) * 4], start=True, stop=True),
                     reads=[("uh", g), "selh_b"], writes=[("ps", pp)])
                S.op("act", lambda e, pp=pp: e.activation(out=uext[:, :, 0:2], in_=bank(pp)[:, 0:4].rearrange("p (a b) -> p a b", a=2), func=AF.Copy),
                     reads=[("ps", pp)], writes=["uext_h"])
                s_b, wb = load_w(w_in[l], 0, KC, cbase + g * 128, 128)
                pbb = gemm_tile(s_b, wb, 0, hT_ch, hT_keys, 512)
                w0 = vcol(l, 4 * KC + 4 + 0 * CG + g)
                w1 = vcol(l, 4 * KC + 4 + 1 * CG + g)
                w2 = vcol(l, 4 * KC + 4 + 2 * CG + g)
                S.op("dve", lambda e, w0=w0: e.tensor_scalar(out=yc[:], in0=uext[:, :, 0:256], scalar1=w0, scalar2=None, op0=ALU.mult),
                     reads=["uext_m", "uext_h", "vecs"], writes=["yc"])
                S.op("dve", lambda e, w1=w1: e.scalar_tensor_tensor(out=yc[:], in0=uext[:, :, 1:257], scalar=w1, in1=yc[:], op0=ALU.mult, op1=ALU.add),
                     reads=["uext_m", "uext_h", "yc", "vecs"], writes=["yc"])
                S.op("dve", lambda e, w2=w2: e.scalar_tensor_tensor(out=yc[:], in0=uext[:, :, 2:258], scalar=w2, in1=yc[:], op0=ALU.mult, op1=ALU.add),
                     reads=["uext_m", "uext_h", "yc", "vecs"], writes=["yc"])
                S.op("dve", lambda e, g=g, pbb=pbb: e.tensor_tensor(out=mixT[:, HA + g, :], in0=yc[:].rearrange("p a b -> p (a b)"),
                                                                   in1=bank(pbb), op=ALU.mult),
                     reads=[("ps", pbb), "yc"], writes=[("mix", HA + g)])
            for hm in range(HM):
                s_, wv = load_w(w_in[l], 0, KC, 3 * AW + 3 * CC + hm * 128, 128)
                pb = gemm_tile(s_, wv, 0, hT_ch, hT_keys, 512)
                head_norm(pb, vcol(l, 4 * KC + 2), qf, "qf", 512, False)
                S.op("act", lambda e: e.activation(out=qb[:], in_=qf[:], func=AF.Copy), reads=["qf"], writes=["qb"])
                ob, lb = next_ol()
                nmt = MEMT // 128
                for mt in range(nmt):
                    attn_core((kmemT[:, hm, mt * 128:(mt + 1) * 128], ("kmemT", hm)), qb[:], 512, 0, None,
                              (vmem[:, mt, hm * 128:(hm + 1) * 128], ("vmem", mt)), ["qb"], mt == 0, mt == nmt - 1, ob, lb)
                attn_finish(ob, lb, HA + CG + hm)
            for (c0, w) in slabs(0, D):
                s_, wv = load_w(w_out[l], 0, KC, c0, w)
                for n0 in range(0, w, 128):
                    n = (c0 + n0) // 128
                    pd = gemm_tile(s_, wv, n0, mix_ch, mix_keys, 512)
                    S.op("dve", lambda e, pd=pd, n=n: e.tensor_tensor(out=xt[:, n, :], in0=bank(pd), in1=xt[:, n, :], op=ALU.add),
                         reads=[("ps", pd), ("xt", n)], writes=[("xt", n)])

        def load_x(src, j):
            src_ap = src[0]
            S.op("sp", lambda e: e.dma_start(out=xt[:], in_=src_ap[:, j * 512:(j + 1) * 512].rearrange("(k p) t -> p k t", p=128)),
                 reads=[("xs", j)], writes=xt_keys, dma=True)

        def store_x(dst, j, key):
            dst_ap = dst[0]
            S.op("sp", lambda e: e.dma_start(out=dst_ap[:, j * 512:(j + 1) * 512].rearrange("(k p) t -> p k t", p=128), in_=xt[:]),
                 reads=xt_keys, writes=[(key, j)], dma=True)

        stop = getattr(c, "stop", 99)

        def program():
            if stage is None:
                for j in range(NT):
                    load_x(xT, j)
                    ffn(0, w_gu1, w_d1, 0)
                    kv_phase(0, j)
                    store_x(xs_out, j, "xs")
                for l in range(c.DEPTH):
                    gather(l)
                    mem_prep(l)
                    last = (l == c.DEPTH - 1)
                    for j in range(NT):
                        load_x(xs_in, j)
                        mixer(l, j)
                        ffn(l, w_gu2, w_d2, 3 * KC)
                        if not last:
                            ffn(l + 1, w_gu1, w_d1, 0)
                            kv_phase(l + 1, j)
                            store_x(xs_out, j, "xs")
                        else:
                            store_x(outT, j, "out")
                return [("out", j) for j in range(NT)]
            if stage == 0:
                for j in range(NT):
                    load_x(xT, j)
                    ffn(0, w_gu1, w_d1, 0)
                    kv_phase(0, j)
                    store_x(xs_out, j, "xso")
                return [("xso", j) for j in range(NT)] + [(n, 0) for n in ("kT_loc", "v_loc", "km_loc", "ha_loc")]
            l = stage - 1
            last = (l == c.DEPTH - 1)
            mem_prep(l)
            for j in range(NT):
                load_x(xs_in, j)
                mixer(l, j)
                ffn(l, w_gu2, w_d2, 3 * KC)
                if not last:
                    ffn(l + 1, w_gu1, w_d1, 0)
                    kv_phase(l + 1, j)
                    store_x(xs_out, j, "xso")
                else:
                    store_x(outT, j, "out")
            if last:
                return [("out", j) for j in range(NT)]
            return [("xso", j) for j in range(NT)] + [(n, l + 1) for n in ("kT_loc", "v_loc", "km_loc", "ha_loc")]

        fin_keys = program()
        S.finish("sp", fin_keys)

        sem_keys = S.sem_keys()
        sems = {}
        for i, k in enumerate(sem_keys):
            sems[k] = es.enter_context(nc.semaphore("s%d" % i))
        block = es.enter_context(nc.Block())
        S.emit(nc, block, sems)
    return nc, in_names, out_names


def core_blocks(c_, NT):
    return [8 * s + c_ for s in range(2 * NT)]


def block_home(b):
    return b % 8, b // 8


def host_layout(cfg, inputs):
    c = cfg
    x = np.asarray(inputs["x"], np.float32)[0]
    mem = np.asarray(inputs["mem"], np.float32)[0]
    positions = np.asarray(inputs["positions"]).astype(np.int32)[0]
    KC, CG = c.KC, c.CG
    vecs = np.zeros((128, c.NV), np.float32)

    def cols(v):
        return np.asarray(v, np.float32).reshape(-1, 128).T

    for l in range(c.DEPTH):
        b = l * c.VL
        vecs[:, b:b + KC] = cols(inputs["ffn1_norm"][l])
        vecs[:, b + KC:b + 2 * KC] = cols(inputs["mix_norm"][l])
        vecs[:, b + 2 * KC:b + 3 * KC] = cols(inputs["mem_norm"][l])
        vecs[:, b + 3 * KC:b + 4 * KC] = cols(inputs["ffn2_norm"][l])
        vecs[:, b + 4 * KC + 0] = np.asarray(inputs["q_norm"][l], np.float32)
        vecs[:, b + 4 * KC + 1] = np.asarray(inputs["k_norm"][l], np.float32)
        vecs[:, b + 4 * KC + 2] = np.asarray(inputs["mq_norm"][l], np.float32)
        vecs[:, b + 4 * KC + 3] = np.asarray(inputs["mk_norm"][l], np.float32)
        cw = np.asarray(inputs["conv_w"][l], np.float32)
        for jj in range(3):
            vecs[:, b + 4 * KC + 4 + jj * CG:b + 4 * KC + 4 + (jj + 1) * CG] = cols(cw[jj])
    invf = (np.float32(THETA) ** (-np.arange(0, HD, 2, dtype=np.float32) / np.float32(HD))).astype(np.float32)
    vecs[:, c.DEPTH * c.VL] = np.concatenate([invf, invf])
    cst = np.zeros((128, 768), np.float32)
    for i in range(64):
        cst[i + 64, i] = -1.0
        cst[i, i + 64] = 1.0
    cst[:, 128:256] = np.eye(128, dtype=np.float32)
    kk = np.arange(256)[:, None]
    qq = np.arange(256)[None, :]
    caus = np.where(kk <= qq, 0.0, NEG).astype(np.float32)
    cst[:, 256:512] = caus[0:128]
    cst[:, 512:768] = caus[128:256]
    memT = np.ascontiguousarray(mem.T)
    in_maps = []
    tok_idx = []
    for c_ in range(NCORE):
        blocks = core_blocks(c_, c.NT)
        idx = np.concatenate([np.arange(b * BLK, (b + 1) * BLK) for b in blocks])
        tok_idx.append(idx)
        qblk = np.repeat(np.asarray(blocks), BLK)
        eb = np.zeros(c.E, np.int64)
        for cc_ in range(NCORE):
            cb = core_blocks(cc_, c.NT)
            for s in range(c.SLOTS):
                eb[cc_ * c.SLOTS + s] = cb[s]
        pm = (eb[None, :] < qblk[:, None]).astype(np.float32)
        sh = np.zeros((c.E2, 4 * c.NT), np.float32)
        for j in range(c.NT):
            for half in range(2):
                b = blocks[2 * j + half]
                if b >= 1:
                    hc, hs = block_home(b - 1)
                    for i in range(2):
                        sh[hc * 2 * c.SLOTS + hs * 2 + i, j * 4 + half * 2 + i] = 1.0
        m = {
            "xT": np.ascontiguousarray(x[idx].T),
            "memT": memT,
            "pos": np.ascontiguousarray(positions[idx][None, :]),
            "vecs": vecs,
            "cst": cst,
            "pmask": pm,
            "selh": sh,
        }
        in_maps.append(m)
    return in_maps, tok_idx


_NC_CACHE = {}


def get_prog(cfg, stage):
    key = (cfg.D, cfg.SEQ, cfg.DEPTH, cfg.MEMT, stage)
    if key not in _NC_CACHE:
        c2 = Cfg(cfg.D, cfg.SEQ, cfg.DEPTH, cfg.MEMT)
        c2.stage = stage
        _NC_CACHE[key] = build(c2)
    return _NC_CACHE[key]


def run(cfg, inputs):
    in_maps, tok_idx = host_layout(cfg, inputs)
    pools = [dict(m) for m in in_maps]
    for l in range(cfg.DEPTH):
        for n in ("ffn1_w_gate_up", "ffn1_w_down", "w_in", "w_mem_kv", "w_out", "ffn2_w_gate_up", "ffn2_w_down"):
            arr = np.asarray(inputs[n], np.float32)[l]
            for p in pools:
                p[f"{n}_{l}"] = arr
    res = None
    for stage in range(cfg.DEPTH + 1):
        nc, in_names, out_names = get_prog(cfg, stage)
        maps = [{n: p[n] for n in in_names} for p in pools]
        res = run_bass_kernel_spmd(nc, maps, core_ids=list(range(NCORE)))
        if stage == cfg.DEPTH:
            break
        l = stage
        for nm in ("kT", "v", "km", "ha"):
            parts = [np.asarray(res.results[c_][f"{nm}_loc{l}"]) for c_ in range(NCORE)]
            al = np.concatenate(parts, axis=0)
            for c_ in range(NCORE):
                pools[c_][f"{nm}_loc{l}"] = parts[c_]
                pools[c_][f"{nm}_all{l}"] = al
        for c_ in range(NCORE):
            pools[c_]["xs_in"] = np.asarray(res.results[c_]["xs_out"])
    out = np.zeros((1, cfg.SEQ, cfg.D), np.float32)
    for c_ in range(NCORE):
        out[0, tok_idx[c_]] = np.asarray(res.results[c_]["outT"]).T
    return out


def kernel(**inputs):
    cfg = Cfg()
    return run(cfg, inputs)
```
